# Optimizing a Trainium2 kernel written in Bass

```python
import jax, jax.numpy as jnp
from jax import lax
import numpy as np

D_MODEL = 1024
BATCH = 4
SEQ = 8192
DEPTH = 1

CHUNK = 64
PLE_DIM = 256
HG_HEADS = 4
HG_DK = 128
HG_DV = (D_MODEL // 2) // HG_HEADS
HG_WIDTH = HG_HEADS * HG_DV
ML_HEADS = 4
ML_DV = (D_MODEL // 2) // ML_HEADS
ML_DQK = ML_DV // 2
ML_WIDTH = ML_HEADS * ML_DV
MIX_WIDTH = HG_WIDTH + ML_WIDTH
CONV_K = 4
D_FF = ((8 * D_MODEL + 3 * 256 - 1) // (3 * 256)) * 256
ALPHA = float((2 * DEPTH) ** 0.25)
BETA = float((8 * DEPTH) ** -0.25)
LN_EPS = 1e-5
RMS_EPS = 1e-6
ML_I_BIAS = -2.0
ML_F_BIAS_LO = 3.0
ML_F_BIAS_HI = 6.0
PROJ_SIZES = (
    HG_HEADS * HG_DK,
    HG_HEADS * HG_DK,
    HG_WIDTH,
    HG_WIDTH,
    ML_HEADS * ML_DQK,
    ML_HEADS * ML_DQK,
    ML_WIDTH,
    ML_WIDTH,
    ML_HEADS,
    ML_HEADS,
)
PROJ_WIDTH = sum(PROJ_SIZES)

kernel_name = "hymba_hgrn2_mlstm_deepnorm_block"


def _split_cols(u):
    idx, acc = [], 0
    for s in PROJ_SIZES[:-1]:
        acc += s
        idx.append(acc)
    return jnp.split(u, idx, axis=-1)


def layer_norm(x, g, b):
    xf = x.astype(jnp.float32)
    mu = jnp.mean(xf, -1, keepdims=True)
    var = jnp.mean(jnp.square(xf - mu), -1, keepdims=True)
    return ((xf - mu) * lax.rsqrt(var + LN_EPS)).astype(x.dtype) * g + b


def head_rms_norm(h, g):
    hf = h.astype(jnp.float32)
    hf = hf * lax.rsqrt(jnp.mean(hf * hf, -1, keepdims=True) + RMS_EPS)
    B, S, H, Dh = h.shape
    return hf.reshape(B, S, H * Dh).astype(g.dtype) * g


def causal_conv(x, w, b):
    S = x.shape[1]
    xp = jnp.pad(x, ((0, 0), (CONV_K - 1, 0), (0, 0)))
    out = xp[:, 0:S] * w[0]
    for k in range(1, CONV_K):
        out = out + xp[:, k:k + S] * w[k]
    return out + b


def to_chunks(t):
    B, S, H, D = t.shape
    return t.reshape(B, S // CHUNK, CHUNK, H, D).transpose(1, 0, 3, 2, 4)


def gate_chunks(t):
    B, S, H = t.shape
    return t.reshape(B, S // CHUNK, CHUNK, H).transpose(1, 0, 3, 2)


def from_chunks(t):
    N, B, H, C, D = t.shape
    return t.transpose(1, 0, 3, 2, 4).reshape(B, N * C, H, D)


def hgrn2_mixer(q, log_f, k, v):
    B, S, H, DK = q.shape
    DV = v.shape[-1]
    mask = jnp.tril(jnp.ones((CHUNK, CHUNK), dtype=bool))[:, :, None]

    def step(state, inp):
        q_, g_, k_, v_ = inp
        b = jnp.cumsum(g_, axis=2)
        diff = b[:, :, :, None, :] - b[:, :, None, :, :]
        decay = jnp.exp(jnp.where(mask, diff, -jnp.inf))
        scores = jnp.einsum('bhtd,bhsd,bhtsd->bhts', q_, k_, decay)
        o_intra = jnp.einsum('bhts,bhsv->bhtv', scores, v_)
        o_inter = jnp.einsum('bhtd,bhdv->bhtv', q_ * jnp.exp(b), state)
        b_last = b[:, :, -1:, :]
        k_dec = k_ * jnp.exp(b_last - b)
        new_state = jnp.exp(b_last[:, :, 0, :])[..., None] * state + jnp.einsum('bhsd,bhsv->bhdv', k_dec, v_)
        return new_state, o_intra + o_inter

    state0 = jnp.zeros((B, H, DK, DV), jnp.float32)
    _, o = lax.scan(step, state0, (to_chunks(q), to_chunks(log_f), to_chunks(k), to_chunks(v)))
    return from_chunks(o).astype(v.dtype)


def mlstm_mixer(q, k, v, i_pre, log_f):
    B, S, H, DQK = q.shape
    DV = v.shape[-1]
    q = q * (DQK ** -0.5)
    mask = jnp.tril(jnp.ones((CHUNK, CHUNK), dtype=bool))

    def step(carry, inp):
        C_st, n_st, m_st = carry
        q_, k_, v_, ig, lf = inp
        g = jnp.cumsum(lf, axis=-1)
        dmat = g[..., :, None] - g[..., None, :] + ig[..., None, :]
        dmat = jnp.where(mask, dmat, -jnp.inf)
        m_inter = g + m_st[..., None]
        m_t = jnp.maximum(m_inter, jnp.max(dmat, -1))
        w_intra = jnp.exp(dmat - m_t[..., None])
        w_inter = jnp.exp(m_inter - m_t)
        qk = jnp.einsum('bhtd,bhsd->bhts', q_, k_) * w_intra
        num = jnp.einsum('bhts,bhsv->bhtv', qk, v_) + w_inter[..., None] * jnp.einsum('bhtd,bhdv->bhtv', q_, C_st)
        den = jnp.sum(qk, -1) + w_inter * jnp.einsum('bhtd,bhd->bht', q_, n_st)
        h = num / jnp.maximum(jnp.abs(den), jnp.exp(-m_t))[..., None]
        g_last = g[..., -1]
        a = g_last[..., None] - g + ig
        m_new = jnp.maximum(g_last + m_st, jnp.max(a, -1))
        ws = jnp.exp(a - m_new[..., None])
        w_old = jnp.exp(g_last + m_st - m_new)
        C_new = w_old[..., None, None] * C_st + jnp.einsum('bhs,bhsd,bhsv->bhdv', ws, k_, v_)
        n_new = w_old[..., None] * n_st + jnp.einsum('bhs,bhsd->bhd', ws, k_)
        return (C_new, n_new, m_new), h

    carry0 = (jnp.zeros((B, H, DQK, DV), jnp.float32),
              jnp.zeros((B, H, DQK), jnp.float32),
              jnp.zeros((B, H), jnp.float32))
    _, h = lax.scan(step, carry0, (to_chunks(q), to_chunks(k), to_chunks(v),
                                   gate_chunks(i_pre), gate_chunks(log_f)))
    return from_chunks(h).astype(v.dtype)


def setup_inputs(seed: int = 0) -> dict:
    key = jax.random.key(seed)
    ks = jax.random.split(key, 20)
    f32 = jnp.float32
    nrm = lambda k, shape, scale: jax.random.normal(k, shape, f32) * scale
    x = nrm(ks[0], (BATCH, SEQ, D_MODEL), 1.0)
    p = nrm(ks[1], (DEPTH, BATCH, SEQ, PLE_DIM), 1.0)
    w_in = nrm(ks[2], (DEPTH, D_MODEL, PROJ_WIDTH), D_MODEL ** -0.5)
    b_in = nrm(ks[3], (DEPTH, PROJ_WIDTH), 0.02)
    ig_off = PROJ_WIDTH - 2 * ML_HEADS
    fg_off = PROJ_WIDTH - ML_HEADS
    b_in = b_in.at[:, ig_off:fg_off].add(ML_I_BIAS)
    b_in = b_in.at[:, fg_off:].add(jnp.linspace(ML_F_BIAS_LO, ML_F_BIAS_HI, ML_HEADS, dtype=f32))
    hg_lb_logits = nrm(ks[4], (DEPTH + 1, HG_HEADS * HG_DK), 0.5)
    ml_conv_w = nrm(ks[5], (DEPTH, CONV_K, 2 * ML_HEADS * ML_DQK), CONV_K ** -0.5)
    ml_conv_b = nrm(ks[6], (DEPTH, 2 * ML_HEADS * ML_DQK), 0.02)
    hg_norm_g = 1.0 + nrm(ks[7], (DEPTH, HG_WIDTH), 0.02)
    ml_norm_g = 1.0 + nrm(ks[8], (DEPTH, ML_WIDTH), 0.02)
    w_out = nrm(ks[9], (DEPTH, MIX_WIDTH, D_MODEL), BETA * MIX_WIDTH ** -0.5)
    ln1_g = 1.0 + nrm(ks[10], (DEPTH, D_MODEL), 0.02)
    ln1_b = nrm(ks[11], (DEPTH, D_MODEL), 0.02)
    w_ffn_gate = nrm(ks[12], (DEPTH, D_MODEL, D_FF), D_MODEL ** -0.5)
    w_ffn_up = nrm(ks[13], (DEPTH, D_MODEL, D_FF), D_MODEL ** -0.5)
    w_ffn_down = nrm(ks[14], (DEPTH, D_FF, D_MODEL), BETA * D_FF ** -0.5)
    ln2_g = 1.0 + nrm(ks[15], (DEPTH, D_MODEL), 0.02)
    ln2_b = nrm(ks[16], (DEPTH, D_MODEL), 0.02)
    ple_w_proj = nrm(ks[17], (DEPTH, PLE_DIM, D_MODEL), PLE_DIM ** -0.5)
    ple_w_gate = nrm(ks[18], (DEPTH, D_MODEL, D_MODEL), D_MODEL ** -0.5)
    ple_b_gate = nrm(ks[19], (DEPTH, D_MODEL), 0.02)
    return {"x": x, "p": p, "w_in": w_in, "b_in": b_in, "hg_lb_logits": hg_lb_logits,
            "ml_conv_w": ml_conv_w, "ml_conv_b": ml_conv_b, "hg_norm_g": hg_norm_g,
            "ml_norm_g": ml_norm_g, "w_out": w_out, "ln1_g": ln1_g, "ln1_b": ln1_b,
            "w_ffn_gate": w_ffn_gate, "w_ffn_up": w_ffn_up, "w_ffn_down": w_ffn_down,
            "ln2_g": ln2_g, "ln2_b": ln2_b, "ple_w_proj": ple_w_proj,
            "ple_w_gate": ple_w_gate, "ple_b_gate": ple_b_gate}


def reference(x, p, w_in, b_in, hg_lb_logits, ml_conv_w, ml_conv_b, hg_norm_g, ml_norm_g,
              w_out, ln1_g, ln1_b, w_ffn_gate, w_ffn_up, w_ffn_down, ln2_g, ln2_b,
              ple_w_proj, ple_w_gate, ple_b_gate):
    B, S, _ = x.shape
    lower_bounds = jnp.cumsum(jax.nn.softmax(hg_lb_logits.astype(jnp.float32), axis=0), axis=0)
    for i in range(DEPTH):
        u = x @ w_in[i] + b_in[i]
        hq, hf, hv, hgate, mq, mk, mv, mo, mig, mfg = _split_cols(u)

        lb = lower_bounds[i]
        log_f = jnp.logaddexp(jnp.log(lb), jnp.log1p(-lb) + jax.nn.log_sigmoid(hf.astype(jnp.float32)))
        k_hg = -jnp.expm1(log_f)
        o_hg = hgrn2_mixer(jax.nn.silu(hq).reshape(B, S, HG_HEADS, HG_DK),
                           log_f.reshape(B, S, HG_HEADS, HG_DK),
                           k_hg.reshape(B, S, HG_HEADS, HG_DK),
                           hv.reshape(B, S, HG_HEADS, HG_DV))
        o_hg = head_rms_norm(o_hg, hg_norm_g[i]) * jax.nn.silu(hgate)

        qk_c = jax.nn.silu(causal_conv(jnp.concatenate([mq, mk], -1), ml_conv_w[i], ml_conv_b[i]))
        mq_c, mk_c = jnp.split(qk_c, 2, axis=-1)
        h_ml = mlstm_mixer(mq_c.reshape(B, S, ML_HEADS, ML_DQK),
                           mk_c.reshape(B, S, ML_HEADS, ML_DQK),
                           mv.reshape(B, S, ML_HEADS, ML_DV),
                           mig.astype(jnp.float32),
                           jax.nn.log_sigmoid(mfg.astype(jnp.float32)))
        o_ml = head_rms_norm(h_ml, ml_norm_g[i]) * jax.nn.sigmoid(mo)

        mix = jnp.concatenate([o_hg, o_ml], -1) @ w_out[i]
        x = layer_norm(ALPHA * x + mix, ln1_g[i], ln1_b[i])

        ffn = (jax.nn.silu(x @ w_ffn_gate[i]) * (x @ w_ffn_up[i])) @ w_ffn_down[i]
        x = layer_norm(ALPHA * x + ffn, ln2_g[i], ln2_b[i])

        x = x + jax.nn.sigmoid(x @ ple_w_gate[i] + ple_b_gate[i]) * (p[i] @ ple_w_proj[i])
    return x
```

```python
import math
import numpy as np
import concourse.bass as bass
import concourse.mybir as mybir
from concourse.bass_utils import run_bass_kernel_spmd

F32 = mybir.dt.float32
BF16 = mybir.dt.bfloat16
AF = mybir.ActivationFunctionType
ALU = mybir.AluOpType

D = 1024
KT = 8
PW = 3592
DFF = 2816
FT = 22
ALPHA = float(2.0 ** 0.25)
LN_EPS = 1e-5
RMS_EPS = 1e-6
C_HQ, C_HF, C_HV, C_HG, C_MQ, C_MK, C_MV, C_MO, C_IG = 0, 512, 1024, 1536, 2048, 2304, 2560, 3072, 3584
FM_COLS = ([C_HQ + 128 * j for j in range(4)] + [C_HF + 128 * j for j in range(4)] +
           [C_HG + 128 * j for j in range(4)] + [C_MQ + 128 * j for j in range(4)] +
           [C_MO + 128 * j for j in range(4)])
J_HQ, J_HF, J_HG, J_QK, J_MO = 0, 4, 8, 12, 16
BLK = 512
BLKB = 256
MID = 63


class Sched:
    def __init__(self, nc):
        self.nc = nc
        self.prog = {e: [] for e in ('pe', 'act', 'dve', 'pool', 'sp')}
        self.sems = {}
        self.cnt = {}
        self.seen = {e: {} for e in self.prog}
        self.lastw = {}
        self.readers = {}
        self._stack = []
        self.nops = {e: 0 for e in self.prog}
        self.bank_last = {}

    def sem(self, name):
        if name not in self.sems:
            cm = self.nc.semaphore(name)
            self.sems[name] = cm.__enter__()
            self._stack.append(cm)
            self.cnt[name] = 0
        return self.sems[name]

    @staticmethod
    def _isbank(k):
        return isinstance(k, tuple) and k[0] == 'BANK'

    def _deps(self, reads, writes, own=None):
        deps = {}

        def add(d):
            if d is not None and deps.get(d[0], 0) < d[1]:
                deps[d[0]] = d[1]
        for k in list(reads) + list(writes):
            if self._isbank(k):
                d = self.bank_last.get(k)
                if d is not None and d[0] != own:
                    add(d)
        for k in reads:
            if not self._isbank(k):
                add(self.lastw.get(k))
        for k in writes:
            if not self._isbank(k):
                add(self.lastw.get(k))
                for d in self.readers.get(k, ()):
                    add(d)
        return deps

    def _waits(self, e, deps, skip=None):
        waits = []
        for s, v in deps.items():
            if s == skip or self.seen[e].get(s, 0) >= v:
                continue
            self.seen[e][s] = v
            waits.append((self.sems[s], v))
        return waits

    def _record(self, reads, writes, tag):
        for k in reads:
            if self._isbank(k):
                self.bank_last[k] = tag
            else:
                self.readers.setdefault(k, []).append(tag)
        for k in writes:
            if self._isbank(k):
                self.bank_last[k] = tag
            else:
                self.lastw[k] = tag
                self.readers[k] = []

    def op(self, e, fn, reads=(), writes=(), inc=True, serial=False):
        sname = 'c_' + e
        h = self.sem(sname)
        deps = self._deps(reads, writes, own=sname)
        skip = sname if e == 'pe' else None
        if serial and self.cnt[sname] > 0:
            deps[sname] = max(deps.get(sname, 0), self.cnt[sname])
            skip = None
        waits = self._waits(e, deps, skip=skip)
        val = self.cnt[sname] + 1
        if inc:
            self.cnt[sname] = val

        def run(eng, fn=fn, waits=waits, inc=inc, h=h):
            for sh, v in waits:
                eng.wait_ge(sh, v)
            ins = fn(eng)
            if inc:
                ins.then_inc(h, 1)
        self.prog[e].append(run)
        self.nops[e] += 1
        self._record(reads, writes, (sname, val))

    def dma(self, e, stream, fn, reads=(), writes=()):
        sname = 'd_' + stream
        h = self.sem(sname)
        waits = self._waits(e, self._deps(reads, writes))
        self.cnt[sname] += 16
        val = self.cnt[sname]

        def run(eng, fn=fn, waits=waits, h=h):
            for sh, v in waits:
                eng.wait_ge(sh, v)
            fn(eng).then_inc(h, 16)
        self.prog[e].append(run)
        self.nops[e] += 1
        self._record(reads, writes, (sname, val))

    def final_wait(self, e, keys):
        deps = {}
        for k in keys:
            d = self.lastw.get(k)
            if d and deps.get(d[0], 0) < d[1]:
                deps[d[0]] = d[1]
        waits = self._waits(e, deps)

        def run(eng, waits=waits):
            for sh, v in waits:
                eng.wait_ge(sh, v)
        self.prog[e].append(run)

    def emit(self):
        with self.nc.Block() as block:
            for e, dec in (('sp', block.sync), ('act', block.scalar), ('dve', block.vector),
                           ('pool', block.gpsimd), ('pe', block.tensor)):
                prog = self.prog[e]

                def body(eng, prog=prog):
                    for r in prog:
                        r(eng)
                dec(body)
        self.prog = {e: [] for e in self.prog}

    def close(self):
        for cm in reversed(self._stack):
            cm.__exit__(None, None, None)


def build(T_PRE, T_OWN, debug=()):
    assert T_PRE % BLK == 0 and T_OWN % BLK == 0
    nc = bass.Bass("TRN2", target_bir_lowering=False)
    S = Sched(nc)
    ctxs = []

    def dram(name, shape, kind="ExternalInput", dt=F32):
        return nc.dram_tensor(name, list(shape), dt, kind=kind).ap()

    def sb(name, shape, dt=F32):
        cm = nc.sbuf_tensor(name, list(shape), dt)
        t = cm.__enter__()
        ctxs.append(cm)
        return t

    def psum(name, shape, dt=F32):
        cm = nc.psum_tensor(name, list(shape), dt)
        t = cm.__enter__()
        ctxs.append(cm)
        return t

    TP = max(T_PRE, BLK)
    xT = dram("xT", [128, KT, T_OWN])
    xTp = dram("xTp", [128, KT, TP])
    pT = dram("pT", [128, 2, T_OWN])
    flag_d = dram("flag", [128, 1])
    w_in_d = dram("w_in", [128, KT, PW])
    b_fm_d = dram("b_fm", [128, 20])
    b_tm_d = dram("b_tm", [128, 1032])
    lbl_d = dram("lbl", [128, 8])
    convw_d = dram("convw", [128, 16])
    convb_d = dram("convb", [128, 4])
    gain_d = dram("gain", [128, 8])
    w_out_d = dram("w_out", [128, KT, D])
    lnp_d = dram("lnp", [128, 32])
    wg_d = dram("wg", [128, KT, DFF])
    wu_d = dram("wu", [128, KT, DFF])
    wd_d = dram("wd", [128, FT, D])
    pwp_d = dram("pwp", [128, 2, D])
    pwg_d = dram("pwg", [128, KT, D])
    pbg_d = dram("pbg", [128, 8])
    outT = dram("outT", [128, KT, T_OWN], kind="ExternalOutput")
    x1s = dram("x1s", [128, KT, T_OWN], kind="Internal")
    dbg_out = {}

    def dump(name, ap_fn, shape, reads):
        if name not in debug:
            return
        d = dram("dbg_" + name, shape, kind="ExternalOutput")
        dbg_out[name] = d
        S.dma('sp', 'dbg_' + name, lambda e: e.dma_start(out=d, in_=ap_fn()), reads=reads, writes=[('dbg', name)])

    mask01 = sb("mask01", [128, 128])
    identf = sb("identf", [128, 128])
    ident = sb("ident", [128, 128], BF16)
    onesF = sb("onesF", [128, 128])
    onesM = sb("onesM", [128, 128], BF16)
    flag = sb("flag_sb", [128, 1])
    lnp = sb("lnp_sb", [128, 32])

    S.op('pool', lambda e: e.memset(mask01[:], 1.0), writes=['mask01'])
    S.op('pool', lambda e: e.affine_select(out=mask01[:], in_=mask01[:], pattern=[[1, 128]], compare_op=ALU.is_ge,
                                           fill=0.0, base=0, channel_multiplier=-1), reads=['mask01'], writes=['mask01'])
    mask2 = sb("mask2", [128, 2, 128])
    S.op('pool', lambda e: e.memset(mask2[:], 1.0), writes=['mask2'])
    S.op('pool', lambda e: e.affine_select(out=mask2[:], in_=mask2[:], pattern=[[0, 2], [1, 128]], compare_op=ALU.is_ge,
                                           fill=0.0, base=0, channel_multiplier=-1), reads=['mask2'], writes=['mask2'])
    S.op('pool', lambda e: e.memset(identf[:], 1.0), writes=['identf'])
    S.op('pool', lambda e: e.affine_select(out=identf[:], in_=identf[:], pattern=[[1, 128]], compare_op=ALU.is_equal,
                                           fill=0.0, base=0, channel_multiplier=-1), reads=['identf'], writes=['identf'])
    S.op('dve', lambda e: e.tensor_copy(out=ident[:], in_=identf[:]), reads=['identf'], writes=['ident'])
    S.op('pool', lambda e: e.memset(onesF[:], 1.0), writes=['onesF'])
    S.op('pool', lambda e: e.memset(onesM[:], 1.0 / 1024.0), writes=['onesM'])
    S.dma('sp', 'c0', lambda e: e.dma_start(out=flag[:], in_=flag_d), writes=['flag'])
    S.dma('sp', 'c1', lambda e: e.dma_start(out=lnp[:], in_=lnp_d), writes=['lnp'])

    PB = [psum("PB%d" % i, [128, 512]) for i in range(4)]
    X = [psum("X%d" % i, [128, 512]) for i in range(2)]
    Y = [psum("Y%d" % i, [128, 512]) for i in range(2)]
    XK = [('BANK', 'X0'), ('BANK', 'X1')]
    YK = [('BANK', 'Y0'), ('BANK', 'Y1')]
    XT = [X[i][:, 256:384].bitcast(BF16) for i in range(2)]
    XT2 = [X[i][:, 384:512].bitcast(BF16) for i in range(2)]
    rr = {'pb': 0}

    def next_pb():
        i = rr['pb'] % 4
        rr['pb'] += 1
        return PB[i], ('BANK', 'PB%d' % i)
    PG = Y[1][:, 400:416]
    PGK = YK[1]

    def run(gen):
        for _ in gen:
            pass

    def chain(*gens):
        for g in gens:
            yield from g

    def merge(ga, na, gb, nb, frac=1.0):
        a = b = 0
        da = db = False
        while not (da and db):
            if not da and (db or a * nb <= b * na * frac):
                try:
                    next(ga)
                    a += 1
                except StopIteration:
                    da = True
            elif not db:
                try:
                    next(gb)
                    b += 1
                except StopIteration:
                    db = True

    ln_tmp = {}

    def layer_norm(buf, bkeyf, W, gcol, bcol, bfout, bfkey, banks=None):
        if W not in ln_tmp:
            ln_tmp[W] = dict(
                ybf=[sb("ln_ybf%d_%d" % (W, i), [128, W], BF16) for i in range(2)],
                ysq=[sb("ln_ysq%d_%d" % (W, i), [128, W], BF16) for i in range(2)],
                rstd=sb("ln_rstd%d" % W, [128, W]),
                nmr=sb("ln_nmr%d" % W, [128, W]),
                t=[sb("ln_t%d_%d" % (W, i), [128, W]) for i in range(2)])
        L = ln_tmp[W]
        L['msq'] = L['t'][0]
        if banks is None:
            pm, pmk = next_pb()
            pq, pqk = next_pb()
        else:
            (pm, pmk), (pq, pqk) = banks
        for n in range(KT):
            yb, ys = L['ybf'][n % 2], L['ysq'][n % 2]
            S.op('dve', lambda e, n=n, yb=yb: e.tensor_copy(out=yb[:], in_=buf[:, n, :]),
                 reads=[bkeyf(n)], writes=[('ln_ybf', W, n % 2)])
            S.op('act', lambda e, n=n, ys=ys: e.activation(out=ys[:], in_=buf[:, n, :], func=AF.Square),
                 reads=[bkeyf(n)], writes=[('ln_ysq', W, n % 2)])
            S.op('pe', lambda e, n=n, yb=yb: e.matmul(pm[:, 0:W], lhsT=onesM[:], rhs=yb[:], start=(n == 0), stop=(n == KT - 1)),
                 reads=[('ln_ybf', W, n % 2), 'onesM'], writes=[pmk], inc=True)
            S.op('pe', lambda e, n=n, ys=ys: e.matmul(pq[:, 0:W], lhsT=onesM[:], rhs=ys[:], start=(n == 0), stop=(n == KT - 1)),
                 reads=[('ln_ysq', W, n % 2), 'onesM'], writes=[pqk], inc=True)
            yield
        S.op('act', lambda e: e.activation(out=L['msq'][:], in_=pm[:, 0:W], func=AF.Square), reads=[pmk], writes=[('ln_t', W, 0)])
        S.op('dve', lambda e: e.scalar_tensor_tensor(out=L['rstd'][:], in0=pq[:, 0:W], scalar=LN_EPS, in1=L['msq'][:],
                                                     op0=ALU.add, op1=ALU.subtract),
             reads=[pqk, ('ln_t', W, 0)], writes=[('ln_rstd', W)])
        S.op('act', lambda e: e.activation(out=L['rstd'][:], in_=L['rstd'][:], func=AF.Ln), reads=[('ln_rstd', W)], writes=[('ln_rstd', W)])
        S.op('act', lambda e: e.activation(out=L['rstd'][:], in_=L['rstd'][:], func=AF.Exp, scale=-0.5),
             reads=[('ln_rstd', W)], writes=[('ln_rstd', W)])
        S.op('dve', lambda e: e.scalar_tensor_tensor(out=L['nmr'][:], in0=pm[:, 0:W], scalar=-1.0, in1=L['rstd'][:],
                                                     op0=ALU.mult, op1=ALU.mult),
             reads=[pmk, ('ln_rstd', W)], writes=[('ln_nmr', W)])
        yield
        for n in range(KT):
            t = L['t'][n % 2]
            S.op('dve', lambda e, n=n, t=t: e.tensor_tensor(out=t[:], in0=buf[:, n, :], in1=L['rstd'][:], op=ALU.mult),
                 reads=[bkeyf(n), ('ln_rstd', W)], writes=[('ln_t', W, n % 2)])
            S.op('dve', lambda e, n=n, t=t: e.tensor_tensor(out=t[:], in0=t[:], in1=L['nmr'][:], op=ALU.add),
                 reads=[('ln_t', W, n % 2), ('ln_nmr', W)], writes=[('ln_t', W, n % 2)])
            S.op('act', lambda e, n=n, t=t: e.activation(out=buf[:, n, :], in_=t[:], func=AF.Identity,
                                                         bias=lnp[:, bcol + n:bcol + n + 1], scale=lnp[:, gcol + n:gcol + n + 1]),
                 reads=[('ln_t', W, n % 2), 'lnp'], writes=[bkeyf(n)])
            if bfout is not None:
                S.op('dve', lambda e, n=n: e.tensor_copy(out=bfout[:, n, :], in_=buf[:, n, :]),
                     reads=[bkeyf(n)], writes=[bfkey(n)])
            yield

    def phase_a():
        w_in = sb("w_in_bf", [128, KT, PW], BF16)
        w_out = sb("w_out_bf", [128, KT, D], BF16)
        b_fm = sb("b_fm_sb", [128, 20])
        b_tm = sb("b_tm_sb", [128, 1032])
        lbl = sb("lbl_sb", [128, 8])
        lb = sb("lb_sb", [128, 4])
        oml = sb("oml_sb", [128, 4])
        convw = sb("convw_sb", [128, 16])
        convb = sb("convb_sb", [128, 4])
        gain = sb("gain_sb", [128, 8])
        dg = sb("dg", [128, 16, 128], BF16)
        resetm = sb("resetm", [128, 512])
        xres = sb("xres", [128, KT, BLK])
        xb = [sb("xb%d" % i, [128, KT, BLK], BF16) for i in range(2)]
        NS = 2
        qt = [sb("qt%d" % i, [128, 4, BLK], BF16) for i in range(NS)]
        kTt = [sb("kTt%d" % i, [128, 4, BLK], BF16) for i in range(NS)]
        gH = [sb("gH%d" % i, [128, 4, BLK], BF16) for i in range(1)]
        gM = [sb("gM%d" % i, [128, 4, BLK], BF16) for i in range(1)]
        qk = [sb("qk%d" % i, [128, 4, BLK], BF16) for i in range(NS)]
        dec = [sb("dec%d" % i, [128, 4, 12]) for i in range(NS)]
        vh = [sb("vh%d" % i, [128, 512], BF16) for i in range(2)]
        vm = [sb("vm%d" % i, [128, 4, 129], BF16) for i in range(2)]
        vmw = [sb("vmw%d" % i, [128, 4, 129], BF16) for i in range(2)]
        convbuf = [sb("convbuf%d" % j, [128, 3 + BLK], BF16) for j in range(4)]
        tU = sb("tU", [128, BLK]); tX = sb("tX", [128, BLK]); tL1 = sb("tL1", [128, BLK]); tL2 = sb("tL2", [128, BLK])
        tB = sb("tB", [128, BLK])
        nbm = sb("nbm", [128, 4]); dl = sb("dl", [128, 4]); kb = sb("kb", [128, 4])
        nb_fm = sb("nb_fm", [128, 20]); lomb = sb("lomb", [128, 4])
        gsb = [sb("gsb%d" % i, [128, 8]) for i in range(2)]
        gef = [sb("gef%d" % i, [128, 4]) for i in range(2)]
        gsp = [sb("gsp%d" % i, [128, 4]) for i in range(2)]
        gtm = [sb("gtm%d" % i, [128, 4]) for i in range(2)]
        wS = [sb("wS%d" % i, [128, 4]) for i in range(2)]
        rT = [sb("rT%d" % i, [128, 4]) for i in range(2)]
        rI = [sb("rI%d" % i, [128, 4]) for i in range(2)]
        egl = [sb("egl%d" % i, [128, 4]) for i in range(2)]
        Sst = [sb("Sst%d" % h, [128, 128]) for h in range(4)]
        Sbf = [sb("Sbf%d" % h, [128, 128], BF16) for h in range(4)]
        Stmp = [sb("Stmp%d" % h, [128, 128]) for h in range(4)]
        Cst = [sb("Cst%d" % h, [128, 129]) for h in range(4)]
        Cbf = [sb("Cbf%d" % h, [128, 129], BF16) for h in range(4)]
        Ctmp = [sb("Ctmp%d" % h, [128, 129]) for h in range(4)]
        kTM2 = [sb("kTM2_%d" % i, [128, 2, 128], BF16) for i in range(2)]
        AT2 = [sb("AT2_%d" % i, [128, 2, 128], BF16) for i in range(2)]
        ATr2 = [sb("ATr2_%d" % i, [128, 2, 128], BF16) for i in range(2)]
        on2 = [sb("on2_%d" % i, [128, 2, 128], BF16) for i in range(2)]
        junk = sb("junk", [128, 128])
        ssq2 = [sb("ssq2_%d" % i, [128, 2]) for i in range(2)]
        rsd2 = [sb("rsd2_%d" % i, [128, 2]) for i in range(2)]
        mt2 = [sb("mt2_%d" % i, [128, 4, 2]) for i in range(2)]
        ocat = sb("ocat", [128, 8, BLK], BF16)
        mrr = {'k': 0}

        WG = [('hf', C_HF, 512), ('mqk', C_MQ, 512), ('hv', C_HV, 512), ('mv', C_MV, 512), ('gt', C_IG, 8),
              ('hq', C_HQ, 512), ('hg', C_HG, 512), ('mo', C_MO, 512)]

        def wkey(c0):
            for nm, g0, gn in WG:
                if g0 <= c0 < g0 + gn:
                    return ('w_in', nm)
            raise KeyError(c0)
        def load_w_in(first):
            for nm, g0, gn in (WG[:2] if first else WG[2:]):
                S.dma('pool', 'w_in_' + nm, lambda e, g0=g0, gn=gn: e.dma_start(out=w_in[:, :, g0:g0 + gn], in_=w_in_d[:, :, g0:g0 + gn]),
                      writes=[('w_in', nm)])
        for i, (t, d, k) in enumerate(((b_fm, b_fm_d, 'b_fm'), (b_tm, b_tm_d, 'b_tm'), (lbl, lbl_d, 'lbl'),
                                       (convw, convw_d, 'convw'), (convb, convb_d, 'convb'), (gain, gain_d, 'gain'))):
            S.dma('sp', 'cA%d' % i, lambda e, t=t, d=d: e.dma_start(out=t[:], in_=d), writes=[k])
        S.op('dve', lambda e: e.tensor_tensor(out=lb[:], in0=lbl[:, 0:4], in1=lbl[:, 4:8], op=ALU.subtract), reads=['lbl'], writes=['lb'])
        S.op('act', lambda e: e.activation(out=lb[:], in_=lb[:], func=AF.Sigmoid), reads=['lb'], writes=['lb'])
        S.op('dve', lambda e: e.tensor_scalar(out=oml[:], in0=lb[:], scalar1=-1.0, scalar2=1.0, op0=ALU.mult, op1=ALU.add),
             reads=['lb'], writes=['oml'])
        S.op('act', lambda e: e.activation(out=lomb[:], in_=oml[:], func=AF.Ln), reads=['oml'], writes=['lomb'])
        S.op('dve', lambda e: e.tensor_scalar(out=nb_fm[:], in0=b_fm[:], scalar1=-1.0, scalar2=None, op0=ALU.mult),
             reads=['b_fm'], writes=['nb_fm'])
        for j in range(4):
            for k in range(4):
                S.op('dve', lambda e, j=j, k=k: e.tensor_scalar(out=dg[:, j * 4 + k, :], in0=identf[:],
                                                                scalar1=convw[:, j * 4 + k:j * 4 + k + 1], scalar2=None, op0=ALU.mult),
                     reads=['identf', 'convw'], writes=[('dg', j)])
        S.op('pool', lambda e: e.memset(resetm[:], 1.0), writes=['resetm'])
        S.op('pool', lambda e: e.memset(resetm[:].rearrange("p (c t) -> p c t", t=128)[:, :, 0:1], 0.0),
             reads=['resetm'], writes=['resetm'])
        for g_ in range(2):
            S.op('pool', lambda e, g_=g_: e.memset(AT2[g_][:], 0.0), writes=[('AT2', g_)])
        for j in range(4):
            S.op('pool', lambda e, j=j: e.memset(convbuf[j][:, 0:3], 0.0), writes=[('cb_carry', j)])
        for h in range(4):
            S.op('pool', lambda e, h=h: e.memset(Sst[h][:], 0.0), writes=[('S', h)])
            S.op('pool', lambda e, h=h: e.memset(Cst[h][:], 0.0), writes=[('C', h)])
            S.op('pool', lambda e, h=h: e.memset(Cbf[h][:], 0.0), writes=[('Cbf', h)])
        for s in range(2):
            S.op('pool', lambda e, s=s: e.memset(vm[s][:], 1.0), writes=[('vm', s)])

        def load_w_out():
            wst = xres[:].rearrange("p (a two) b -> p a (two b)", two=2)
            for half in range(2):
                S.dma('sp', 'wo_st', lambda e, half=half: e.dma_start(out=wst, in_=w_out_d[:, half * 4:half * 4 + 4, :]),
                      writes=[('xres', n) for n in range(KT)])
                for q in range(4):
                    kt = half * 4 + q
                    S.op('dve', lambda e, q=q, kt=kt: e.tensor_scalar(out=w_out[:, kt, :], in0=wst[:, q, :],
                                                                      scalar1=gain[:, kt:kt + 1], scalar2=None, op0=ALU.mult),
                         reads=[('xres', n) for n in range(KT)] + ['gain'], writes=[('w_out', kt)])

        def load_x(src, blk, slot):
            for half in range(2):
                S.dma('pool', 'xb%d_%d' % (slot, half),
                      lambda e, half=half: e.dma_start(out=xb[slot][:, half * 4:half * 4 + 4, :],
                                                       in_=src[:, half * 4:half * 4 + 4, blk * BLK:(blk + 1) * BLK]),
                      writes=[('xb', slot, half)])

        def fm_proj(slot, j):
            ps, pk = next_pb()
            c0 = FM_COLS[j]
            for kt in range(KT):
                S.op('pe', lambda e, kt=kt: e.matmul(ps[:], lhsT=w_in[:, kt, c0:c0 + 128], rhs=xb[slot][:, kt, :],
                                                     start=(kt == 0), stop=(kt == KT - 1)),
                     reads=[wkey(c0), ('xb', slot, kt // 4)], writes=[pk], inc=(kt == KT - 1))
            return ps, pk

        def bias(j):
            return b_fm[:, j:j + 1]

        def fm_a(slot, fs, own, need_q=True):
            for h in range(4):
                ps, pk = fm_proj(slot, J_HF + h)
                S.op('act', lambda e, ps=ps, h=h: e.activation(out=tU[:], in_=ps[:], func=AF.Exp, bias=nb_fm[:, J_HF + h:J_HF + h + 1], scale=-1.0),
                     reads=[pk, 'nb_fm'], writes=['tU'])
                S.op('dve', lambda e, ps=ps, h=h: e.tensor_scalar(out=tX[:], in0=ps[:], scalar1=bias(J_HF + h), scalar2=None, op0=ALU.add),
                     reads=[pk, 'b_fm'], writes=['tX'])
                yield
                S.op('act', lambda e, h=h: e.activation(out=tL1[:], in_=tU[:], func=AF.Ln, scale=lb[:, h:h + 1], bias=1.0),
                     reads=['tU', 'lb'], writes=['tL1'])
                S.op('act', lambda e: e.activation(out=tL2[:], in_=tU[:], func=AF.Ln, bias=1.0, scale=1.0), reads=['tU'], writes=['tL2'])
                yield
                S.op('pool', lambda e: e.tensor_tensor(out=tL1[:], in0=tL1[:], in1=tL2[:], op=ALU.subtract), reads=['tL1', 'tL2'], writes=['tL1'])
                S.op('pool', lambda e: e.tensor_tensor(out=tX[:], in0=tX[:], in1=tL2[:], op=ALU.add), reads=['tX', 'tL2'], writes=['tX'])
                yield
                S.op('dve', lambda e: e.tensor_tensor_scan(out=tB[:], data0=resetm[:], data1=tL1[:], initial=0.0, op0=ALU.mult, op1=ALU.add),
                     reads=['tL1', 'resetm'], writes=['tB'])
                yield
                tBv = tB[:].rearrange("p (c t) -> p c t", t=128)
                S.op('dve', lambda e, tBv=tBv: e.tensor_scalar(out=nbm[:], in0=tBv[:, :, MID], scalar1=-1.0, scalar2=None, op0=ALU.mult),
                     reads=['tB'], writes=['nbm'])
                S.op('dve', lambda e, tBv=tBv: e.tensor_tensor(out=dl[:], in0=tBv[:, :, 127], in1=tBv[:, :, MID], op=ALU.subtract),
                     reads=['tB'], writes=['dl'])
                S.op('dve', lambda e, tBv=tBv, h=h: e.tensor_scalar(out=kb[:], in0=tBv[:, :, MID], scalar1=lomb[:, h:h + 1], scalar2=None, op0=ALU.add),
                     reads=['tB', 'lomb'], writes=['kb'])
                S.op('pool', lambda e: e.tensor_tensor(out=tX[:], in0=tX[:], in1=tB[:], op=ALU.add), reads=['tX', 'tB'], writes=['tX'])
                yield
                for c in range(4):
                    cs = slice(c * 128, (c + 1) * 128)
                    S.op('act', lambda e, c=c, cs=cs, h=h: e.activation(out=kTt[fs][:, h, cs], in_=tX[:, cs], func=AF.Exp,
                                                                         bias=kb[:, c:c + 1], scale=-1.0),
                         reads=['tX', 'kb'], writes=[('kTt', fs, h)])
                S.op('act', lambda e, h=h: e.activation(out=dec[fs][:, h, 0:4], in_=dl[:], func=AF.Exp), reads=['dl'], writes=[('dec', fs, h, 0)])
                S.op('act', lambda e, h=h, tBv=tBv: e.activation(out=dec[fs][:, h, 4:8], in_=tBv[:, :, 127], func=AF.Exp),
                     reads=['tB'], writes=[('dec', fs, h, 1)])
                S.op('act', lambda e, h=h, tBv=tBv: e.activation(out=dec[fs][:, h, 8:12], in_=tBv[:, :, MID], func=AF.Exp),
                     reads=['tB'], writes=[('dec', fs, h, 2)])
                yield
                if own:
                    ps, pk = fm_proj(slot, J_HQ + h)
                    S.op('act', lambda e, ps=ps, h=h: e.activation(out=tU[:], in_=ps[:], func=AF.Exp, bias=nb_fm[:, J_HQ + h:J_HQ + h + 1], scale=-1.0),
                         reads=[pk, 'nb_fm'], writes=['tU'])
                    S.op('dve', lambda e, ps=ps, h=h: e.tensor_scalar(out=tL1[:], in0=ps[:], scalar1=bias(J_HQ + h), scalar2=None, op0=ALU.add),
                         reads=[pk, 'b_fm'], writes=['tL1'])
                    yield
                    S.op('act', lambda e: e.activation(out=tL2[:], in_=tU[:], func=AF.Ln, bias=1.0, scale=1.0), reads=['tU'], writes=['tL2'])
                    yield
                    S.op('pool', lambda e: e.tensor_tensor(out=tL2[:], in0=tB[:], in1=tL2[:], op=ALU.subtract), reads=['tB', 'tL2'], writes=['tL2'])
                    yield
                    for c in range(4):
                        cs = slice(c * 128, (c + 1) * 128)
                        S.op('act', lambda e, c=c, cs=cs: e.activation(out=tL2[:, cs], in_=tL2[:, cs], func=AF.Exp, bias=nbm[:, c:c + 1], scale=1.0),
                             reads=['tL2', 'nbm'], writes=['tL2'])
                    yield
                    S.op('pool', lambda e, h=h: e.tensor_tensor(out=qt[fs][:, h, :], in0=tL1[:], in1=tL2[:], op=ALU.mult),
                         reads=['tL1', 'tL2'], writes=[('qt', fs, h)])
                    yield
            for j in range(4):
                if j < 2 and not need_q:
                    continue
                ps, pk = fm_proj(slot, J_QK + j)
                S.op('act', lambda e, ps=ps, j=j: e.activation(out=convbuf[j][:, 3:3 + BLK], in_=ps[:], func=AF.Identity,
                                                               bias=bias(J_QK + j), scale=1.0),
                     reads=[pk, 'b_fm'], writes=[('cb_body', j)])
                ps2, pk2 = next_pb()
                for k in range(4):
                    S.op('pe', lambda e, ps2=ps2, j=j, k=k: e.matmul(ps2[:], lhsT=dg[:, j * 4 + k, :], rhs=convbuf[j][:, k:k + BLK],
                                                                    start=(k == 0), stop=(k == 3)),
                         reads=[('dg', j), ('cb_body', j), ('cb_carry', j)], writes=[pk2], inc=(k == 3))
                S.op('act', lambda e, ps2=ps2, j=j: e.activation(out=qk[fs][:, j, :], in_=ps2[:], func=AF.Silu,
                                                                 bias=convb[:, j:j + 1], scale=1.0),
                     reads=[pk2, 'convb'], writes=[('qk', fs, j)])
                S.op('pool', lambda e, j=j: e.tensor_copy(out=convbuf[j][:, 0:3], in_=convbuf[j][:, BLK:BLK + 3]),
                     reads=[('cb_body', j)], writes=[('cb_carry', j)])
                yield

        def fm_b(slot):
            for h in range(4):
                ps, pk = fm_proj(slot, J_MO + h)
                S.op('act', lambda e, ps=ps, h=h: e.activation(out=gM[0][:, h, :], in_=ps[:], func=AF.Sigmoid, bias=bias(J_MO + h), scale=1.0),
                     reads=[pk, 'b_fm'], writes=[('gM', 0, h)])
                yield
            for h in range(4):
                ps, pk = fm_proj(slot, J_HG + h)
                S.op('act', lambda e, ps=ps, h=h: e.activation(out=gH[0][:, h, :], in_=ps[:], func=AF.Silu, bias=bias(J_HG + h), scale=1.0),
                     reads=[pk, 'b_fm'], writes=[('gH', 0, h)])
                yield

        def tm_proj(slot, c, gs):
            cs = slice(c * 128, (c + 1) * 128)
            ps, pk = next_pb()
            for kt in range(KT):
                S.op('pe', lambda e, ps=ps, kt=kt: e.matmul(ps[:], lhsT=xb[slot][:, kt, cs], rhs=w_in[:, kt, C_HV:C_HV + 512],
                                                            start=(kt == 0), stop=(kt == KT - 1)),
                     reads=[('w_in', 'hv'), ('xb', slot, kt // 4)], writes=[pk], inc=(kt == KT - 1))
            S.op('dve', lambda e, ps=ps: e.tensor_tensor(out=vh[gs][:], in0=ps[:], in1=b_tm[:, 0:512], op=ALU.add),
                 reads=[pk, 'b_tm'], writes=[('vh', gs)])
            ps, pk = next_pb()
            for kt in range(KT):
                S.op('pe', lambda e, ps=ps, kt=kt: e.matmul(ps[:], lhsT=xb[slot][:, kt, cs], rhs=w_in[:, kt, C_MV:C_MV + 512],
                                                            start=(kt == 0), stop=(kt == KT - 1)),
                     reads=[('w_in', 'mv'), ('xb', slot, kt // 4)], writes=[pk], inc=(kt == KT - 1))
            S.op('dve', lambda e, ps=ps: e.tensor_tensor(out=vm[gs][:, :, 0:128],
                                                         in0=ps[:].rearrange("p (h v) -> p h v", v=128),
                                                         in1=b_tm[:, 512:1024].rearrange("p (h v) -> p h v", v=128), op=ALU.add),
                 reads=[pk, 'b_tm'], writes=[('vm', gs)])

        def scale_vm(gs):
            for h in range(4):
                S.op('act', lambda e, h=h: e.activation(out=vmw[gs][:, h, :], in_=vm[gs][:, h, :], func=AF.Identity, scale=wS[gs][:, h:h + 1]),
                     reads=[('vm', gs), ('wS', gs)], writes=[('vmw', gs)])

        def gates(slot, c, gs):
            cs = slice(c * 128, (c + 1) * 128)
            for kt in range(KT):
                S.op('pe', lambda e, kt=kt: e.matmul(PG[:, 0:8], lhsT=xb[slot][:, kt, cs], rhs=w_in[:, kt, C_IG:C_IG + 8],
                                                     start=(kt == 0), stop=(kt == KT - 1)),
                     reads=[('w_in', 'gt'), ('xb', slot, kt // 4)], writes=[PGK], inc=(kt == KT - 1))
            S.op('dve', lambda e: e.tensor_tensor(out=gsb[gs][:], in0=PG[:, 0:8], in1=b_tm[:, 1024:1032], op=ALU.add),
                 reads=[PGK, 'b_tm'], writes=[('gsb', gs)])
            S.op('act', lambda e: e.activation(out=gef[gs][:], in_=gsb[gs][:, 4:8], func=AF.Exp, scale=-1.0),
                 reads=[('gsb', gs)], writes=[('gef', gs)])
            S.op('act', lambda e: e.activation(out=gsp[gs][:], in_=gef[gs][:], func=AF.Ln, bias=1.0, scale=1.0),
                 reads=[('gef', gs)], writes=[('gsp', gs)])
            S.op('pe', lambda e: e.matmul(PG[:, 8:12], lhsT=mask01[:], rhs=gsp[gs][:], start=True, stop=True),
                 reads=['mask01', ('gsp', gs)], writes=[PGK])
            S.op('pe', lambda e: e.matmul(PG[:, 12:16], lhsT=onesF[:], rhs=gsp[gs][:], start=True, stop=True),
                 reads=['onesF', ('gsp', gs)], writes=[PGK])
            S.op('dve', lambda e: e.tensor_tensor(out=gtm[gs][:], in0=PG[:, 8:12], in1=gsb[gs][:, 0:4], op=ALU.add),
                 reads=[PGK, ('gsb', gs)], writes=[('gtm', gs)])
            S.op('act', lambda e: e.activation(out=wS[gs][:], in_=gtm[gs][:], func=AF.Exp), reads=[('gtm', gs)], writes=[('wS', gs)])
            S.op('act', lambda e: e.activation(out=rI[gs][:], in_=PG[:, 8:12], func=AF.Exp, scale=1.0, bias=math.log(8.0)),
                 reads=[PGK], writes=[('rT', gs)])
            S.op('act', lambda e: e.activation(out=egl[gs][:], in_=PG[:, 12:16], func=AF.Exp, scale=-1.0),
                 reads=[PGK], writes=[('egl', gs)])

        def grp_stages(kind, g, h0, fs, c, gs, own):
            cs = slice(c * 128, (c + 1) * 128)
            Xg, Yg, XKg, YKg = X[g], Y[g], XK[g], YK[g]
            hs = (h0, h0 + 1)
            jj = h0 // 2
            if kind == 'H':
                for k, h in enumerate(hs):
                    S.op('pe', lambda e, k=k, h=h: e.transpose(out=XT[g][:, k * 128:(k + 1) * 128], in_=kTt[fs][:, h, cs], identity=ident[:]),
                         reads=[('kTt', fs, h), 'ident'], writes=[XKg])
                if own:
                    for k, h in enumerate(hs):
                        S.op('pe', lambda e, k=k, h=h: e.matmul(Xg[:, k * 128:(k + 1) * 128], lhsT=kTt[fs][:, h, cs], rhs=qt[fs][:, h, cs],
                                                                start=True, stop=True),
                             reads=[('kTt', fs, h), ('qt', fs, h)], writes=[XKg])
            else:
                for k, h in enumerate(hs):
                    b0 = k * 64
                    S.op('pe', lambda e, k=k, b0=b0: e.transpose(out=XT[g][:, k * 128:k * 128 + 64], in_=qk[fs][b0:b0 + 64, 2 + jj, cs],
                                                                 identity=ident[b0:b0 + 64, b0:b0 + 64]),
                         reads=[('qk', fs, 2 + jj), 'ident'], writes=[XKg], serial=(k == 1))
                    if own:
                        S.op('pe', lambda e, k=k, b0=b0: e.matmul(Xg[:, k * 128:(k + 1) * 128], lhsT=qk[fs][b0:b0 + 64, 2 + jj, cs],
                                                                  rhs=qk[fs][b0:b0 + 64, jj, cs], start=True, stop=True),
                             reads=[('qk', fs, jj), ('qk', fs, 2 + jj)], writes=[XKg])
            yield
            if kind == 'H':
                S.op('act', lambda e: e.copy(out=kTM2[g][:].rearrange("p a b -> p (a b)"), in_=XT[g][:, 0:256]),
                     reads=[XKg], writes=[('kTM2', g)])
                if own:
                    S.op('dve', lambda e: e.copy_predicated(out=AT2[g][:].rearrange("p a b -> p (a b)"),
                                                            mask=mask2[:].rearrange("p a b -> p (a b)").bitcast(mybir.dt.int32),
                                                            data=Xg[:, 0:256]),
                         reads=[XKg, 'mask2', ('AT2', g)], writes=[('AT2', g)])
                    for h in hs:
                        S.op('dve', lambda e, h=h: e.tensor_scalar(out=Sbf[h][:], in0=Sst[h][:], scalar1=dec[fs][:, h, 8 + c:9 + c],
                                                                   scalar2=None, op0=ALU.mult),
                             reads=[('S', h), ('dec', fs, h, 2)], writes=[('Sbf', h)])
            else:
                S.op('act', lambda e: e.copy(out=kTM2[g][:, :, 0:64], in_=XT[g][:, 0:256].rearrange("p (k c) -> p k c", c=128)[:, :, 0:64]),
                     reads=[XKg], writes=[('kTM2', g)])
                if own:
                    S.op('dve', lambda e: e.copy_predicated(out=AT2[g][:].rearrange("p a b -> p (a b)"),
                                                            mask=mask2[:].rearrange("p a b -> p (a b)").bitcast(mybir.dt.int32),
                                                            data=Xg[:, 0:256]),
                         reads=[XKg, 'mask2', ('AT2', g)], writes=[('AT2', g)])
            yield
            for k, h in enumerate(hs):
                if kind == 'H':
                    v_ap = vh[gs][:, h * 128:(h + 1) * 128]
                    S.op('pe', lambda e, k=k, v_ap=v_ap: e.matmul(Yg[:, k * 128:(k + 1) * 128], lhsT=kTM2[g][:, k, :], rhs=v_ap, start=True, stop=True),
                         reads=[('kTM2', g), ('vh', gs)], writes=[YKg])
                else:
                    b0 = k * 64
                    S.op('pe', lambda e, k=k, h=h, b0=b0: e.matmul(Yg[b0:b0 + 64, 0:129], lhsT=kTM2[g][:, k, 0:64], rhs=vmw[gs][:, h, :], start=True, stop=True),
                         reads=[('kTM2', g), ('vmw', gs)], writes=[YKg])
            if own:
                for k, h in enumerate(hs):
                    if kind == 'H':
                        v_ap = vh[gs][:, h * 128:(h + 1) * 128]
                        po = Yg[:, 256 + k * 128:256 + (k + 1) * 128]
                        S.op('pe', lambda e, k=k, po=po, v_ap=v_ap: e.matmul(po, lhsT=AT2[g][:, k, :], rhs=v_ap, start=True, stop=False),
                             reads=[('AT2', g), ('vh', gs)], writes=[YKg], inc=False)
                        S.op('pe', lambda e, po=po, h=h: e.matmul(po, lhsT=qt[fs][:, h, cs], rhs=Sbf[h][:], start=False, stop=True),
                             reads=[('qt', fs, h), ('Sbf', h)], writes=[YKg])
                    else:
                        b0 = k * 64
                        po = Yg[:, 129 + k * 129:129 + (k + 1) * 129]
                        S.op('pe', lambda e, k=k, h=h, po=po: e.matmul(po, lhsT=AT2[g][:, k, :], rhs=vmw[gs][:, h, :], start=True, stop=False),
                             reads=[('AT2', g), ('vmw', gs)], writes=[YKg], inc=False)
                        S.op('pe', lambda e, h=h, po=po, b0=b0: e.matmul(po, lhsT=qk[fs][b0:b0 + 64, jj, cs], rhs=Cbf[h][b0:b0 + 64, :], start=False, stop=True),
                             reads=[('qk', fs, jj), ('Cbf', h)], writes=[YKg])
            yield
            for k, h in enumerate(hs):
                if kind == 'H':
                    S.op('dve', lambda e, h=h: e.tensor_scalar(out=Stmp[h][:], in0=Sst[h][:], scalar1=dec[fs][:, h, 4 + c:5 + c],
                                                               scalar2=None, op0=ALU.mult),
                         reads=[('S', h), ('dec', fs, h, 1)], writes=[('Stmp', h)])
                    S.op('dve', lambda e, k=k, h=h: e.scalar_tensor_tensor(out=Sst[h][:], in0=Yg[:, k * 128:(k + 1) * 128], scalar=dec[fs][:, h, c:c + 1],
                                                                           in1=Stmp[h][:], op0=ALU.mult, op1=ALU.add),
                         reads=[YKg, ('dec', fs, h, 0), ('Stmp', h)], writes=[('S', h)])
                else:
                    b0 = k * 64
                    S.op('dve', lambda e, h=h, b0=b0: e.tensor_scalar(out=Ctmp[h][b0:b0 + 64, :], in0=Cst[h][b0:b0 + 64, :],
                                                                      scalar1=egl[gs][b0:b0 + 64, h:h + 1], scalar2=None, op0=ALU.mult),
                         reads=[('C', h), ('egl', gs)], writes=[('Ctmp', h)])
                    S.op('dve', lambda e, h=h, b0=b0: e.scalar_tensor_tensor(out=Cst[h][b0:b0 + 64, :], in0=Yg[b0:b0 + 64, 0:129],
                                                                             scalar=egl[gs][b0:b0 + 64, h:h + 1], in1=Ctmp[h][b0:b0 + 64, :],
                                                                             op0=ALU.mult, op1=ALU.add),
                         reads=[YKg, ('egl', gs), ('Ctmp', h)], writes=[('C', h)])
                    S.op('pool', lambda e, h=h, b0=b0: e.tensor_copy(out=Cbf[h][b0:b0 + 64, :], in_=Cst[h][b0:b0 + 64, :]),
                         reads=[('C', h)], writes=[('Cbf', h)])
            if not own:
                yield
                return
            if kind == 'H':
                pos = [Yg[:, 256 + k * 128:256 + (k + 1) * 128] for k in range(2)]
            else:
                pos = [Yg[:, 129 + k * 129:129 + k * 129 + 128] for k in range(2)]
                m = mt2[g]
                den = Yg[:, 129:387].rearrange("p (k c) -> p k c", c=129)[:, :, 128]
                S.op('dve', lambda e: e.tensor_tensor(out=m[:, 0, :], in0=den, in1=rI[gs][:, h0:h0 + 2], op=ALU.max),
                     reads=[YKg, ('rT', gs)], writes=[('mt2', g)])
                S.op('dve', lambda e: e.scalar_tensor_tensor(out=m[:, 1, :], in0=den, scalar=-1.0, in1=m[:, 0, :], op0=ALU.mult, op1=ALU.max),
                     reads=[YKg, ('mt2', g)], writes=[('mt2', g)])
                S.op('dve', lambda e: e.reciprocal(out=m[:, 3, :], in_=m[:, 1, :]), reads=[('mt2', g)], writes=[('mt2', g)])
            for k in range(2):
                if kind == 'H':
                    S.op('act', lambda e, k=k: e.activation(out=junk[:], in_=pos[k], func=AF.Square, accum_out=ssq2[g][:, k:k + 1]),
                         reads=[YKg], writes=['junk', ('ssq2', g)])
                else:
                    S.op('act', lambda e, k=k: e.activation(out=junk[:], in_=pos[k], func=AF.Square, accum_out=ssq2[g][:, k:k + 1],
                                                            scale=mt2[g][:, 3, k:k + 1]),
                         reads=[YKg, ('mt2', g)], writes=['junk', ('ssq2', g)])
            S.op('act', lambda e: e.activation(out=rsd2[g][:], in_=ssq2[g][:], func=AF.Ln, scale=1.0 / 128.0, bias=RMS_EPS),
                 reads=[('ssq2', g)], writes=[('rsd2', g)])
            S.op('act', lambda e: e.activation(out=rsd2[g][:], in_=rsd2[g][:], func=AF.Exp, scale=-0.5),
                 reads=[('rsd2', g)], writes=[('rsd2', g)])
            if kind == 'M':
                S.op('dve', lambda e: e.tensor_tensor(out=rsd2[g][:], in0=rsd2[g][:], in1=mt2[g][:, 3, :], op=ALU.mult),
                     reads=[('rsd2', g), ('mt2', g)], writes=[('rsd2', g)])
            for k in range(2):
                S.op('act', lambda e, k=k: e.activation(out=on2[g][:, k, :], in_=pos[k], func=AF.Identity, scale=rsd2[g][:, k:k + 1]),
                     reads=[YKg, ('rsd2', g)], writes=[('on2', g)])
            yield
            for k in range(2):
                S.op('pe', lambda e, k=k: e.transpose(out=XT2[g][:, k * 128:(k + 1) * 128], in_=on2[g][:, k, :], identity=ident[:]),
                     reads=[('on2', g), 'ident'], writes=[XKg])
            yield
            j0 = h0 if kind == 'H' else 4 + h0
            gt = gH[0] if kind == 'H' else gM[0]
            gk = 'gH' if kind == 'H' else 'gM'
            S.op('dve', lambda e: e.tensor_tensor(out=ocat[:, j0:j0 + 2, cs], in0=XT2[g][:, 0:256].rearrange("p (a b) -> p a b", b=128),
                                                  in1=gt[:, h0:h0 + 2, cs], op=ALU.mult),
                 reads=[XKg, (gk, 0, h0), (gk, 0, h0 + 1)], writes=[('ocat', j0), ('ocat', j0 + 1)])
            yield

        def mixer_kind(kind, fs, c, gs, own, mid_hook=None):
            ga = grp_stages(kind, 0, 0, fs, c, gs, own)
            gb = grp_stages(kind, 1, 2, fs, c, gs, own)
            alive = [ga, gb]
            step = 0
            while alive:
                for g_ in list(alive):
                    try:
                        next(g_)
                    except StopIteration:
                        alive.remove(g_)
                step += 1
                if step == 4 and mid_hook is not None:
                    yield
                    mid_hook()
                yield

        def outproj_ln(blk, next_slot=None):
            W = BLK
            if W not in ln_tmp:
                ln_tmp[W] = dict(
                    ybf=[sb("ln_ybf%d_%d" % (W, i), [128, W], BF16) for i in range(2)],
                    ysq=[sb("ln_ysq%d_%d" % (W, i), [128, W], BF16) for i in range(2)],
                    rstd=sb("ln_rstd%d" % W, [128, W]),
                    nmr=sb("ln_nmr%d" % W, [128, W]),
                    t=[sb("ln_t%d_%d" % (W, i), [128, W]) for i in range(2)])
            L = ln_tmp[W]
            pm, pmk, pq, pqk = X[0], XK[0], Y[0], YK[0]

            def prep(n):
                S.op('dve', lambda e: e.tensor_copy(out=L['ybf'][n % 2][:], in_=xres[:, n, :]), reads=[('xres', n)], writes=[('ln_ybf', W, n % 2)])
                S.op('act', lambda e: e.activation(out=L['ysq'][n % 2][:], in_=xres[:, n, :], func=AF.Square),
                     reads=[('xres', n)], writes=[('ln_ysq', W, n % 2)])

            def stat(n):
                S.op('pe', lambda e: e.matmul(pm[:], lhsT=onesM[:], rhs=L['ybf'][n % 2][:], start=(n == 0), stop=(n == KT - 1)),
                     reads=[('ln_ybf', W, n % 2), 'onesM'], writes=[pmk])
                S.op('pe', lambda e: e.matmul(pq[:], lhsT=onesM[:], rhs=L['ysq'][n % 2][:], start=(n == 0), stop=(n == KT - 1)),
                     reads=[('ln_ysq', W, n % 2), 'onesM'], writes=[pqk])

            S.dma('sp', 'xres', lambda e: e.dma_start(out=xres[:], in_=xT[:, :, blk * BLK:(blk + 1) * BLK]),
                  writes=[('xres', n) for n in range(KT)])
            for n in range(KT + 2):
                if n < KT:
                    ps, pk = next_pb()
                    for kt in range(KT):
                        S.op('pe', lambda e, ps=ps, kt=kt, n=n: e.matmul(ps[:], lhsT=w_out[:, kt, n * 128:(n + 1) * 128], rhs=ocat[:, kt, :],
                                                                        start=(kt == 0), stop=(kt == KT - 1)),
                             reads=[('w_out', kt), ('ocat', kt)], writes=[pk], inc=(kt == KT - 1))
                    S.op('dve', lambda e, ps=ps, n=n: e.scalar_tensor_tensor(out=xres[:, n, :], in0=xres[:, n, :], scalar=ALPHA, in1=ps[:],
                                                                             op0=ALU.mult, op1=ALU.add),
                         reads=[pk, ('xres', n)], writes=[('xres', n)])
                if 0 <= n - 1 < KT:
                    prep(n - 1)
                if 0 <= n - 2 < KT:
                    stat(n - 2)
                yield
            msq = L['t'][0]
            S.op('act', lambda e: e.activation(out=msq[:], in_=pm[:], func=AF.Square), reads=[pmk], writes=[('ln_t', W, 0)])
            S.op('dve', lambda e: e.scalar_tensor_tensor(out=L['rstd'][:], in0=pq[:], scalar=LN_EPS, in1=msq[:], op0=ALU.add, op1=ALU.subtract),
                 reads=[pqk, ('ln_t', W, 0)], writes=[('ln_rstd', W)])
            S.op('act', lambda e: e.activation(out=L['rstd'][:], in_=L['rstd'][:], func=AF.Ln), reads=[('ln_rstd', W)], writes=[('ln_rstd', W)])
            S.op('act', lambda e: e.activation(out=L['rstd'][:], in_=L['rstd'][:], func=AF.Exp, scale=-0.5), reads=[('ln_rstd', W)], writes=[('ln_rstd', W)])
            S.op('dve', lambda e: e.scalar_tensor_tensor(out=L['nmr'][:], in0=pm[:], scalar=-1.0, in1=L['rstd'][:], op0=ALU.mult, op1=ALU.mult),
                 reads=[pmk, ('ln_rstd', W)], writes=[('ln_nmr', W)])
            yield
            fb = fm_b(next_slot) if next_slot is not None else iter(())
            for n in range(KT):
                try:
                    next(fb)
                except StopIteration:
                    pass
                t = L['t'][n % 2]
                S.op('dve', lambda e, n=n, t=t: e.tensor_tensor(out=t[:], in0=xres[:, n, :], in1=L['rstd'][:], op=ALU.mult),
                     reads=[('xres', n), ('ln_rstd', W)], writes=[('ln_t', W, n % 2)])
                S.op('dve', lambda e, n=n, t=t: e.tensor_tensor(out=t[:], in0=t[:], in1=L['nmr'][:], op=ALU.add),
                     reads=[('ln_t', W, n % 2), ('ln_nmr', W)], writes=[('ln_t', W, n % 2)])
                S.op('act', lambda e, n=n, t=t: e.activation(out=xres[:, n, :], in_=t[:], func=AF.Identity,
                                                             bias=lnp[:, 8 + n:9 + n], scale=lnp[:, n:n + 1]),
                     reads=[('ln_t', W, n % 2), 'lnp'], writes=[('xres', n)])
                yield
            S.dma('sp', 'x1st', lambda e: e.dma_start(out=x1s[:, :, blk * BLK:(blk + 1) * BLK], in_=xres[:]),
                  reads=[('xres', n) for n in range(KT)], writes=[('x1s', blk)])
            yield

        nblk_pre = T_PRE // BLK
        nblk_own = T_OWN // BLK
        seq = [(xTp, b, False) for b in range(nblk_pre)] + [(xT, b, True) for b in range(nblk_own)]
        gch = {'n': 0}

        def ch_all(slot, fs, own, prefetch=None):
            g0 = gch['n']
            gch['n'] += 4
            gates(slot, 0, g0 % 2)
            yield
            tm_proj(slot, 0, g0 % 2)
            scale_vm(g0 % 2)
            yield
            for c in range(4):
                gs = (g0 + c) % 2
                if c == 3 and prefetch is not None:
                    load_x(prefetch[0], prefetch[1], slot)
                yield from mixer_kind('H', fs, c, gs, own)
                if c < 3:
                    gates(slot, c + 1, (gs + 1) % 2)
                    yield
                    if own:
                        yield from mixer_kind('M', fs, c, gs, own, mid_hook=lambda c=c, gs=gs: (tm_proj(slot, c + 1, (gs + 1) % 2), scale_vm((gs + 1) % 2)))
                    else:
                        yield from mixer_kind('M', fs, c, gs, own)
                        tm_proj(slot, c + 1, (gs + 1) % 2)
                        scale_vm((gs + 1) % 2)
                        yield
                else:
                    yield from mixer_kind('M', fs, c, gs, own)

        def carry_flag():
            for j in range(4):
                S.op('dve', lambda e, j=j: e.tensor_scalar(out=convbuf[j][:, 0:3], in0=convbuf[j][:, 0:3], scalar1=flag[:, 0:1],
                                                           scalar2=None, op0=ALU.mult),
                     reads=[('cb_carry', j), 'flag'], writes=[('cb_carry', j)])

        def state_flag():
            for h in range(4):
                S.op('dve', lambda e, h=h: e.tensor_scalar(out=Sst[h][:], in0=Sst[h][:], scalar1=flag[:, 0:1], scalar2=None, op0=ALU.mult),
                     reads=[('S', h), 'flag'], writes=[('S', h)])
                S.op('dve', lambda e, h=h: e.tensor_scalar(out=Cst[h][:], in0=Cst[h][:], scalar1=flag[:, 0:1], scalar2=None, op0=ALU.mult),
                     reads=[('C', h), 'flag'], writes=[('C', h)])
                S.op('dve', lambda e, h=h: e.tensor_copy(out=Cbf[h][:], in_=Cst[h][:]), reads=[('C', h)], writes=[('Cbf', h)])

        load_x(seq[0][0], seq[0][1], 0)
        load_w_in(True)
        load_w_in(False)
        if len(seq) > 1:
            load_x(seq[1][0], seq[1][1], 1)
        if seq[0][2]:
            carry_flag()
        def needq(i):
            return seq[i][2] or (i + 1 < len(seq) and seq[i + 1][2])
        run(fm_a(0, 0, seq[0][2], needq(0)))
        load_w_out()
        for bi, (src, blk, own) in enumerate(seq):
            slot = bi % 2
            nxt = seq[bi + 1] if bi + 1 < len(seq) else None
            nn = seq[bi + 2] if bi + 2 < len(seq) else None
            if own and blk == 0:
                state_flag()
            main = [ch_all(slot, slot, own, prefetch=(nn[0], nn[1]) if nn is not None else None)]
            nmain = 4 * (2 + 14)
            if own:
                nxt_own = nxt is not None and nxt[2]
                if blk == 0:
                    main = [fm_b(slot)] + main
                    nmain += 8
                main = main + [outproj_ln(blk, (bi + 1) % 2 if nxt_own else None)]
                nmain += 21
            if nxt is not None:
                if nxt[2] and nxt[1] == 0:
                    carry_flag()
                merge(chain(*main), nmain, fm_a((bi + 1) % 2, (bi + 1) % 2, nxt[2], needq(bi + 1)), 48 if nxt[2] else 28,
                      frac=1.0)
            else:
                run(chain(*main))
            if own and blk == 0:
                dump('ocat', lambda: ocat[:], [128, 8, BLK], [('ocat', j) for j in range(8)])

    def phase_b():
        W = BLKB
        wg = sb("wg_bf", [128, KT, DFF], BF16)
        wu = sb("wu_bf", [128, KT, DFF], BF16)
        wd = sb("wd_bf", [128, FT, D], BF16)
        pwp = sb("pwp_bf", [128, 2, D], BF16)
        pwg = sb("pwg_bf", [128, KT, D], BF16)
        pbg = sb("pbg_sb", [128, 8])
        xr = [sb("xr%d" % i, [128, KT, W]) for i in range(2)]
        xbf = [sb("xbf%d" % i, [128, KT, W], BF16) for i in range(3)]
        pb = [sb("pb%d" % i, [128, 2, W], BF16) for i in range(2)]
        _hT = sb("hT", [128, FT, W], BF16)
        hT = [_hT, _hT]
        sg = [sb("sg%d" % i, [128, W]) for i in range(2)]
        sgp = [sb("sgp%d" % i, [128, W]) for i in range(2)]
        def load_weights():
            FG = 4
            for f0 in range(0, FT, FG):
                f1 = min(FT, f0 + FG)
                S.dma('pool', 'wg%d' % f0, lambda e, f0=f0, f1=f1: e.dma_start(out=wg[:, :, f0 * 128:f1 * 128], in_=wg_d[:, :, f0 * 128:f1 * 128]),
                      writes=[('wg', f) for f in range(f0, f1)])
                S.dma('pool', 'wu%d' % f0, lambda e, f0=f0, f1=f1: e.dma_start(out=wu[:, :, f0 * 128:f1 * 128], in_=wu_d[:, :, f0 * 128:f1 * 128]),
                      writes=[('wu', f) for f in range(f0, f1)])
            for f0 in range(0, FT, FG):
                f1 = min(FT, f0 + FG)
                S.dma('pool', 'wd%d' % f0, lambda e, f0=f0, f1=f1: e.dma_start(out=wd[:, f0:f1, :], in_=wd_d[:, f0:f1, :]),
                      writes=[('wd', f) for f in range(f0, f1)])
            for kt in range(KT):
                S.dma('pool', 'pwg%d' % kt, lambda e, kt=kt: e.dma_start(out=pwg[:, kt, :], in_=pwg_d[:, kt, :]), writes=[('pwg', kt)])
            S.dma('pool', 'pwp', lambda e: e.dma_start(out=pwp[:], in_=pwp_d), writes=['pwp'])
        S.dma('sp', 'pbg', lambda e: e.dma_start(out=pbg[:], in_=pbg_d), writes=['pbg'])

        nb = T_OWN // W

        def load_r(b):
            s = b % 2
            ts = slice(b * W, (b + 1) * W)
            S.dma('sp', 'xr%d' % s, lambda e: e.dma_start(out=xr[s][:], in_=x1s[:, :, ts]),
                  reads=[('x1s', (b * W) // BLK)], writes=[('xr', s, n) for n in range(KT)])
            S.dma('pool', 'pb%d' % s, lambda e: e.dma_start(out=pb[s][:], in_=pT[:, :, ts]), writes=[('pb', s)])

        def load_bf(b):
            s3 = b % 3
            ts = slice(b * W, (b + 1) * W)
            S.dma('pool', 'xbf%d' % s3, lambda e: e.dma_start(out=xbf[s3][:], in_=x1s[:, :, ts]),
                  reads=[('x1s', (b * W) // BLK)], writes=[('xbf', s3, n) for n in range(KT)])

        def gateup(b):
            s = b % 2
            s3 = b % 3
            for f in range(FT):
                pg_, pgk = next_pb()
                for kt in range(KT):
                    S.op('pe', lambda e, pg_=pg_, kt=kt, f=f: e.matmul(pg_[:, 0:W], lhsT=wg[:, kt, f * 128:(f + 1) * 128], rhs=xbf[s3][:, kt, :],
                                                                      start=(kt == 0), stop=(kt == KT - 1)),
                         reads=[('wg', f), ('xbf', s3, kt)], writes=[pgk], inc=(kt == KT - 1))
                pu_, puk = next_pb()
                for kt in range(KT):
                    S.op('pe', lambda e, pu_=pu_, kt=kt, f=f: e.matmul(pu_[:, 0:W], lhsT=wu[:, kt, f * 128:(f + 1) * 128], rhs=xbf[s3][:, kt, :],
                                                                      start=(kt == 0), stop=(kt == KT - 1)),
                         reads=[('wu', f), ('xbf', s3, kt)], writes=[puk], inc=(kt == KT - 1))
                S.op('act', lambda e, pg_=pg_, f=f: e.activation(out=sg[f % 2][:], in_=pg_[:, 0:W], func=AF.Silu),
                     reads=[pgk], writes=[('sg', f % 2)])
                S.op('dve', lambda e, pu_=pu_, f=f: e.tensor_tensor(out=hT[s][:, f, :], in0=pu_[:, 0:W], in1=sg[f % 2][:], op=ALU.mult),
                     reads=[puk, ('sg', f % 2)], writes=[('hT', f)])
                yield

        def down(b):
            s = b % 2
            for n in range(KT):
                pd_, pdk = next_pb()
                for f in range(FT):
                    S.op('pe', lambda e, pd_=pd_, f=f, n=n: e.matmul(pd_[:, 0:W], lhsT=wd[:, f, n * 128:(n + 1) * 128], rhs=hT[s][:, f, :],
                                                                    start=(f == 0), stop=(f == FT - 1)),
                         reads=[('wd', f), ('hT', f)], writes=[pdk], inc=(f == FT - 1))
                S.op('dve', lambda e, pd_=pd_, n=n: e.scalar_tensor_tensor(out=xr[s][:, n, :], in0=xr[s][:, n, :], scalar=ALPHA, in1=pd_[:, 0:W],
                                                                           op0=ALU.mult, op1=ALU.add),
                     reads=[pdk, ('xr', s, n)], writes=[('xr', s, n)])
                yield

        BK_A, BK_S, BK_O = XK[0], XK[1], YK[0]
        PMA, PMS, PMO = X[0], X[1], Y[0]

        L2 = dict(ybf=[sb("l2_ybf%d" % i, [128, W], BF16) for i in range(2)],
                  ysq=[sb("l2_ysq%d" % i, [128, W], BF16) for i in range(2)],
                  rstd=sb("l2_rstd", [128, W]), nmr=sb("l2_nmr", [128, W]),
                  t=[sb("l2_t%d" % i, [128, W]) for i in range(2)])

        def ln_prep(s, n):
            S.op('dve', lambda e: e.tensor_copy(out=L2['ybf'][n % 2][:], in_=xr[s][:, n, :]), reads=[('xr', s, n)], writes=[('l2ybf', n % 2)])
            S.op('act', lambda e: e.activation(out=L2['ysq'][n % 2][:], in_=xr[s][:, n, :], func=AF.Square),
                 reads=[('xr', s, n)], writes=[('l2ysq', n % 2)])

        def ln_stat(s, n):
            S.op('pe', lambda e: e.matmul(PMA[:, 0:W], lhsT=onesM[:], rhs=L2['ybf'][n % 2][:], start=(n == 0), stop=(n == KT - 1)),
                 reads=[('l2ybf', n % 2), 'onesM'], writes=[BK_A])
            S.op('pe', lambda e: e.matmul(PMS[:, 0:W], lhsT=onesM[:], rhs=L2['ysq'][n % 2][:], start=(n == 0), stop=(n == KT - 1)),
                 reads=[('l2ysq', n % 2), 'onesM'], writes=[BK_S])

        def ln_chain(s):
            msq = L2['t'][0]
            S.op('act', lambda e: e.activation(out=msq[:], in_=PMA[:, 0:W], func=AF.Square), reads=[BK_A], writes=[('l2t', 0)])
            S.op('dve', lambda e: e.scalar_tensor_tensor(out=L2['rstd'][:], in0=PMS[:, 0:W], scalar=LN_EPS, in1=msq[:],
                                                         op0=ALU.add, op1=ALU.subtract),
                 reads=[BK_S, ('l2t', 0)], writes=['l2rstd'])
            S.op('act', lambda e: e.activation(out=L2['rstd'][:], in_=L2['rstd'][:], func=AF.Ln), reads=['l2rstd'], writes=['l2rstd'])
            S.op('act', lambda e: e.activation(out=L2['rstd'][:], in_=L2['rstd'][:], func=AF.Exp, scale=-0.5), reads=['l2rstd'], writes=['l2rstd'])
            S.op('dve', lambda e: e.scalar_tensor_tensor(out=L2['nmr'][:], in0=PMA[:, 0:W], scalar=-1.0, in1=L2['rstd'][:],
                                                         op0=ALU.mult, op1=ALU.mult),
                 reads=[BK_A, 'l2rstd'], writes=['l2nmr'])

        def ln_norm(s, n, s3):
            t = L2['t'][n % 2]
            S.op('dve', lambda e: e.tensor_tensor(out=t[:], in0=xr[s][:, n, :], in1=L2['rstd'][:], op=ALU.mult),
                 reads=[('xr', s, n), 'l2rstd'], writes=[('l2t', n % 2)])
            S.op('dve', lambda e: e.tensor_tensor(out=t[:], in0=t[:], in1=L2['nmr'][:], op=ALU.add),
                 reads=[('l2t', n % 2), 'l2nmr'], writes=[('l2t', n % 2)])
            S.op('act', lambda e: e.activation(out=xr[s][:, n, :], in_=t[:], func=AF.Identity,
                                               bias=lnp[:, 24 + n:25 + n], scale=lnp[:, 16 + n:17 + n]),
                 reads=[('l2t', n % 2), 'lnp'], writes=[('xr', s, n)])
            S.op('dve', lambda e: e.tensor_copy(out=xbf[s3][:, n, :], in_=xr[s][:, n, :]), reads=[('xr', s, n)], writes=[('xbf', s3, n)])

        def ple(b):
            s = b % 2
            s3 = b % 3
            ts = slice(b * W, (b + 1) * W)
            for n in range(KT):
                pgo, BK_O = (Y[0], YK[0]) if n % 2 == 0 else (Y[1], YK[1])
                for kt in range(KT):
                    S.op('pe', lambda e, kt=kt, n=n, pgo=pgo: e.matmul(pgo[:, 0:W], lhsT=pwg[:, kt, n * 128:(n + 1) * 128], rhs=xbf[s3][:, kt, :],
                                                                      start=(kt == 0), stop=(kt == KT - 1)),
                         reads=[('pwg', kt), ('xbf', s3, kt)], writes=[BK_O], inc=(kt == KT - 1))
                pp_, ppk = (PMA, BK_A) if n % 2 == 0 else (PMS, BK_S)
                for k2 in range(2):
                    S.op('pe', lambda e, pp_=pp_, k2=k2, n=n: e.matmul(pp_[:, 0:W], lhsT=pwp[:, k2, n * 128:(n + 1) * 128], rhs=pb[s][:, k2, :],
                                                                      start=(k2 == 0), stop=(k2 == 1)),
                         reads=['pwp', ('pb', s)], writes=[ppk], inc=(k2 == 1))
                S.op('act', lambda e, n=n, pgo=pgo: e.activation(out=sgp[n % 2][:], in_=pgo[:, 0:W], func=AF.Sigmoid,
                                                                 bias=pbg[:, n:n + 1], scale=1.0),
                     reads=[BK_O, 'pbg'], writes=[('sgp', n % 2)])
                S.op('dve', lambda e, pp_=pp_, n=n: e.tensor_tensor(out=sgp[n % 2][:], in0=pp_[:, 0:W], in1=sgp[n % 2][:], op=ALU.mult),
                     reads=[ppk, ('sgp', n % 2)], writes=[('sgp', n % 2)])
                S.op('pool', lambda e, n=n: e.tensor_tensor(out=xr[s][:, n, :], in0=xr[s][:, n, :], in1=sgp[n % 2][:], op=ALU.add),
                     reads=[('xr', s, n), ('sgp', n % 2)], writes=[('xr', s, n)])
            S.dma('sp', 'out%d' % s, lambda e: e.dma_start(out=outT[:, :, ts], in_=xr[s][:]),
                  reads=[('xr', s, n) for n in range(KT)], writes=[('out', s)])

        load_bf(0)
        load_r(0)
        if nb > 1:
            load_bf(1)
        load_weights()
        run(gateup(0))
        run(down(0))
        for b in range(nb):
            s = b % 2
            if b + 2 < nb:
                load_bf(b + 2)
            if b + 1 < nb:
                load_r(b + 1)
                g = gateup(b + 1)
            else:
                g = iter(())
            f = 0
            alive = True
            while alive or f <= 2 * (KT - 1) + 2:
                if f % 2 == 0 and f // 2 < KT:
                    ln_prep(s, f // 2)
                try:
                    next(g)
                except StopIteration:
                    alive = False
                if f % 2 == 0 and 0 <= f // 2 - 1 < KT:
                    ln_stat(s, f // 2 - 1)
                f += 1
            ln_chain(s)
            d = down(b + 1) if b + 1 < nb else iter(())
            for n in range(KT):
                try:
                    next(d)
                except StopIteration:
                    pass
                ln_norm(s, n, b % 3)
            run(d)
            ple(b)
        S.final_wait('sp', [('out', 0), ('out', 1)])

    base = len(ctxs)
    phase_a()
    S.final_wait('sp', [('x1s', b) for b in range(T_OWN // BLK)] + [('dbg', n) for n in dbg_out])
    S.emit()
    for cm in reversed(ctxs[base:]):
        cm.__exit__(None, None, None)
    del ctxs[base:]
    phase_b()
    S.emit()
    for cm in reversed(ctxs):
        cm.__exit__(None, None, None)
    S.close()
    return nc, S, dbg_out


def _tile_rows(w):
    K, N = w.shape
    return np.ascontiguousarray(w.reshape(K // 128, 128, N).transpose(1, 0, 2))


def _cols(v):
    return np.ascontiguousarray(v.reshape(-1, 128).T)


def _tokT(a):
    T, F = a.shape
    return np.ascontiguousarray(a.T.reshape(F // 128, 128, T).transpose(1, 0, 2))


def make_weights(inp):
    f = np.float32
    w_in = np.asarray(inp["w_in"], f)[0]
    b_in = np.asarray(inp["b_in"], f)[0]
    lg = np.asarray(inp["hg_lb_logits"], f)
    cw = np.asarray(inp["ml_conv_w"], f)[0]
    cb = np.asarray(inp["ml_conv_b"], f)[0]
    W = {}
    W["w_in"] = _tile_rows(w_in)
    W["b_fm"] = np.ascontiguousarray(np.stack([b_in[c:c + 128] for c in FM_COLS], axis=1))
    btm = np.concatenate([b_in[C_HV:C_HV + 512], b_in[C_MV:C_MV + 512], b_in[C_IG:C_IG + 8]])
    W["b_tm"] = np.ascontiguousarray(np.broadcast_to(btm[None, :], (128, 1032)))
    W["lbl"] = np.ascontiguousarray(np.concatenate([_cols(lg[0]), _cols(lg[1])], axis=1))
    W["convw"] = np.ascontiguousarray(cw.reshape(4, 4, 128).transpose(2, 1, 0).reshape(128, 16))
    W["convb"] = _cols(cb)
    W["gain"] = _cols(np.concatenate([np.asarray(inp["hg_norm_g"], f)[0], np.asarray(inp["ml_norm_g"], f)[0]]))
    W["w_out"] = _tile_rows(np.asarray(inp["w_out"], f)[0])
    W["lnp"] = np.ascontiguousarray(np.concatenate([_cols(np.asarray(inp[k], f)[0]) for k in ("ln1_g", "ln1_b", "ln2_g", "ln2_b")], axis=1))
    W["wg"] = _tile_rows(np.asarray(inp["w_ffn_gate"], f)[0])
    W["wu"] = _tile_rows(np.asarray(inp["w_ffn_up"], f)[0])
    W["wd"] = _tile_rows(np.asarray(inp["w_ffn_down"], f)[0])
    W["pwp"] = _tile_rows(np.asarray(inp["ple_w_proj"], f)[0])
    W["pwg"] = _tile_rows(np.asarray(inp["ple_w_gate"], f)[0])
    W["pbg"] = _cols(np.asarray(inp["ple_b_gate"], f)[0])
    return W


def make_core(W, x_own, x_pre, p_own, flagv):
    m = dict(W)
    m["xT"] = _tokT(x_own)
    m["xTp"] = _tokT(x_pre)
    m["pT"] = _tokT(p_own)
    m["flag"] = np.full((128, 1), flagv, np.float32)
    return m


def untile_out(oT):
    return np.ascontiguousarray(oT.transpose(1, 0, 2).reshape(D, -1).T)


_NC_CACHE = {}


def kernel(**inputs):
    x = np.asarray(inputs["x"], np.float32)
    p = np.asarray(inputs["p"], np.float32)[0]
    B, SEQ, _ = x.shape
    HALF = SEQ // 2
    key = (HALF,)
    if key not in _NC_CACHE:
        _NC_CACHE[key] = build(HALF, HALF)[0]
    nc = _NC_CACHE[key]
    W = make_weights(inputs)
    in_maps = []
    for b in range(B):
        for h in range(2):
            t0 = h * HALF
            in_maps.append(make_core(W, x[b, t0:t0 + HALF], x[b, 0:HALF], p[b, t0:t0 + HALF], float(h)))
    res = run_bass_kernel_spmd(nc, in_maps, core_ids=list(range(2 * B)))
    out = np.empty((B, SEQ, D), np.float32)
    for b in range(B):
        for h in range(2):
            out[b, h * HALF:(h + 1) * HALF] = untile_out(np.asarray(res.results[2 * b + h]["outT"]))
    return out
```

```python
import math
import numpy as np
import concourse.bass as bass
import concourse.mybir as mybir
from concourse.bass_utils import run_bass_kernel_spmd

F32 = mybir.dt.float32
BF16 = mybir.dt.bfloat16
AF = mybir.ActivationFunctionType
ALU = mybir.AluOpType

D = 1024
KT = 8
PW = 3592
DFF = 2816
FT = 22
ALPHA = float(2.0 ** 0.25)
LN_EPS = 1e-5
RMS_EPS = 1e-6
C_HQ, C_HF, C_HV, C_HG, C_MQ, C_MK, C_MV, C_MO, C_IG = 0, 512, 1024, 1536, 2048, 2304, 2560, 3072, 3584
FM_COLS = ([C_HQ + 128 * j for j in range(4)] + [C_HF + 128 * j for j in range(4)] +
           [C_HG + 128 * j for j in range(4)] + [C_MQ + 128 * j for j in range(4)] +
           [C_MO + 128 * j for j in range(4)])
J_HQ, J_HF, J_HG, J_QK, J_MO = 0, 4, 8, 12, 16
BLK = 512
BLKB = 256
MID = 63


class Sched:
    def __init__(self, nc):
        self.nc = nc
        self.prog = {e: [] for e in ('pe', 'act', 'dve', 'pool', 'sp')}
        self.sems = {}
        self.cnt = {}
        self.seen = {e: {} for e in self.prog}
        self.lastw = {}
        self.readers = {}
        self._stack = []
        self.nops = {e: 0 for e in self.prog}
        self.bank_last = {}

    def sem(self, name):
        if name not in self.sems:
            cm = self.nc.semaphore(name)
            self.sems[name] = cm.__enter__()
            self._stack.append(cm)
            self.cnt[name] = 0
        return self.sems[name]

    @staticmethod
    def _isbank(k):
        return isinstance(k, tuple) and k[0] == 'BANK'

    def _deps(self, reads, writes, own=None):
        deps = {}

        def add(d):
            if d is not None and deps.get(d[0], 0) < d[1]:
                deps[d[0]] = d[1]
        for k in list(reads) + list(writes):
            if self._isbank(k):
                d = self.bank_last.get(k)
                if d is not None and d[0] != own:
                    add(d)
        for k in reads:
            if not self._isbank(k):
                add(self.lastw.get(k))
        for k in writes:
            if not self._isbank(k):
                add(self.lastw.get(k))
                for d in self.readers.get(k, ()):
                    add(d)
        return deps

    def _waits(self, e, deps, skip=None):
        waits = []
        for s, v in deps.items():
            if s == skip or self.seen[e].get(s, 0) >= v:
                continue
            self.seen[e][s] = v
            waits.append((self.sems[s], v))
        return waits

    def _record(self, reads, writes, tag):
        for k in reads:
            if self._isbank(k):
                self.bank_last[k] = tag
            else:
                self.readers.setdefault(k, []).append(tag)
        for k in writes:
            if self._isbank(k):
                self.bank_last[k] = tag
            else:
                self.lastw[k] = tag
                self.readers[k] = []

    def op(self, e, fn, reads=(), writes=(), inc=True, serial=False):
        sname = 'c_' + e
        h = self.sem(sname)
        deps = self._deps(reads, writes, own=sname)
        skip = sname if e == 'pe' else None
        if serial and self.cnt[sname] > 0:
            deps[sname] = max(deps.get(sname, 0), self.cnt[sname])
            skip = None
        waits = self._waits(e, deps, skip=skip)
        val = self.cnt[sname] + 1
        if inc:
            self.cnt[sname] = val

        def run(eng, fn=fn, waits=waits, inc=inc, h=h):
            for sh, v in waits:
                eng.wait_ge(sh, v)
            ins = fn(eng)
            if inc:
                ins.then_inc(h, 1)
        self.prog[e].append(run)
        self.nops[e] += 1
        self._record(reads, writes, (sname, val))

    def dma(self, e, stream, fn, reads=(), writes=()):
        sname = 'd_' + stream
        h = self.sem(sname)
        waits = self._waits(e, self._deps(reads, writes))
        self.cnt[sname] += 16
        val = self.cnt[sname]

        def run(eng, fn=fn, waits=waits, h=h):
            for sh, v in waits:
                eng.wait_ge(sh, v)
            fn(eng).then_inc(h, 16)
        self.prog[e].append(run)
        self.nops[e] += 1
        self._record(reads, writes, (sname, val))

    def final_wait(self, e, keys):
        deps = {}
        for k in keys:
            d = self.lastw.get(k)
            if d and deps.get(d[0], 0) < d[1]:
                deps[d[0]] = d[1]
        waits = self._waits(e, deps)

        def run(eng, waits=waits):
            for sh, v in waits:
                eng.wait_ge(sh, v)
        self.prog[e].append(run)

    def emit(self):
        with self.nc.Block() as block:
            for e, dec in (('sp', block.sync), ('act', block.scalar), ('dve', block.vector),
                           ('pool', block.gpsimd), ('pe', block.tensor)):
                prog = self.prog[e]

                def body(eng, prog=prog):
                    for r in prog:
                        r(eng)
                dec(body)
        self.prog = {e: [] for e in self.prog}

    def close(self):
        for cm in reversed(self._stack):
            cm.__exit__(None, None, None)


def build(T_PRE, T_OWN, debug=()):
    assert T_PRE % BLK == 0 and T_OWN % BLK == 0
    nc = bass.Bass("TRN2", target_bir_lowering=False)
    S = Sched(nc)
    ctxs = []

    def dram(name, shape, kind="ExternalInput", dt=F32):
        return nc.dram_tensor(name, list(shape), dt, kind=kind).ap()

    def sb(name, shape, dt=F32):
        cm = nc.sbuf_tensor(name, list(shape), dt)
        t = cm.__enter__()
        ctxs.append(cm)
        return t

    def psum(name, shape, dt=F32):
        cm = nc.psum_tensor(name, list(shape), dt)
        t = cm.__enter__()
        ctxs.append(cm)
        return t

    TP = max(T_PRE, BLK)
    xT = dram("xT", [128, KT, T_OWN])
    xTp = dram("xTp", [128, KT, TP])
    pT = dram("pT", [128, 2, T_OWN])
    flag_d = dram("flag", [128, 1])
    w_in_d = dram("w_in", [128, KT, PW])
    b_fm_d = dram("b_fm", [128, 20])
    b_tm_d = dram("b_tm", [128, 1032])
    lbl_d = dram("lbl", [128, 8])
    convw_d = dram("convw", [128, 16])
    convb_d = dram("convb", [128, 4])
    gain_d = dram("gain", [128, 8])
    w_out_d = dram("w_out", [128, KT, D])
    lnp_d = dram("lnp", [128, 32])
    wg_d = dram("wg", [128, KT, DFF])
    wu_d = dram("wu", [128, KT, DFF])
    wd_d = dram("wd", [128, FT, D])
    pwp_d = dram("pwp", [128, 2, D])
    pwg_d = dram("pwg", [128, KT, D])
    pbg_d = dram("pbg", [128, 8])
    outT = dram("outT", [128, KT, T_OWN], kind="ExternalOutput")
    x1s = dram("x1s", [128, KT, T_OWN], kind="Internal")
    dbg_out = {}

    def dump(name, ap_fn, shape, reads):
        if name not in debug:
            return
        d = dram("dbg_" + name, shape, kind="ExternalOutput")
        dbg_out[name] = d
        S.dma('sp', 'dbg_' + name, lambda e: e.dma_start(out=d, in_=ap_fn()), reads=reads, writes=[('dbg', name)])

    mask01 = sb("mask01", [128, 128])
    identf = sb("identf", [128, 128])
    ident = sb("ident", [128, 128], BF16)
    onesF = sb("onesF", [128, 128])
    onesM = sb("onesM", [128, 128], BF16)
    flag = sb("flag_sb", [128, 1])
    lnp = sb("lnp_sb", [128, 32])

    S.op('pool', lambda e: e.memset(mask01[:], 1.0), writes=['mask01'])
    S.op('pool', lambda e: e.affine_select(out=mask01[:], in_=mask01[:], pattern=[[1, 128]], compare_op=ALU.is_ge,
                                           fill=0.0, base=0, channel_multiplier=-1), reads=['mask01'], writes=['mask01'])
    mask2 = sb("mask2", [128, 2, 128])
    S.op('pool', lambda e: e.memset(mask2[:], 1.0), writes=['mask2'])
    S.op('pool', lambda e: e.affine_select(out=mask2[:], in_=mask2[:], pattern=[[0, 2], [1, 128]], compare_op=ALU.is_ge,
                                           fill=0.0, base=0, channel_multiplier=-1), reads=['mask2'], writes=['mask2'])
    S.op('pool', lambda e: e.memset(identf[:], 1.0), writes=['identf'])
    S.op('pool', lambda e: e.affine_select(out=identf[:], in_=identf[:], pattern=[[1, 128]], compare_op=ALU.is_equal,
                                           fill=0.0, base=0, channel_multiplier=-1), reads=['identf'], writes=['identf'])
    S.op('dve', lambda e: e.tensor_copy(out=ident[:], in_=identf[:]), reads=['identf'], writes=['ident'])
    S.op('pool', lambda e: e.memset(onesF[:], 1.0), writes=['onesF'])
    S.op('pool', lambda e: e.memset(onesM[:], 1.0 / 1024.0), writes=['onesM'])
    S.dma('sp', 'c0', lambda e: e.dma_start(out=flag[:], in_=flag_d), writes=['flag'])
    S.dma('sp', 'c1', lambda e: e.dma_start(out=lnp[:], in_=lnp_d), writes=['lnp'])

    PB = [psum("PB%d" % i, [128, 512]) for i in range(4)]
    X = [psum("X%d" % i, [128, 512]) for i in range(2)]
    Y = [psum("Y%d" % i, [128, 512]) for i in range(2)]
    XK = [('BANK', 'X0'), ('BANK', 'X1')]
    YK = [('BANK', 'Y0'), ('BANK', 'Y1')]
    XT = [X[i][:, 256:384].bitcast(BF16) for i in range(2)]
    XT2 = [X[i][:, 384:512].bitcast(BF16) for i in range(2)]
    rr = {'pb': 0}

    def next_pb():
        i = rr['pb'] % 4
        rr['pb'] += 1
        return PB[i], ('BANK', 'PB%d' % i)
    PG = Y[1][:, 400:416]
    PGK = YK[1]

    def run(gen):
        for _ in gen:
            pass

    def chain(*gens):
        for g in gens:
            yield from g

    def merge(ga, na, gb, nb, frac=1.0):
        a = b = 0
        da = db = False
        while not (da and db):
            if not da and (db or a * nb <= b * na * frac):
                try:
                    next(ga)
                    a += 1
                except StopIteration:
                    da = True
            elif not db:
                try:
                    next(gb)
                    b += 1
                except StopIteration:
                    db = True

    ln_tmp = {}

    def layer_norm(buf, bkeyf, W, gcol, bcol, bfout, bfkey, banks=None):
        if W not in ln_tmp:
            ln_tmp[W] = dict(
                ybf=[sb("ln_ybf%d_%d" % (W, i), [128, W], BF16) for i in range(2)],
                ysq=[sb("ln_ysq%d_%d" % (W, i), [128, W], BF16) for i in range(2)],
                rstd=sb("ln_rstd%d" % W, [128, W]),
                nmr=sb("ln_nmr%d" % W, [128, W]),
                t=[sb("ln_t%d_%d" % (W, i), [128, W]) for i in range(2)])
        L = ln_tmp[W]
        L['msq'] = L['t'][0]
        if banks is None:
            pm, pmk = next_pb()
            pq, pqk = next_pb()
        else:
            (pm, pmk), (pq, pqk) = banks
        for n in range(KT):
            yb, ys = L['ybf'][n % 2], L['ysq'][n % 2]
            S.op('dve', lambda e, n=n, yb=yb: e.tensor_copy(out=yb[:], in_=buf[:, n, :]),
                 reads=[bkeyf(n)], writes=[('ln_ybf', W, n % 2)])
            S.op('act', lambda e, n=n, ys=ys: e.activation(out=ys[:], in_=buf[:, n, :], func=AF.Square),
                 reads=[bkeyf(n)], writes=[('ln_ysq', W, n % 2)])
            S.op('pe', lambda e, n=n, yb=yb: e.matmul(pm[:, 0:W], lhsT=onesM[:], rhs=yb[:], start=(n == 0), stop=(n == KT - 1)),
                 reads=[('ln_ybf', W, n % 2), 'onesM'], writes=[pmk], inc=True)
            S.op('pe', lambda e, n=n, ys=ys: e.matmul(pq[:, 0:W], lhsT=onesM[:], rhs=ys[:], start=(n == 0), stop=(n == KT - 1)),
                 reads=[('ln_ysq', W, n % 2), 'onesM'], writes=[pqk], inc=True)
            yield
        S.op('act', lambda e: e.activation(out=L['msq'][:], in_=pm[:, 0:W], func=AF.Square), reads=[pmk], writes=[('ln_t', W, 0)])
        S.op('dve', lambda e: e.scalar_tensor_tensor(out=L['rstd'][:], in0=pq[:, 0:W], scalar=LN_EPS, in1=L['msq'][:],
                                                     op0=ALU.add, op1=ALU.subtract),
             reads=[pqk, ('ln_t', W, 0)], writes=[('ln_rstd', W)])
        S.op('act', lambda e: e.activation(out=L['rstd'][:], in_=L['rstd'][:], func=AF.Ln), reads=[('ln_rstd', W)], writes=[('ln_rstd', W)])
        S.op('act', lambda e: e.activation(out=L['rstd'][:], in_=L['rstd'][:], func=AF.Exp, scale=-0.5),
             reads=[('ln_rstd', W)], writes=[('ln_rstd', W)])
        S.op('dve', lambda e: e.scalar_tensor_tensor(out=L['nmr'][:], in0=pm[:, 0:W], scalar=-1.0, in1=L['rstd'][:],
                                                     op0=ALU.mult, op1=ALU.mult),
             reads=[pmk, ('ln_rstd', W)], writes=[('ln_nmr', W)])
        yield
        for n in range(KT):
            t = L['t'][n % 2]
            S.op('dve', lambda e, n=n, t=t: e.tensor_tensor(out=t[:], in0=buf[:, n, :], in1=L['rstd'][:], op=ALU.mult),
                 reads=[bkeyf(n), ('ln_rstd', W)], writes=[('ln_t', W, n % 2)])
            S.op('dve', lambda e, n=n, t=t: e.tensor_tensor(out=t[:], in0=t[:], in1=L['nmr'][:], op=ALU.add),
                 reads=[('ln_t', W, n % 2), ('ln_nmr', W)], writes=[('ln_t', W, n % 2)])
            S.op('act', lambda e, n=n, t=t: e.activation(out=buf[:, n, :], in_=t[:], func=AF.Identity,
                                                         bias=lnp[:, bcol + n:bcol + n + 1], scale=lnp[:, gcol + n:gcol + n + 1]),
                 reads=[('ln_t', W, n % 2), 'lnp'], writes=[bkeyf(n)])
            if bfout is not None:
                S.op('dve', lambda e, n=n: e.tensor_copy(out=bfout[:, n, :], in_=buf[:, n, :]),
                     reads=[bkeyf(n)], writes=[bfkey(n)])
            yield

    def phase_a():
        w_in = sb("w_in_bf", [128, KT, PW], BF16)
        w_out = sb("w_out_bf", [128, KT, D], BF16)
        b_fm = sb("b_fm_sb", [128, 20])
        b_tm = sb("b_tm_sb", [128, 1032])
        lbl = sb("lbl_sb", [128, 8])
        lb = sb("lb_sb", [128, 4])
        oml = sb("oml_sb", [128, 4])
        convw = sb("convw_sb", [128, 16])
        convb = sb("convb_sb", [128, 4])
        gain = sb("gain_sb", [128, 8])
        dg = sb("dg", [128, 16, 128], BF16)
        resetm = sb("resetm", [128, 512])
        xres = sb("xres", [128, KT, BLK])
        xb = [sb("xb%d" % i, [128, KT, BLK], BF16) for i in range(2)]
        NS = 2
        qt = [sb("qt%d" % i, [128, 4, BLK], BF16) for i in range(NS)]
        kTt = [sb("kTt%d" % i, [128, 4, BLK], BF16) for i in range(NS)]
        gH = [sb("gH%d" % i, [128, 4, BLK], BF16) for i in range(1)]
        gM = [sb("gM%d" % i, [128, 4, BLK], BF16) for i in range(1)]
        qk = [sb("qk%d" % i, [128, 4, BLK], BF16) for i in range(NS)]
        dec = [sb("dec%d" % i, [128, 4, 12]) for i in range(NS)]
        vh = [sb("vh%d" % i, [128, 512], BF16) for i in range(2)]
        vm = [sb("vm%d" % i, [128, 4, 129], BF16) for i in range(2)]
        convbuf = [sb("convbuf%d" % j, [128, 3 + BLK], BF16) for j in range(4)]
        tU = sb("tU", [128, BLK]); tX = sb("tX", [128, BLK]); tL1 = sb("tL1", [128, BLK]); tL2 = sb("tL2", [128, BLK])
        tB = sb("tB", [128, BLK])
        nbm = sb("nbm", [128, 4]); dl = sb("dl", [128, 4]); kb = sb("kb", [128, 4])
        nb_fm = sb("nb_fm", [128, 20]); lomb = sb("lomb", [128, 4])
        gsb = [sb("gsb%d" % i, [128, 8]) for i in range(2)]
        gef = [sb("gef%d" % i, [128, 4]) for i in range(2)]
        gsp = [sb("gsp%d" % i, [128, 4]) for i in range(2)]
        gtm = [sb("gtm%d" % i, [128, 4]) for i in range(2)]
        wS = [sb("wS%d" % i, [128, 4]) for i in range(2)]
        rT = [sb("rT%d" % i, [128, 4]) for i in range(2)]
        rI = [sb("rI%d" % i, [128, 4]) for i in range(2)]
        egl = [sb("egl%d" % i, [128, 4]) for i in range(2)]
        Sst = [sb("Sst%d" % h, [128, 128]) for h in range(4)]
        Sbf = [sb("Sbf%d" % h, [128, 128], BF16) for h in range(4)]
        Stmp = [sb("Stmp%d" % h, [128, 128]) for h in range(4)]
        Cst = [sb("Cst%d" % h, [128, 129]) for h in range(4)]
        Cbf = [sb("Cbf%d" % h, [128, 129], BF16) for h in range(4)]
        Ctmp = [sb("Ctmp%d" % h, [128, 129]) for h in range(4)]
        kTM2 = [sb("kTM2_%d" % i, [128, 2, 128], BF16) for i in range(2)]
        AT2 = [sb("AT2_%d" % i, [128, 2, 128], BF16) for i in range(2)]
        ATr2 = [sb("ATr2_%d" % i, [128, 2, 128], BF16) for i in range(2)]
        on2 = [sb("on2_%d" % i, [128, 2, 128], BF16) for i in range(2)]
        junk = sb("junk", [128, 128])
        ssq2 = [sb("ssq2_%d" % i, [128, 2]) for i in range(2)]
        rsd2 = [sb("rsd2_%d" % i, [128, 2]) for i in range(2)]
        mt2 = [sb("mt2_%d" % i, [128, 4, 2]) for i in range(2)]
        ocat = sb("ocat", [128, 8, BLK], BF16)
        mrr = {'k': 0}

        WG = [('hf', C_HF, 512), ('mqk', C_MQ, 512), ('hv', C_HV, 512), ('mv', C_MV, 512), ('gt', C_IG, 8),
              ('hq', C_HQ, 512), ('hg', C_HG, 512), ('mo', C_MO, 512)]

        def wkey(c0):
            for nm, g0, gn in WG:
                if g0 <= c0 < g0 + gn:
                    return ('w_in', nm)
            raise KeyError(c0)
        def load_w_in(part):
            for nm, g0, gn in (WG[:2], WG[2:5], WG[5:])[part]:
                S.dma('pool', 'w_in_' + nm, lambda e, g0=g0, gn=gn: e.dma_start(out=w_in[:, :, g0:g0 + gn], in_=w_in_d[:, :, g0:g0 + gn]),
                      writes=[('w_in', nm)])
        for i, (t, d, k) in enumerate(((b_fm, b_fm_d, 'b_fm'), (b_tm, b_tm_d, 'b_tm'), (lbl, lbl_d, 'lbl'),
                                       (convw, convw_d, 'convw'), (convb, convb_d, 'convb'), (gain, gain_d, 'gain'))):
            S.dma('sp', 'cA%d' % i, lambda e, t=t, d=d: e.dma_start(out=t[:], in_=d), writes=[k])
        S.op('dve', lambda e: e.tensor_tensor(out=lb[:], in0=lbl[:, 0:4], in1=lbl[:, 4:8], op=ALU.subtract), reads=['lbl'], writes=['lb'])
        S.op('act', lambda e: e.activation(out=lb[:], in_=lb[:], func=AF.Sigmoid), reads=['lb'], writes=['lb'])
        S.op('dve', lambda e: e.tensor_scalar(out=oml[:], in0=lb[:], scalar1=-1.0, scalar2=1.0, op0=ALU.mult, op1=ALU.add),
             reads=['lb'], writes=['oml'])
        S.op('act', lambda e: e.activation(out=lomb[:], in_=oml[:], func=AF.Ln), reads=['oml'], writes=['lomb'])
        S.op('dve', lambda e: e.tensor_scalar(out=nb_fm[:], in0=b_fm[:], scalar1=-1.0, scalar2=None, op0=ALU.mult),
             reads=['b_fm'], writes=['nb_fm'])
        for j in range(4):
            for k in range(4):
                S.op('dve', lambda e, j=j, k=k: e.tensor_scalar(out=dg[:, j * 4 + k, :], in0=identf[:],
                                                                scalar1=convw[:, j * 4 + k:j * 4 + k + 1], scalar2=None, op0=ALU.mult),
                     reads=['identf', 'convw'], writes=[('dg', j)])
        S.op('pool', lambda e: e.memset(resetm[:], 1.0), writes=['resetm'])
        S.op('pool', lambda e: e.memset(resetm[:].rearrange("p (c t) -> p c t", t=128)[:, :, 0:1], 0.0),
             reads=['resetm'], writes=['resetm'])
        for g_ in range(2):
            S.op('pool', lambda e, g_=g_: e.memset(AT2[g_][:], 0.0), writes=[('AT2', g_)])
        for j in range(4):
            S.op('pool', lambda e, j=j: e.memset(convbuf[j][:, 0:3], 0.0), writes=[('cb_carry', j)])
        for h in range(4):
            S.op('pool', lambda e, h=h: e.memset(Sst[h][:], 0.0), writes=[('S', h)])
            S.op('pool', lambda e, h=h: e.memset(Cst[h][:], 0.0), writes=[('C', h)])
            S.op('pool', lambda e, h=h: e.memset(Cbf[h][:], 0.0), writes=[('Cbf', h)])
        for s in range(2):
            S.op('pool', lambda e, s=s: e.memset(vm[s][:], 1.0), writes=[('vm', s)])

        def load_w_out():
            wst = xres[:].rearrange("p (a two) b -> p a (two b)", two=2)
            for half in range(2):
                S.dma('sp', 'wo_st', lambda e, half=half: e.dma_start(out=wst, in_=w_out_d[:, half * 4:half * 4 + 4, :]),
                      writes=[('xres', n) for n in range(KT)])
                for q in range(4):
                    kt = half * 4 + q
                    S.op('dve', lambda e, q=q, kt=kt: e.tensor_scalar(out=w_out[:, kt, :], in0=wst[:, q, :],
                                                                      scalar1=gain[:, kt:kt + 1], scalar2=None, op0=ALU.mult),
                         reads=[('xres', n) for n in range(KT)] + ['gain'], writes=[('w_out', kt)])

        def load_x(src, blk, slot):
            for half in range(2):
                S.dma('pool', 'xb%d_%d' % (slot, half),
                      lambda e, half=half: e.dma_start(out=xb[slot][:, half * 4:half * 4 + 4, :],
                                                       in_=src[:, half * 4:half * 4 + 4, blk * BLK:(blk + 1) * BLK]),
                      writes=[('xb', slot, half)])

        def fm_proj(slot, j):
            ps, pk = next_pb()
            c0 = FM_COLS[j]
            for kt in range(KT):
                S.op('pe', lambda e, kt=kt: e.matmul(ps[:], lhsT=w_in[:, kt, c0:c0 + 128], rhs=xb[slot][:, kt, :],
                                                     start=(kt == 0), stop=(kt == KT - 1)),
                     reads=[wkey(c0), ('xb', slot, kt // 4)], writes=[pk], inc=(kt == KT - 1))
            return ps, pk

        def bias(j):
            return b_fm[:, j:j + 1]

        def fm_a(slot, fs, own, need_q=True):
            for h in range(4):
                ps, pk = fm_proj(slot, J_HF + h)
                S.op('act', lambda e, ps=ps, h=h: e.activation(out=tU[:], in_=ps[:], func=AF.Exp, bias=nb_fm[:, J_HF + h:J_HF + h + 1], scale=-1.0),
                     reads=[pk, 'nb_fm'], writes=['tU'])
                S.op('dve', lambda e, ps=ps, h=h: e.tensor_scalar(out=tX[:], in0=ps[:], scalar1=bias(J_HF + h), scalar2=None, op0=ALU.add),
                     reads=[pk, 'b_fm'], writes=['tX'])
                yield
                S.op('act', lambda e, h=h: e.activation(out=tL1[:], in_=tU[:], func=AF.Ln, scale=lb[:, h:h + 1], bias=1.0),
                     reads=['tU', 'lb'], writes=['tL1'])
                S.op('act', lambda e: e.activation(out=tL2[:], in_=tU[:], func=AF.Ln, bias=1.0, scale=1.0), reads=['tU'], writes=['tL2'])
                yield
                S.op('pool', lambda e: e.tensor_tensor(out=tL1[:], in0=tL1[:], in1=tL2[:], op=ALU.subtract), reads=['tL1', 'tL2'], writes=['tL1'])
                S.op('pool', lambda e: e.tensor_tensor(out=tX[:], in0=tX[:], in1=tL2[:], op=ALU.add), reads=['tX', 'tL2'], writes=['tX'])
                yield
                S.op('dve', lambda e: e.tensor_tensor_scan(out=tB[:], data0=resetm[:], data1=tL1[:], initial=0.0, op0=ALU.mult, op1=ALU.add),
                     reads=['tL1', 'resetm'], writes=['tB'])
                yield
                tBv = tB[:].rearrange("p (c t) -> p c t", t=128)
                S.op('dve', lambda e, tBv=tBv: e.tensor_scalar(out=nbm[:], in0=tBv[:, :, MID], scalar1=-1.0, scalar2=None, op0=ALU.mult),
                     reads=['tB'], writes=['nbm'])
                S.op('dve', lambda e, tBv=tBv: e.tensor_tensor(out=dl[:], in0=tBv[:, :, 127], in1=tBv[:, :, MID], op=ALU.subtract),
                     reads=['tB'], writes=['dl'])
                S.op('dve', lambda e, tBv=tBv, h=h: e.tensor_scalar(out=kb[:], in0=tBv[:, :, MID], scalar1=lomb[:, h:h + 1], scalar2=None, op0=ALU.add),
                     reads=['tB', 'lomb'], writes=['kb'])
                S.op('pool', lambda e: e.tensor_tensor(out=tX[:], in0=tX[:], in1=tB[:], op=ALU.add), reads=['tX', 'tB'], writes=['tX'])
                yield
                for c in range(4):
                    cs = slice(c * 128, (c + 1) * 128)
                    S.op('act', lambda e, c=c, cs=cs, h=h: e.activation(out=kTt[fs][:, h, cs], in_=tX[:, cs], func=AF.Exp,
                                                                         bias=kb[:, c:c + 1], scale=-1.0),
                         reads=['tX', 'kb'], writes=[('kTt', fs, h)])
                S.op('act', lambda e, h=h: e.activation(out=dec[fs][:, h, 0:4], in_=dl[:], func=AF.Exp), reads=['dl'], writes=[('dec', fs, h, 0)])
                S.op('act', lambda e, h=h, tBv=tBv: e.activation(out=dec[fs][:, h, 4:8], in_=tBv[:, :, 127], func=AF.Exp),
                     reads=['tB'], writes=[('dec', fs, h, 1)])
                S.op('act', lambda e, h=h, tBv=tBv: e.activation(out=dec[fs][:, h, 8:12], in_=tBv[:, :, MID], func=AF.Exp),
                     reads=['tB'], writes=[('dec', fs, h, 2)])
                yield
                if own:
                    ps, pk = fm_proj(slot, J_HQ + h)
                    S.op('act', lambda e, ps=ps, h=h: e.activation(out=tU[:], in_=ps[:], func=AF.Exp, bias=nb_fm[:, J_HQ + h:J_HQ + h + 1], scale=-1.0),
                         reads=[pk, 'nb_fm'], writes=['tU'])
                    S.op('dve', lambda e, ps=ps, h=h: e.tensor_scalar(out=tL1[:], in0=ps[:], scalar1=bias(J_HQ + h), scalar2=None, op0=ALU.add),
                         reads=[pk, 'b_fm'], writes=['tL1'])
                    yield
                    S.op('act', lambda e: e.activation(out=tL2[:], in_=tU[:], func=AF.Ln, bias=1.0, scale=1.0), reads=['tU'], writes=['tL2'])
                    yield
                    S.op('pool', lambda e: e.tensor_tensor(out=tL2[:], in0=tB[:], in1=tL2[:], op=ALU.subtract), reads=['tB', 'tL2'], writes=['tL2'])
                    yield
                    for c in range(4):
                        cs = slice(c * 128, (c + 1) * 128)
                        S.op('act', lambda e, c=c, cs=cs: e.activation(out=tL2[:, cs], in_=tL2[:, cs], func=AF.Exp, bias=nbm[:, c:c + 1], scale=1.0),
                             reads=['tL2', 'nbm'], writes=['tL2'])
                    yield
                    S.op('pool', lambda e, h=h: e.tensor_tensor(out=qt[fs][:, h, :], in0=tL1[:], in1=tL2[:], op=ALU.mult),
                         reads=['tL1', 'tL2'], writes=[('qt', fs, h)])
                    yield
            for j in range(4):
                if j < 2 and not need_q:
                    continue
                ps, pk = fm_proj(slot, J_QK + j)
                S.op('act', lambda e, ps=ps, j=j: e.activation(out=convbuf[j][:, 3:3 + BLK], in_=ps[:], func=AF.Identity,
                                                               bias=bias(J_QK + j), scale=1.0),
                     reads=[pk, 'b_fm'], writes=[('cb_body', j)])
                ps2, pk2 = next_pb()
                for k in range(4):
                    S.op('pe', lambda e, ps2=ps2, j=j, k=k: e.matmul(ps2[:], lhsT=dg[:, j * 4 + k, :], rhs=convbuf[j][:, k:k + BLK],
                                                                    start=(k == 0), stop=(k == 3)),
                         reads=[('dg', j), ('cb_body', j), ('cb_carry', j)], writes=[pk2], inc=(k == 3))
                S.op('act', lambda e, ps2=ps2, j=j: e.activation(out=qk[fs][:, j, :], in_=ps2[:], func=AF.Silu,
                                                                 bias=convb[:, j:j + 1], scale=1.0),
                     reads=[pk2, 'convb'], writes=[('qk', fs, j)])
                S.op('pool', lambda e, j=j: e.tensor_copy(out=convbuf[j][:, 0:3], in_=convbuf[j][:, BLK:BLK + 3]),
                     reads=[('cb_body', j)], writes=[('cb_carry', j)])
                yield

        def fm_b(slot):
            for h in range(4):
                ps, pk = fm_proj(slot, J_MO + h)
                S.op('act', lambda e, ps=ps, h=h: e.activation(out=gM[0][:, h, :], in_=ps[:], func=AF.Sigmoid, bias=bias(J_MO + h), scale=1.0),
                     reads=[pk, 'b_fm'], writes=[('gM', 0, h)])
                yield
            for h in range(4):
                ps, pk = fm_proj(slot, J_HG + h)
                S.op('act', lambda e, ps=ps, h=h: e.activation(out=gH[0][:, h, :], in_=ps[:], func=AF.Silu, bias=bias(J_HG + h), scale=1.0),
                     reads=[pk, 'b_fm'], writes=[('gH', 0, h)])
                yield

        def tm_proj(slot, c, gs):
            cs = slice(c * 128, (c + 1) * 128)
            ps, pk = next_pb()
            for kt in range(KT):
                S.op('pe', lambda e, ps=ps, kt=kt: e.matmul(ps[:], lhsT=xb[slot][:, kt, cs], rhs=w_in[:, kt, C_HV:C_HV + 512],
                                                            start=(kt == 0), stop=(kt == KT - 1)),
                     reads=[('w_in', 'hv'), ('xb', slot, kt // 4)], writes=[pk], inc=(kt == KT - 1))
            S.op('dve', lambda e, ps=ps: e.tensor_tensor(out=vh[gs][:], in0=ps[:], in1=b_tm[:, 0:512], op=ALU.add),
                 reads=[pk, 'b_tm'], writes=[('vh', gs)])
            ps, pk = next_pb()
            for kt in range(KT):
                S.op('pe', lambda e, ps=ps, kt=kt: e.matmul(ps[:], lhsT=xb[slot][:, kt, cs], rhs=w_in[:, kt, C_MV:C_MV + 512],
                                                            start=(kt == 0), stop=(kt == KT - 1)),
                     reads=[('w_in', 'mv'), ('xb', slot, kt // 4)], writes=[pk], inc=(kt == KT - 1))
            S.op('dve', lambda e, ps=ps: e.tensor_tensor(out=vm[gs][:, :, 0:128],
                                                         in0=ps[:].rearrange("p (h v) -> p h v", v=128),
                                                         in1=b_tm[:, 512:1024].rearrange("p (h v) -> p h v", v=128), op=ALU.add),
                 reads=[pk, 'b_tm'], writes=[('vm', gs)])

        def gates(slot, c, gs):
            cs = slice(c * 128, (c + 1) * 128)
            for kt in range(KT):
                S.op('pe', lambda e, kt=kt: e.matmul(PG[:, 0:8], lhsT=xb[slot][:, kt, cs], rhs=w_in[:, kt, C_IG:C_IG + 8],
                                                     start=(kt == 0), stop=(kt == KT - 1)),
                     reads=[('w_in', 'gt'), ('xb', slot, kt // 4)], writes=[PGK], inc=(kt == KT - 1))
            S.op('dve', lambda e: e.tensor_tensor(out=gsb[gs][:], in0=PG[:, 0:8], in1=b_tm[:, 1024:1032], op=ALU.add),
                 reads=[PGK, 'b_tm'], writes=[('gsb', gs)])
            S.op('act', lambda e: e.activation(out=gef[gs][:], in_=gsb[gs][:, 4:8], func=AF.Exp, scale=-1.0),
                 reads=[('gsb', gs)], writes=[('gef', gs)])
            S.op('act', lambda e: e.activation(out=gsp[gs][:], in_=gef[gs][:], func=AF.Ln, bias=1.0, scale=1.0),
                 reads=[('gef', gs)], writes=[('gsp', gs)])
            S.op('pe', lambda e: e.matmul(PG[:, 8:12], lhsT=mask01[:], rhs=gsp[gs][:], start=True, stop=True),
                 reads=['mask01', ('gsp', gs)], writes=[PGK])
            S.op('pe', lambda e: e.matmul(PG[:, 12:16], lhsT=onesF[:], rhs=gsp[gs][:], start=True, stop=True),
                 reads=['onesF', ('gsp', gs)], writes=[PGK])
            S.op('dve', lambda e: e.tensor_tensor(out=gtm[gs][:], in0=PG[:, 8:12], in1=gsb[gs][:, 0:4], op=ALU.add),
                 reads=[PGK, ('gsb', gs)], writes=[('gtm', gs)])
            S.op('act', lambda e: e.activation(out=wS[gs][:], in_=gtm[gs][:], func=AF.Exp), reads=[('gtm', gs)], writes=[('wS', gs)])
            S.op('act', lambda e: e.activation(out=rI[gs][:], in_=PG[:, 8:12], func=AF.Exp, scale=1.0, bias=math.log(8.0)),
                 reads=[PGK], writes=[('rT', gs)])
            S.op('act', lambda e: e.activation(out=egl[gs][:], in_=PG[:, 12:16], func=AF.Exp, scale=-1.0),
                 reads=[PGK], writes=[('egl', gs)])

        def grp_stages(kind, g, h0, fs, c, gs, own):
            cs = slice(c * 128, (c + 1) * 128)
            Xg, Yg, XKg, YKg = X[g], Y[g], XK[g], YK[g]
            hs = (h0, h0 + 1)
            jj = h0 // 2
            if kind == 'H':
                for k, h in enumerate(hs):
                    S.op('pe', lambda e, k=k, h=h: e.transpose(out=XT[g][:, k * 128:(k + 1) * 128], in_=kTt[fs][:, h, cs], identity=ident[:]),
                         reads=[('kTt', fs, h), 'ident'], writes=[XKg])
                if own:
                    for k, h in enumerate(hs):
                        S.op('pe', lambda e, k=k, h=h: e.matmul(Xg[:, k * 128:(k + 1) * 128], lhsT=kTt[fs][:, h, cs], rhs=qt[fs][:, h, cs],
                                                                start=True, stop=True),
                             reads=[('kTt', fs, h), ('qt', fs, h)], writes=[XKg])
            else:
                for k, h in enumerate(hs):
                    b0 = k * 64
                    S.op('pe', lambda e, k=k, b0=b0: e.transpose(out=XT[g][:, k * 128:k * 128 + 64], in_=qk[fs][b0:b0 + 64, 2 + jj, cs],
                                                                 identity=ident[b0:b0 + 64, b0:b0 + 64]),
                         reads=[('qk', fs, 2 + jj), 'ident'], writes=[XKg], serial=(k == 1))
                    if own:
                        S.op('pe', lambda e, k=k, b0=b0: e.matmul(Xg[:, k * 128:(k + 1) * 128], lhsT=qk[fs][b0:b0 + 64, 2 + jj, cs],
                                                                  rhs=qk[fs][b0:b0 + 64, jj, cs], start=True, stop=True),
                             reads=[('qk', fs, jj), ('qk', fs, 2 + jj)], writes=[XKg])
            yield
            if kind == 'H':
                S.op('act', lambda e: e.copy(out=kTM2[g][:].rearrange("p a b -> p (a b)"), in_=XT[g][:, 0:256]),
                     reads=[XKg], writes=[('kTM2', g)])
                if own:
                    S.op('dve', lambda e: e.copy_predicated(out=AT2[g][:].rearrange("p a b -> p (a b)"),
                                                            mask=mask2[:].rearrange("p a b -> p (a b)").bitcast(mybir.dt.int32),
                                                            data=Xg[:, 0:256]),
                         reads=[XKg, 'mask2', ('AT2', g)], writes=[('AT2', g)])
                    if c == 0:
                        for h in hs:
                            S.op('dve', lambda e, h=h: e.tensor_scalar(out=Sbf[h][:], in0=Sst[h][:], scalar1=dec[fs][:, h, 8 + c:9 + c],
                                                                       scalar2=None, op0=ALU.mult),
                                 reads=[('S', h), ('dec', fs, h, 2)], writes=[('Sbf', h)])
            else:
                for k, h in enumerate(hs):
                    S.op('act', lambda e, k=k, h=h: e.activation(out=kTM2[g][:, k, 0:64], in_=XT[g][:, k * 128:k * 128 + 64], func=AF.Identity,
                                                                 scale=wS[gs][:, h:h + 1]),
                         reads=[XKg, ('wS', gs)], writes=[('kTM2', g)])
                if own:
                    for k, h in enumerate(hs):
                        S.op('dve', lambda e, k=k, h=h: e.scalar_tensor_tensor(out=AT2[g][:, k, :], in0=Xg[:, k * 128:(k + 1) * 128],
                                                                               scalar=wS[gs][:, h:h + 1], in1=mask01[:], op0=ALU.mult, op1=ALU.mult),
                             reads=[XKg, 'mask01', ('wS', gs)], writes=[('AT2', g)])
            yield
            for k, h in enumerate(hs):
                if kind == 'H':
                    v_ap = vh[gs][:, h * 128:(h + 1) * 128]
                    S.op('pe', lambda e, k=k, v_ap=v_ap: e.matmul(Yg[:, k * 128:(k + 1) * 128], lhsT=kTM2[g][:, k, :], rhs=v_ap, start=True, stop=True),
                         reads=[('kTM2', g), ('vh', gs)], writes=[YKg])
                else:
                    b0 = k * 64
                    S.op('pe', lambda e, k=k, h=h, b0=b0: e.matmul(Yg[b0:b0 + 64, 0:129], lhsT=kTM2[g][:, k, 0:64], rhs=vm[gs][:, h, :], start=True, stop=True),
                         reads=[('kTM2', g), ('vm', gs)], writes=[YKg])
            if own:
                for k, h in enumerate(hs):
                    if kind == 'H':
                        v_ap = vh[gs][:, h * 128:(h + 1) * 128]
                        po = Yg[:, 256 + k * 128:256 + (k + 1) * 128]
                        S.op('pe', lambda e, k=k, po=po, v_ap=v_ap: e.matmul(po, lhsT=AT2[g][:, k, :], rhs=v_ap, start=True, stop=False),
                             reads=[('AT2', g), ('vh', gs)], writes=[YKg], inc=False)
                        S.op('pe', lambda e, po=po, h=h: e.matmul(po, lhsT=qt[fs][:, h, cs], rhs=Sbf[h][:], start=False, stop=True),
                             reads=[('qt', fs, h), ('Sbf', h)], writes=[YKg])
                    else:
                        b0 = k * 64
                        po = Yg[:, 129 + k * 129:129 + (k + 1) * 129]
                        S.op('pe', lambda e, k=k, h=h, po=po: e.matmul(po, lhsT=AT2[g][:, k, :], rhs=vm[gs][:, h, :], start=True, stop=False),
                             reads=[('AT2', g), ('vm', gs)], writes=[YKg], inc=False)
                        S.op('pe', lambda e, h=h, po=po, b0=b0: e.matmul(po, lhsT=qk[fs][b0:b0 + 64, jj, cs], rhs=Cbf[h][b0:b0 + 64, :], start=False, stop=True),
                             reads=[('qk', fs, jj), ('Cbf', h)], writes=[YKg])
            yield
            for k, h in enumerate(hs):
                if kind == 'H':
                    S.op('dve', lambda e, h=h: e.tensor_scalar(out=Stmp[h][:], in0=Sst[h][:], scalar1=dec[fs][:, h, 4 + c:5 + c],
                                                               scalar2=None, op0=ALU.mult),
                         reads=[('S', h), ('dec', fs, h, 1)], writes=[('Stmp', h)])
                    S.op('dve', lambda e, k=k, h=h: e.scalar_tensor_tensor(out=Sst[h][:], in0=Yg[:, k * 128:(k + 1) * 128], scalar=dec[fs][:, h, c:c + 1],
                                                                           in1=Stmp[h][:], op0=ALU.mult, op1=ALU.add),
                         reads=[YKg, ('dec', fs, h, 0), ('Stmp', h)], writes=[('S', h)])
                    if own and c < 3:
                        S.op('dve', lambda e, h=h: e.tensor_scalar(out=Sbf[h][:], in0=Sst[h][:], scalar1=dec[fs][:, h, 9 + c:10 + c],
                                                                   scalar2=None, op0=ALU.mult),
                             reads=[('S', h), ('dec', fs, h, 2)], writes=[('Sbf', h)])
                else:
                    b0 = k * 64
                    S.op('dve', lambda e, h=h, b0=b0: e.tensor_scalar(out=Ctmp[h][b0:b0 + 64, :], in0=Cst[h][b0:b0 + 64, :],
                                                                      scalar1=egl[gs][b0:b0 + 64, h:h + 1], scalar2=None, op0=ALU.mult),
                         reads=[('C', h), ('egl', gs)], writes=[('Ctmp', h)])
                    S.op('dve', lambda e, h=h, b0=b0: e.scalar_tensor_tensor(out=Cst[h][b0:b0 + 64, :], in0=Yg[b0:b0 + 64, 0:129],
                                                                             scalar=egl[gs][b0:b0 + 64, h:h + 1], in1=Ctmp[h][b0:b0 + 64, :],
                                                                             op0=ALU.mult, op1=ALU.add),
                         reads=[YKg, ('egl', gs), ('Ctmp', h)], writes=[('C', h)])
                    S.op('pool', lambda e, h=h, b0=b0: e.tensor_copy(out=Cbf[h][b0:b0 + 64, :], in_=Cst[h][b0:b0 + 64, :]),
                         reads=[('C', h)], writes=[('Cbf', h)])
            if not own:
                yield
                return
            if kind == 'H':
                pos = [Yg[:, 256 + k * 128:256 + (k + 1) * 128] for k in range(2)]
            else:
                pos = [Yg[:, 129 + k * 129:129 + k * 129 + 128] for k in range(2)]
                m = mt2[g]
                den = Yg[:, 129:387].rearrange("p (k c) -> p k c", c=129)[:, :, 128]
                S.op('dve', lambda e: e.tensor_tensor(out=m[:, 0, :], in0=den, in1=rI[gs][:, h0:h0 + 2], op=ALU.max),
                     reads=[YKg, ('rT', gs)], writes=[('mt2', g)])
                S.op('dve', lambda e: e.scalar_tensor_tensor(out=m[:, 1, :], in0=den, scalar=-1.0, in1=m[:, 0, :], op0=ALU.mult, op1=ALU.max),
                     reads=[YKg, ('mt2', g)], writes=[('mt2', g)])
                S.op('dve', lambda e: e.reciprocal(out=m[:, 3, :], in_=m[:, 1, :]), reads=[('mt2', g)], writes=[('mt2', g)])
            for k in range(2):
                if kind == 'H':
                    S.op('act', lambda e, k=k: e.activation(out=junk[:], in_=pos[k], func=AF.Square, accum_out=ssq2[g][:, k:k + 1]),
                         reads=[YKg], writes=['junk', ('ssq2', g)])
                else:
                    S.op('act', lambda e, k=k: e.activation(out=junk[:], in_=pos[k], func=AF.Square, accum_out=ssq2[g][:, k:k + 1],
                                                            scale=mt2[g][:, 3, k:k + 1]),
                         reads=[YKg, ('mt2', g)], writes=['junk', ('ssq2', g)])
            S.op('act', lambda e: e.activation(out=rsd2[g][:], in_=ssq2[g][:], func=AF.Ln, scale=1.0 / 128.0, bias=RMS_EPS),
                 reads=[('ssq2', g)], writes=[('rsd2', g)])
            S.op('act', lambda e: e.activation(out=rsd2[g][:], in_=rsd2[g][:], func=AF.Exp, scale=-0.5),
                 reads=[('rsd2', g)], writes=[('rsd2', g)])
            if kind == 'M':
                S.op('dve', lambda e: e.tensor_tensor(out=rsd2[g][:], in0=rsd2[g][:], in1=mt2[g][:, 3, :], op=ALU.mult),
                     reads=[('rsd2', g), ('mt2', g)], writes=[('rsd2', g)])
            for k in range(2):
                S.op('act', lambda e, k=k: e.activation(out=on2[g][:, k, :], in_=pos[k], func=AF.Identity, scale=rsd2[g][:, k:k + 1]),
                     reads=[YKg, ('rsd2', g)], writes=[('on2', g)])
            yield
            for k in range(2):
                S.op('pe', lambda e, k=k: e.transpose(out=XT2[g][:, k * 128:(k + 1) * 128], in_=on2[g][:, k, :], identity=ident[:]),
                     reads=[('on2', g), 'ident'], writes=[XKg])
            yield
            j0 = h0 if kind == 'H' else 4 + h0
            gt = gH[0] if kind == 'H' else gM[0]
            gk = 'gH' if kind == 'H' else 'gM'
            S.op('dve', lambda e: e.tensor_tensor(out=ocat[:, j0:j0 + 2, cs], in0=XT2[g][:, 0:256].rearrange("p (a b) -> p a b", b=128),
                                                  in1=gt[:, h0:h0 + 2, cs], op=ALU.mult),
                 reads=[XKg, (gk, 0, h0), (gk, 0, h0 + 1)], writes=[('ocat', j0), ('ocat', j0 + 1)])
            yield

        def mixer_kind(kind, fs, c, gs, own, mid_hook=None):
            ga = grp_stages(kind, 0, 0, fs, c, gs, own)
            gb = grp_stages(kind, 1, 2, fs, c, gs, own)
            alive = [ga, gb]
            step = 0
            while alive:
                for g_ in list(alive):
                    try:
                        next(g_)
                    except StopIteration:
                        alive.remove(g_)
                step += 1
                if step == 4 and mid_hook is not None:
                    yield
                    mid_hook()
                yield

        def outproj_ln(blk, next_slot=None):
            W = BLK
            if W not in ln_tmp:
                ln_tmp[W] = dict(
                    ybf=[sb("ln_ybf%d_%d" % (W, i), [128, W], BF16) for i in range(2)],
                    ysq=[sb("ln_ysq%d_%d" % (W, i), [128, W], BF16) for i in range(2)],
                    rstd=sb("ln_rstd%d" % W, [128, W]),
                    nmr=sb("ln_nmr%d" % W, [128, W]),
                    t=[sb("ln_t%d_%d" % (W, i), [128, W]) for i in range(2)])
            L = ln_tmp[W]
            pm, pmk, pq, pqk = X[0], XK[0], Y[0], YK[0]

            def prep(n):
                S.op('dve', lambda e: e.tensor_copy(out=L['ybf'][n % 2][:], in_=xres[:, n, :]), reads=[('xres', n)], writes=[('ln_ybf', W, n % 2)])
                S.op('act', lambda e: e.activation(out=L['ysq'][n % 2][:], in_=xres[:, n, :], func=AF.Square),
                     reads=[('xres', n)], writes=[('ln_ysq', W, n % 2)])

            def stat(n):
                S.op('pe', lambda e: e.matmul(pm[:], lhsT=onesM[:], rhs=L['ybf'][n % 2][:], start=(n == 0), stop=(n == KT - 1)),
                     reads=[('ln_ybf', W, n % 2), 'onesM'], writes=[pmk])
                S.op('pe', lambda e: e.matmul(pq[:], lhsT=onesM[:], rhs=L['ysq'][n % 2][:], start=(n == 0), stop=(n == KT - 1)),
                     reads=[('ln_ysq', W, n % 2), 'onesM'], writes=[pqk])

            S.dma('sp', 'xres', lambda e: e.dma_start(out=xres[:], in_=xT[:, :, blk * BLK:(blk + 1) * BLK]),
                  writes=[('xres', n) for n in range(KT)])
            for n in range(KT + 2):
                if n < KT:
                    ps, pk = next_pb()
                    for kt in range(KT):
                        S.op('pe', lambda e, ps=ps, kt=kt, n=n: e.matmul(ps[:], lhsT=w_out[:, kt, n * 128:(n + 1) * 128], rhs=ocat[:, kt, :],
                                                                        start=(kt == 0), stop=(kt == KT - 1)),
                             reads=[('w_out', kt), ('ocat', kt)], writes=[pk], inc=(kt == KT - 1))
                    S.op('dve', lambda e, ps=ps, n=n: e.scalar_tensor_tensor(out=xres[:, n, :], in0=xres[:, n, :], scalar=ALPHA, in1=ps[:],
                                                                             op0=ALU.mult, op1=ALU.add),
                         reads=[pk, ('xres', n)], writes=[('xres', n)])
                if 0 <= n - 1 < KT:
                    prep(n - 1)
                if 0 <= n - 2 < KT:
                    stat(n - 2)
                yield
            msq = L['t'][0]
            S.op('act', lambda e: e.activation(out=msq[:], in_=pm[:], func=AF.Square), reads=[pmk], writes=[('ln_t', W, 0)])
            S.op('dve', lambda e: e.scalar_tensor_tensor(out=L['rstd'][:], in0=pq[:], scalar=LN_EPS, in1=msq[:], op0=ALU.add, op1=ALU.subtract),
                 reads=[pqk, ('ln_t', W, 0)], writes=[('ln_rstd', W)])
            S.op('act', lambda e: e.activation(out=L['rstd'][:], in_=L['rstd'][:], func=AF.Ln), reads=[('ln_rstd', W)], writes=[('ln_rstd', W)])
            S.op('act', lambda e: e.activation(out=L['rstd'][:], in_=L['rstd'][:], func=AF.Exp, scale=-0.5), reads=[('ln_rstd', W)], writes=[('ln_rstd', W)])
            S.op('dve', lambda e: e.scalar_tensor_tensor(out=L['nmr'][:], in0=pm[:], scalar=-1.0, in1=L['rstd'][:], op0=ALU.mult, op1=ALU.mult),
                 reads=[pmk, ('ln_rstd', W)], writes=[('ln_nmr', W)])
            yield
            fb = fm_b(next_slot) if next_slot is not None else iter(())
            for n in range(KT):
                try:
                    next(fb)
                except StopIteration:
                    pass
                t = L['t'][n % 2]
                S.op('dve', lambda e, n=n, t=t: e.tensor_tensor(out=t[:], in0=xres[:, n, :], in1=L['rstd'][:], op=ALU.mult),
                     reads=[('xres', n), ('ln_rstd', W)], writes=[('ln_t', W, n % 2)])
                S.op('dve', lambda e, n=n, t=t: e.tensor_tensor(out=t[:], in0=t[:], in1=L['nmr'][:], op=ALU.add),
                     reads=[('ln_t', W, n % 2), ('ln_nmr', W)], writes=[('ln_t', W, n % 2)])
                S.op('act', lambda e, n=n, t=t: e.activation(out=xres[:, n, :], in_=t[:], func=AF.Identity,
                                                             bias=lnp[:, 8 + n:9 + n], scale=lnp[:, n:n + 1]),
                     reads=[('ln_t', W, n % 2), 'lnp'], writes=[('xres', n)])
                yield
            S.dma('sp', 'x1st', lambda e: e.dma_start(out=x1s[:, :, blk * BLK:(blk + 1) * BLK], in_=xres[:]),
                  reads=[('xres', n) for n in range(KT)], writes=[('x1s', blk)])
            yield

        nblk_pre = T_PRE // BLK
        nblk_own = T_OWN // BLK
        seq = [(xTp, b, False) for b in range(nblk_pre)] + [(xT, b, True) for b in range(nblk_own)]
        gch = {'n': 0}

        def ch_all(slot, fs, own, prefetch=None):
            g0 = gch['n']
            gch['n'] += 4
            gates(slot, 0, g0 % 2)
            yield
            tm_proj(slot, 0, g0 % 2)
            yield
            for c in range(4):
                gs = (g0 + c) % 2
                if c == 3 and prefetch is not None:
                    load_x(prefetch[0], prefetch[1], slot)
                yield from mixer_kind('H', fs, c, gs, own)
                if c < 3:
                    gates(slot, c + 1, (gs + 1) % 2)
                    yield
                    if own:
                        yield from mixer_kind('M', fs, c, gs, own, mid_hook=lambda c=c, gs=gs: tm_proj(slot, c + 1, (gs + 1) % 2))
                    else:
                        yield from mixer_kind('M', fs, c, gs, own)
                        tm_proj(slot, c + 1, (gs + 1) % 2)
                        yield
                else:
                    yield from mixer_kind('M', fs, c, gs, own)

        def carry_flag():
            for j in range(4):
                S.op('dve', lambda e, j=j: e.tensor_scalar(out=convbuf[j][:, 0:3], in0=convbuf[j][:, 0:3], scalar1=flag[:, 0:1],
                                                           scalar2=None, op0=ALU.mult),
                     reads=[('cb_carry', j), 'flag'], writes=[('cb_carry', j)])

        def state_flag():
            for h in range(4):
                S.op('dve', lambda e, h=h: e.tensor_scalar(out=Sst[h][:], in0=Sst[h][:], scalar1=flag[:, 0:1], scalar2=None, op0=ALU.mult),
                     reads=[('S', h), 'flag'], writes=[('S', h)])
                S.op('dve', lambda e, h=h: e.tensor_scalar(out=Cst[h][:], in0=Cst[h][:], scalar1=flag[:, 0:1], scalar2=None, op0=ALU.mult),
                     reads=[('C', h), 'flag'], writes=[('C', h)])
                S.op('dve', lambda e, h=h: e.tensor_copy(out=Cbf[h][:], in_=Cst[h][:]), reads=[('C', h)], writes=[('Cbf', h)])

        load_x(seq[0][0], seq[0][1], 0)
        load_w_in(0)
        load_w_in(1)
        if len(seq) > 1:
            load_x(seq[1][0], seq[1][1], 1)
        if seq[0][2]:
            carry_flag()
        def needq(i):
            return seq[i][2] or (i + 1 < len(seq) and seq[i + 1][2])
        run(fm_a(0, 0, seq[0][2], needq(0)))
        load_w_in(2)
        load_w_out()
        for bi, (src, blk, own) in enumerate(seq):
            slot = bi % 2
            nxt = seq[bi + 1] if bi + 1 < len(seq) else None
            nn = seq[bi + 2] if bi + 2 < len(seq) else None
            if own and blk == 0:
                state_flag()
            main = [ch_all(slot, slot, own, prefetch=(nn[0], nn[1]) if nn is not None else None)]
            nmain = 4 * (2 + 14)
            if own:
                nxt_own = nxt is not None and nxt[2]
                if blk == 0:
                    main = [fm_b(slot)] + main
                    nmain += 8
                main = main + [outproj_ln(blk, (bi + 1) % 2 if nxt_own else None)]
                nmain += 21
            if nxt is not None:
                if nxt[2] and nxt[1] == 0:
                    carry_flag()
                merge(chain(*main), nmain, fm_a((bi + 1) % 2, (bi + 1) % 2, nxt[2], needq(bi + 1)), 48 if nxt[2] else 28,
                      frac=1.0)
            else:
                run(chain(*main))
            if own and blk == 0:
                dump('ocat', lambda: ocat[:], [128, 8, BLK], [('ocat', j) for j in range(8)])

    def phase_b():
        W = BLKB
        wg = sb("wg_bf", [128, KT, DFF], BF16)
        wu = sb("wu_bf", [128, KT, DFF], BF16)
        wd = sb("wd_bf", [128, FT, D], BF16)
        pwp = sb("pwp_bf", [128, 2, D], BF16)
        pwg = sb("pwg_bf", [128, KT, D], BF16)
        pbg = sb("pbg_sb", [128, 8])
        xr = [sb("xr%d" % i, [128, KT, W]) for i in range(2)]
        xbf = [sb("xbf%d" % i, [128, KT, W], BF16) for i in range(3)]
        pb = [sb("pb%d" % i, [128, 2, W], BF16) for i in range(2)]
        _hT = sb("hT", [128, FT, W], BF16)
        hT = [_hT, _hT]
        sg = [sb("sg%d" % i, [128, W]) for i in range(2)]
        sgp = [sb("sgp%d" % i, [128, W]) for i in range(2)]
        def load_weights():
            FG = 4
            for f0 in [0, 1, 2] + list(range(4, FT, FG)):
                f1 = min(FT, f0 + (1 if f0 < 2 else (2 if f0 == 2 else FG)))
                S.dma('pool', 'wg%d' % f0, lambda e, f0=f0, f1=f1: e.dma_start(out=wg[:, :, f0 * 128:f1 * 128], in_=wg_d[:, :, f0 * 128:f1 * 128]),
                      writes=[('wg', f) for f in range(f0, f1)])
                S.dma('pool', 'wu%d' % f0, lambda e, f0=f0, f1=f1: e.dma_start(out=wu[:, :, f0 * 128:f1 * 128], in_=wu_d[:, :, f0 * 128:f1 * 128]),
                      writes=[('wu', f) for f in range(f0, f1)])
            for f0 in range(0, FT, FG):
                f1 = min(FT, f0 + FG)
                S.dma('pool', 'wd%d' % f0, lambda e, f0=f0, f1=f1: e.dma_start(out=wd[:, f0:f1, :], in_=wd_d[:, f0:f1, :]),
                      writes=[('wd', f) for f in range(f0, f1)])
            for kt in range(KT):
                S.dma('pool', 'pwg%d' % kt, lambda e, kt=kt: e.dma_start(out=pwg[:, kt, :], in_=pwg_d[:, kt, :]), writes=[('pwg', kt)])
            S.dma('pool', 'pwp', lambda e: e.dma_start(out=pwp[:], in_=pwp_d), writes=['pwp'])
        S.dma('sp', 'pbg', lambda e: e.dma_start(out=pbg[:], in_=pbg_d), writes=['pbg'])

        nb = T_OWN // W

        def load_r(b):
            s = b % 2
            ts = slice(b * W, (b + 1) * W)
            S.dma('sp', 'xr%d' % s, lambda e: e.dma_start(out=xr[s][:], in_=x1s[:, :, ts]),
                  reads=[('x1s', (b * W) // BLK)], writes=[('xr', s, n) for n in range(KT)])
            S.dma('pool', 'pb%d' % s, lambda e: e.dma_start(out=pb[s][:], in_=pT[:, :, ts]), writes=[('pb', s)])

        def load_bf(b):
            s3 = b % 3
            ts = slice(b * W, (b + 1) * W)
            S.dma('pool', 'xbf%d' % s3, lambda e: e.dma_start(out=xbf[s3][:], in_=x1s[:, :, ts]),
                  reads=[('x1s', (b * W) // BLK)], writes=[('xbf', s3, n) for n in range(KT)])

        def gateup(b):
            s = b % 2
            s3 = b % 3
            for f in range(FT):
                pg_, pgk = next_pb()
                for kt in range(KT):
                    S.op('pe', lambda e, pg_=pg_, kt=kt, f=f: e.matmul(pg_[:, 0:W], lhsT=wg[:, kt, f * 128:(f + 1) * 128], rhs=xbf[s3][:, kt, :],
                                                                      start=(kt == 0), stop=(kt == KT - 1)),
                         reads=[('wg', f), ('xbf', s3, kt)], writes=[pgk], inc=(kt == KT - 1))
                pu_, puk = next_pb()
                for kt in range(KT):
                    S.op('pe', lambda e, pu_=pu_, kt=kt, f=f: e.matmul(pu_[:, 0:W], lhsT=wu[:, kt, f * 128:(f + 1) * 128], rhs=xbf[s3][:, kt, :],
                                                                      start=(kt == 0), stop=(kt == KT - 1)),
                         reads=[('wu', f), ('xbf', s3, kt)], writes=[puk], inc=(kt == KT - 1))
                S.op('act', lambda e, pg_=pg_, f=f: e.activation(out=sg[f % 2][:], in_=pg_[:, 0:W], func=AF.Silu),
                     reads=[pgk], writes=[('sg', f % 2)])
                S.op('dve', lambda e, pu_=pu_, f=f: e.tensor_tensor(out=hT[s][:, f, :], in0=pu_[:, 0:W], in1=sg[f % 2][:], op=ALU.mult),
                     reads=[puk, ('sg', f % 2)], writes=[('hT', f)])
                yield

        def down(b):
            s = b % 2
            for n in range(KT):
                pd_, pdk = next_pb()
                for f in range(FT):
                    S.op('pe', lambda e, pd_=pd_, f=f, n=n: e.matmul(pd_[:, 0:W], lhsT=wd[:, f, n * 128:(n + 1) * 128], rhs=hT[s][:, f, :],
                                                                    start=(f == 0), stop=(f == FT - 1)),
                         reads=[('wd', f), ('hT', f)], writes=[pdk], inc=(f == FT - 1))
                S.op('dve', lambda e, pd_=pd_, n=n: e.scalar_tensor_tensor(out=xr[s][:, n, :], in0=xr[s][:, n, :], scalar=ALPHA, in1=pd_[:, 0:W],
                                                                           op0=ALU.mult, op1=ALU.add),
                     reads=[pdk, ('xr', s, n)], writes=[('xr', s, n)])
                yield

        BK_A, BK_S, BK_O = XK[0], XK[1], YK[0]
        PMA, PMS, PMO = X[0], X[1], Y[0]

        L2 = dict(ybf=[sb("l2_ybf%d" % i, [128, W], BF16) for i in range(2)],
                  ysq=[sb("l2_ysq%d" % i, [128, W], BF16) for i in range(2)],
                  rstd=sb("l2_rstd", [128, W]), nmr=sb("l2_nmr", [128, W]),
                  t=[sb("l2_t%d" % i, [128, W]) for i in range(2)])

        def ln_prep(s, n):
            S.op('dve', lambda e: e.tensor_copy(out=L2['ybf'][n % 2][:], in_=xr[s][:, n, :]), reads=[('xr', s, n)], writes=[('l2ybf', n % 2)])
            S.op('act', lambda e: e.activation(out=L2['ysq'][n % 2][:], in_=xr[s][:, n, :], func=AF.Square),
                 reads=[('xr', s, n)], writes=[('l2ysq', n % 2)])

        def ln_stat(s, n):
            S.op('pe', lambda e: e.matmul(PMA[:, 0:W], lhsT=onesM[:], rhs=L2['ybf'][n % 2][:], start=(n == 0), stop=(n == KT - 1)),
                 reads=[('l2ybf', n % 2), 'onesM'], writes=[BK_A])
            S.op('pe', lambda e: e.matmul(PMS[:, 0:W], lhsT=onesM[:], rhs=L2['ysq'][n % 2][:], start=(n == 0), stop=(n == KT - 1)),
                 reads=[('l2ysq', n % 2), 'onesM'], writes=[BK_S])

        def ln_chain(s):
            msq = L2['t'][0]
            S.op('act', lambda e: e.activation(out=msq[:], in_=PMA[:, 0:W], func=AF.Square), reads=[BK_A], writes=[('l2t', 0)])
            S.op('dve', lambda e: e.scalar_tensor_tensor(out=L2['rstd'][:], in0=PMS[:, 0:W], scalar=LN_EPS, in1=msq[:],
                                                         op0=ALU.add, op1=ALU.subtract),
                 reads=[BK_S, ('l2t', 0)], writes=['l2rstd'])
            S.op('act', lambda e: e.activation(out=L2['rstd'][:], in_=L2['rstd'][:], func=AF.Ln), reads=['l2rstd'], writes=['l2rstd'])
            S.op('act', lambda e: e.activation(out=L2['rstd'][:], in_=L2['rstd'][:], func=AF.Exp, scale=-0.5), reads=['l2rstd'], writes=['l2rstd'])
            S.op('dve', lambda e: e.scalar_tensor_tensor(out=L2['nmr'][:], in0=PMA[:, 0:W], scalar=-1.0, in1=L2['rstd'][:],
                                                         op0=ALU.mult, op1=ALU.mult),
                 reads=[BK_A, 'l2rstd'], writes=['l2nmr'])

        def ln_norm(s, n, s3):
            t = L2['t'][n % 2]
            S.op('dve', lambda e: e.tensor_tensor(out=t[:], in0=xr[s][:, n, :], in1=L2['rstd'][:], op=ALU.mult),
                 reads=[('xr', s, n), 'l2rstd'], writes=[('l2t', n % 2)])
            S.op('dve', lambda e: e.tensor_tensor(out=t[:], in0=t[:], in1=L2['nmr'][:], op=ALU.add),
                 reads=[('l2t', n % 2), 'l2nmr'], writes=[('l2t', n % 2)])
            S.op('act', lambda e: e.activation(out=xr[s][:, n, :], in_=t[:], func=AF.Identity,
                                               bias=lnp[:, 24 + n:25 + n], scale=lnp[:, 16 + n:17 + n]),
                 reads=[('l2t', n % 2), 'lnp'], writes=[('xr', s, n)])
            S.op('dve', lambda e: e.tensor_copy(out=xbf[s3][:, n, :], in_=xr[s][:, n, :]), reads=[('xr', s, n)], writes=[('xbf', s3, n)])

        def ple(b):
            s = b % 2
            s3 = b % 3
            ts = slice(b * W, (b + 1) * W)
            for n in range(KT):
                pgo, BK_O = (Y[0], YK[0]) if n % 2 == 0 else (Y[1], YK[1])
                for kt in range(KT):
                    S.op('pe', lambda e, kt=kt, n=n, pgo=pgo: e.matmul(pgo[:, 0:W], lhsT=pwg[:, kt, n * 128:(n + 1) * 128], rhs=xbf[s3][:, kt, :],
                                                                      start=(kt == 0), stop=(kt == KT - 1)),
                         reads=[('pwg', kt), ('xbf', s3, kt)], writes=[BK_O], inc=(kt == KT - 1))
                pp_, ppk = (PMA, BK_A) if n % 2 == 0 else (PMS, BK_S)
                for k2 in range(2):
                    S.op('pe', lambda e, pp_=pp_, k2=k2, n=n: e.matmul(pp_[:, 0:W], lhsT=pwp[:, k2, n * 128:(n + 1) * 128], rhs=pb[s][:, k2, :],
                                                                      start=(k2 == 0), stop=(k2 == 1)),
                         reads=['pwp', ('pb', s)], writes=[ppk], inc=(k2 == 1))
                S.op('act', lambda e, n=n, pgo=pgo: e.activation(out=sgp[n % 2][:], in_=pgo[:, 0:W], func=AF.Sigmoid,
                                                                 bias=pbg[:, n:n + 1], scale=1.0),
                     reads=[BK_O, 'pbg'], writes=[('sgp', n % 2)])
                S.op('dve', lambda e, pp_=pp_, n=n: e.tensor_tensor(out=sgp[n % 2][:], in0=pp_[:, 0:W], in1=sgp[n % 2][:], op=ALU.mult),
                     reads=[ppk, ('sgp', n % 2)], writes=[('sgp', n % 2)])
                S.op('pool', lambda e, n=n: e.tensor_tensor(out=xr[s][:, n, :], in0=xr[s][:, n, :], in1=sgp[n % 2][:], op=ALU.add),
                     reads=[('xr', s, n), ('sgp', n % 2)], writes=[('xr', s, n)])
            S.dma('sp', 'out%d' % s, lambda e: e.dma_start(out=outT[:, :, ts], in_=xr[s][:]),
                  reads=[('xr', s, n) for n in range(KT)], writes=[('out', s)])

        load_bf(0)
        load_r(0)
        if nb > 1:
            load_bf(1)
        load_weights()
        run(gateup(0))
        run(down(0))
        for b in range(nb):
            s = b % 2
            if b + 2 < nb:
                load_bf(b + 2)
            if b + 1 < nb:
                load_r(b + 1)
                g = gateup(b + 1)
            else:
                g = iter(())
            f = 0
            alive = True
            while alive or f <= 2 * (KT - 1) + 2:
                if f % 2 == 0 and f // 2 < KT:
                    ln_prep(s, f // 2)
                try:
                    next(g)
                except StopIteration:
                    alive = False
                if f % 2 == 0 and 0 <= f // 2 - 1 < KT:
                    ln_stat(s, f // 2 - 1)
                f += 1
            ln_chain(s)
            d = down(b + 1) if b + 1 < nb else iter(())
            for n in range(KT):
                try:
                    next(d)
                except StopIteration:
                    pass
                ln_norm(s, n, b % 3)
            run(d)
            ple(b)
        S.final_wait('sp', [('out', 0), ('out', 1)])

    base = len(ctxs)
    phase_a()
    S.final_wait('sp', [('x1s', b) for b in range(T_OWN // BLK)] + [('dbg', n) for n in dbg_out])
    S.emit()
    for cm in reversed(ctxs[base:]):
        cm.__exit__(None, None, None)
    del ctxs[base:]
    phase_b()
    S.emit()
    for cm in reversed(ctxs):
        cm.__exit__(None, None, None)
    S.close()
    return nc, S, dbg_out


def _tile_rows(w):
    K, N = w.shape
    return np.ascontiguousarray(w.reshape(K // 128, 128, N).transpose(1, 0, 2))


def _cols(v):
    return np.ascontiguousarray(v.reshape(-1, 128).T)


def _tokT(a):
    T, F = a.shape
    return np.ascontiguousarray(a.T.reshape(F // 128, 128, T).transpose(1, 0, 2))


def make_weights(inp):
    f = np.float32
    w_in = np.asarray(inp["w_in"], f)[0]
    b_in = np.asarray(inp["b_in"], f)[0]
    lg = np.asarray(inp["hg_lb_logits"], f)
    cw = np.asarray(inp["ml_conv_w"], f)[0]
    cb = np.asarray(inp["ml_conv_b"], f)[0]
    W = {}
    W["w_in"] = _tile_rows(w_in)
    W["b_fm"] = np.ascontiguousarray(np.stack([b_in[c:c + 128] for c in FM_COLS], axis=1))
    btm = np.concatenate([b_in[C_HV:C_HV + 512], b_in[C_MV:C_MV + 512], b_in[C_IG:C_IG + 8]])
    W["b_tm"] = np.ascontiguousarray(np.broadcast_to(btm[None, :], (128, 1032)))
    W["lbl"] = np.ascontiguousarray(np.concatenate([_cols(lg[0]), _cols(lg[1])], axis=1))
    W["convw"] = np.ascontiguousarray(cw.reshape(4, 4, 128).transpose(2, 1, 0).reshape(128, 16))
    W["convb"] = _cols(cb)
    W["gain"] = _cols(np.concatenate([np.asarray(inp["hg_norm_g"], f)[0], np.asarray(inp["ml_norm_g"], f)[0]]))
    W["w_out"] = _tile_rows(np.asarray(inp["w_out"], f)[0])
    W["lnp"] = np.ascontiguousarray(np.concatenate([_cols(np.asarray(inp[k], f)[0]) for k in ("ln1_g", "ln1_b", "ln2_g", "ln2_b")], axis=1))
    W["wg"] = _tile_rows(np.asarray(inp["w_ffn_gate"], f)[0])
    W["wu"] = _tile_rows(np.asarray(inp["w_ffn_up"], f)[0])
    W["wd"] = _tile_rows(np.asarray(inp["w_ffn_down"], f)[0])
    W["pwp"] = _tile_rows(np.asarray(inp["ple_w_proj"], f)[0])
    W["pwg"] = _tile_rows(np.asarray(inp["ple_w_gate"], f)[0])
    W["pbg"] = _cols(np.asarray(inp["ple_b_gate"], f)[0])
    return W


def make_core(W, x_own, x_pre, p_own, flagv):
    m = dict(W)
    m["xT"] = _tokT(x_own)
    m["xTp"] = _tokT(x_pre)
    m["pT"] = _tokT(p_own)
    m["flag"] = np.full((128, 1), flagv, np.float32)
    return m


def untile_out(oT):
    return np.ascontiguousarray(oT.transpose(1, 0, 2).reshape(D, -1).T)


_NC_CACHE = {}


def kernel(**inputs):
    x = np.asarray(inputs["x"], np.float32)
    p = np.asarray(inputs["p"], np.float32)[0]
    B, SEQ, _ = x.shape
    HALF = SEQ // 2
    key = (HALF,)
    if key not in _NC_CACHE:
        _NC_CACHE[key] = build(HALF, HALF)[0]
    nc = _NC_CACHE[key]
    W = make_weights(inputs)
    in_maps = []
    for b in range(B):
        for h in range(2):
            t0 = h * HALF
            in_maps.append(make_core(W, x[b, t0:t0 + HALF], x[b, 0:HALF], p[b, t0:t0 + HALF], float(h)))
    res = run_bass_kernel_spmd(nc, in_maps, core_ids=list(range(2 * B)))
    out = np.empty((B, SEQ, D), np.float32)
    for b in range(B):
        for h in range(2):
            out[b, h * HALF:(h + 1) * HALF] = untile_out(np.asarray(res.results[2 * b + h]["outT"]))
    return out
```

```python
import math
import numpy as np
import concourse.bass as bass
import concourse.mybir as mybir
from concourse.bass_utils import run_bass_kernel_spmd

F32 = mybir.dt.float32
BF16 = mybir.dt.bfloat16
AF = mybir.ActivationFunctionType
ALU = mybir.AluOpType

D = 1024
KT = 8
PW = 3592
DFF = 2816
FT = 22
ALPHA = float(2.0 ** 0.25)
LN_EPS = 1e-5
RMS_EPS = 1e-6
C_HQ, C_HF, C_HV, C_HG, C_MQ, C_MK, C_MV, C_MO, C_IG = 0, 512, 1024, 1536, 2048, 2304, 2560, 3072, 3584
FM_COLS = ([C_HQ + 128 * j for j in range(4)] + [C_HF + 128 * j for j in range(4)] +
           [C_HG + 128 * j for j in range(4)] + [C_MQ + 128 * j for j in range(4)] +
           [C_MO + 128 * j for j in range(4)])
J_HQ, J_HF, J_HG, J_QK, J_MO = 0, 4, 8, 12, 16
BLK = 512
BLKB = 256
MID = 63


class Sched:
    def __init__(self, nc):
        self.nc = nc
        self.prog = {e: [] for e in ('pe', 'act', 'dve', 'pool', 'sp')}
        self.sems = {}
        self.cnt = {}
        self.seen = {e: {} for e in self.prog}
        self.lastw = {}
        self.readers = {}
        self._stack = []
        self.nops = {e: 0 for e in self.prog}
        self.bank_last = {}

    def sem(self, name):
        if name not in self.sems:
            cm = self.nc.semaphore(name)
            self.sems[name] = cm.__enter__()
            self._stack.append(cm)
            self.cnt[name] = 0
        return self.sems[name]

    @staticmethod
    def _isbank(k):
        return isinstance(k, tuple) and k[0] == 'BANK'

    def _deps(self, reads, writes, own=None):
        deps = {}

        def add(d):
            if d is not None and deps.get(d[0], 0) < d[1]:
                deps[d[0]] = d[1]
        for k in list(reads) + list(writes):
            if self._isbank(k):
                d = self.bank_last.get(k)
                if d is not None and d[0] != own:
                    add(d)
        for k in reads:
            if not self._isbank(k):
                add(self.lastw.get(k))
        for k in writes:
            if not self._isbank(k):
                add(self.lastw.get(k))
                for d in self.readers.get(k, ()):
                    add(d)
        return deps

    def _waits(self, e, deps, skip=None):
        waits = []
        for s, v in deps.items():
            if s == skip or self.seen[e].get(s, 0) >= v:
                continue
            self.seen[e][s] = v
            waits.append((self.sems[s], v))
        return waits

    def _record(self, reads, writes, tag):
        for k in reads:
            if self._isbank(k):
                self.bank_last[k] = tag
            else:
                self.readers.setdefault(k, []).append(tag)
        for k in writes:
            if self._isbank(k):
                self.bank_last[k] = tag
            else:
                self.lastw[k] = tag
                self.readers[k] = []

    def op(self, e, fn, reads=(), writes=(), inc=True, serial=False):
        sname = 'c_' + e
        h = self.sem(sname)
        deps = self._deps(reads, writes, own=sname)
        skip = sname if e == 'pe' else None
        if serial and self.cnt[sname] > 0:
            deps[sname] = max(deps.get(sname, 0), self.cnt[sname])
            skip = None
        waits = self._waits(e, deps, skip=skip)
        val = self.cnt[sname] + 1
        if inc:
            self.cnt[sname] = val

        def run(eng, fn=fn, waits=waits, inc=inc, h=h):
            for sh, v in waits:
                eng.wait_ge(sh, v)
            ins = fn(eng)
            if inc:
                ins.then_inc(h, 1)
        self.prog[e].append(run)
        self.nops[e] += 1
        self._record(reads, writes, (sname, val))

    def dma(self, e, stream, fn, reads=(), writes=()):
        sname = 'd_' + stream
        h = self.sem(sname)
        waits = self._waits(e, self._deps(reads, writes))
        self.cnt[sname] += 16
        val = self.cnt[sname]

        def run(eng, fn=fn, waits=waits, h=h):
            for sh, v in waits:
                eng.wait_ge(sh, v)
            fn(eng).then_inc(h, 16)
        self.prog[e].append(run)
        self.nops[e] += 1
        self._record(reads, writes, (sname, val))

    def final_wait(self, e, keys):
        deps = {}
        for k in keys:
            d = self.lastw.get(k)
            if d and deps.get(d[0], 0) < d[1]:
                deps[d[0]] = d[1]
        waits = self._waits(e, deps)

        def run(eng, waits=waits):
            for sh, v in waits:
                eng.wait_ge(sh, v)
        self.prog[e].append(run)

    def emit(self):
        with self.nc.Block() as block:
            for e, dec in (('sp', block.sync), ('act', block.scalar), ('dve', block.vector),
                           ('pool', block.gpsimd), ('pe', block.tensor)):
                prog = self.prog[e]

                def body(eng, prog=prog):
                    for r in prog:
                        r(eng)
                dec(body)
        self.prog = {e: [] for e in self.prog}

    def close(self):
        for cm in reversed(self._stack):
            cm.__exit__(None, None, None)


def build(T_PRE, T_OWN, debug=()):
    assert T_PRE % BLK == 0 and T_OWN % BLK == 0
    nc = bass.Bass("TRN2", target_bir_lowering=False)
    S = Sched(nc)
    ctxs = []

    def dram(name, shape, kind="ExternalInput", dt=F32):
        return nc.dram_tensor(name, list(shape), dt, kind=kind).ap()

    def sb(name, shape, dt=F32):
        cm = nc.sbuf_tensor(name, list(shape), dt)
        t = cm.__enter__()
        ctxs.append(cm)
        return t

    def psum(name, shape, dt=F32):
        cm = nc.psum_tensor(name, list(shape), dt)
        t = cm.__enter__()
        ctxs.append(cm)
        return t

    TP = max(T_PRE, BLK)
    xT = dram("xT", [128, KT, T_OWN])
    xTp = dram("xTp", [128, KT, TP])
    pT = dram("pT", [128, 2, T_OWN])
    flag_d = dram("flag", [128, 1])
    w_in_d = dram("w_in", [128, KT, PW])
    b_fm_d = dram("b_fm", [128, 20])
    b_tm_d = dram("b_tm", [128, 1032])
    lbl_d = dram("lbl", [128, 8])
    convw_d = dram("convw", [128, 16])
    convb_d = dram("convb", [128, 4])
    gain_d = dram("gain", [128, 8])
    w_out_d = dram("w_out", [128, KT, D])
    lnp_d = dram("lnp", [128, 32])
    wg_d = dram("wg", [128, KT, DFF])
    wu_d = dram("wu", [128, KT, DFF])
    wd_d = dram("wd", [128, FT, D])
    pwp_d = dram("pwp", [128, 2, D])
    pwg_d = dram("pwg", [128, KT, D])
    pbg_d = dram("pbg", [128, 8])
    outT = dram("outT", [128, KT, T_OWN], kind="ExternalOutput")
    x1s = dram("x1s", [128, KT, T_OWN], kind="Internal")
    dbg_out = {}

    def dump(name, ap_fn, shape, reads):
        if name not in debug:
            return
        d = dram("dbg_" + name, shape, kind="ExternalOutput")
        dbg_out[name] = d
        S.dma('sp', 'dbg_' + name, lambda e: e.dma_start(out=d, in_=ap_fn()), reads=reads, writes=[('dbg', name)])

    mask01 = sb("mask01", [128, 128])
    identf = sb("identf", [128, 128])
    ident = sb("ident", [128, 128], BF16)
    onesF = sb("onesF", [128, 128])
    onesM = sb("onesM", [128, 128], BF16)
    flag = sb("flag_sb", [128, 1])
    lnp = sb("lnp_sb", [128, 32])

    S.op('pool', lambda e: e.memset(mask01[:], 1.0), writes=['mask01'])
    S.op('pool', lambda e: e.affine_select(out=mask01[:], in_=mask01[:], pattern=[[1, 128]], compare_op=ALU.is_ge,
                                           fill=0.0, base=0, channel_multiplier=-1), reads=['mask01'], writes=['mask01'])
    mask2 = sb("mask2", [128, 2, 128])
    S.op('pool', lambda e: e.memset(mask2[:], 1.0), writes=['mask2'])
    S.op('pool', lambda e: e.affine_select(out=mask2[:], in_=mask2[:], pattern=[[0, 2], [1, 128]], compare_op=ALU.is_ge,
                                           fill=0.0, base=0, channel_multiplier=-1), reads=['mask2'], writes=['mask2'])
    S.op('pool', lambda e: e.memset(identf[:], 1.0), writes=['identf'])
    S.op('pool', lambda e: e.affine_select(out=identf[:], in_=identf[:], pattern=[[1, 128]], compare_op=ALU.is_equal,
                                           fill=0.0, base=0, channel_multiplier=-1), reads=['identf'], writes=['identf'])
    S.op('dve', lambda e: e.tensor_copy(out=ident[:], in_=identf[:]), reads=['identf'], writes=['ident'])
    S.op('pool', lambda e: e.memset(onesF[:], 1.0), writes=['onesF'])
    S.op('pool', lambda e: e.memset(onesM[:], 1.0 / 1024.0), writes=['onesM'])
    S.dma('sp', 'c0', lambda e: e.dma_start(out=flag[:], in_=flag_d), writes=['flag'])
    S.dma('sp', 'c1', lambda e: e.dma_start(out=lnp[:], in_=lnp_d), writes=['lnp'])

    PB = [psum("PB%d" % i, [128, 512]) for i in range(4)]
    X = [psum("X%d" % i, [128, 512]) for i in range(2)]
    Y = [psum("Y%d" % i, [128, 512]) for i in range(2)]
    XK = [('BANK', 'X0'), ('BANK', 'X1')]
    YK = [('BANK', 'Y0'), ('BANK', 'Y1')]
    XT = [X[i][:, 256:384].bitcast(BF16) for i in range(2)]
    XT2 = [X[i][:, 384:512].bitcast(BF16) for i in range(2)]
    rr = {'pb': 0}

    def next_pb():
        i = rr['pb'] % 4
        rr['pb'] += 1
        return PB[i], ('BANK', 'PB%d' % i)
    PG = Y[1][:, 400:416]
    PGK = YK[1]

    def run(gen):
        for _ in gen:
            pass

    def chain(*gens):
        for g in gens:
            yield from g

    def merge(ga, na, gb, nb, frac=1.0):
        a = b = 0
        da = db = False
        while not (da and db):
            if not da and (db or a * nb <= b * na * frac):
                try:
                    next(ga)
                    a += 1
                except StopIteration:
                    da = True
            elif not db:
                try:
                    next(gb)
                    b += 1
                except StopIteration:
                    db = True

    ln_tmp = {}

    def layer_norm(buf, bkeyf, W, gcol, bcol, bfout, bfkey, banks=None):
        if W not in ln_tmp:
            ln_tmp[W] = dict(
                ybf=[sb("ln_ybf%d_%d" % (W, i), [128, W], BF16) for i in range(2)],
                ysq=[sb("ln_ysq%d_%d" % (W, i), [128, W], BF16) for i in range(2)],
                rstd=sb("ln_rstd%d" % W, [128, W]),
                nmr=sb("ln_nmr%d" % W, [128, W]),
                t=[sb("ln_t%d_%d" % (W, i), [128, W]) for i in range(2)])
        L = ln_tmp[W]
        L['msq'] = L['t'][0]
        if banks is None:
            pm, pmk = next_pb()
            pq, pqk = next_pb()
        else:
            (pm, pmk), (pq, pqk) = banks
        for n in range(KT):
            yb, ys = L['ybf'][n % 2], L['ysq'][n % 2]
            S.op('dve', lambda e, n=n, yb=yb: e.tensor_copy(out=yb[:], in_=buf[:, n, :]),
                 reads=[bkeyf(n)], writes=[('ln_ybf', W, n % 2)])
            S.op('act', lambda e, n=n, ys=ys: e.activation(out=ys[:], in_=buf[:, n, :], func=AF.Square),
                 reads=[bkeyf(n)], writes=[('ln_ysq', W, n % 2)])
            S.op('pe', lambda e, n=n, yb=yb: e.matmul(pm[:, 0:W], lhsT=onesM[:], rhs=yb[:], start=(n == 0), stop=(n == KT - 1)),
                 reads=[('ln_ybf', W, n % 2), 'onesM'], writes=[pmk], inc=True)
            S.op('pe', lambda e, n=n, ys=ys: e.matmul(pq[:, 0:W], lhsT=onesM[:], rhs=ys[:], start=(n == 0), stop=(n == KT - 1)),
                 reads=[('ln_ysq', W, n % 2), 'onesM'], writes=[pqk], inc=True)
            yield
        S.op('act', lambda e: e.activation(out=L['msq'][:], in_=pm[:, 0:W], func=AF.Square), reads=[pmk], writes=[('ln_t', W, 0)])
        S.op('dve', lambda e: e.scalar_tensor_tensor(out=L['rstd'][:], in0=pq[:, 0:W], scalar=LN_EPS, in1=L['msq'][:],
                                                     op0=ALU.add, op1=ALU.subtract),
             reads=[pqk, ('ln_t', W, 0)], writes=[('ln_rstd', W)])
        S.op('act', lambda e: e.activation(out=L['rstd'][:], in_=L['rstd'][:], func=AF.Ln), reads=[('ln_rstd', W)], writes=[('ln_rstd', W)])
        S.op('act', lambda e: e.activation(out=L['rstd'][:], in_=L['rstd'][:], func=AF.Exp, scale=-0.5),
             reads=[('ln_rstd', W)], writes=[('ln_rstd', W)])
        S.op('dve', lambda e: e.scalar_tensor_tensor(out=L['nmr'][:], in0=pm[:, 0:W], scalar=-1.0, in1=L['rstd'][:],
                                                     op0=ALU.mult, op1=ALU.mult),
             reads=[pmk, ('ln_rstd', W)], writes=[('ln_nmr', W)])
        yield
        for n in range(KT):
            t = L['t'][n % 2]
            S.op('dve', lambda e, n=n, t=t: e.tensor_tensor(out=t[:], in0=buf[:, n, :], in1=L['rstd'][:], op=ALU.mult),
                 reads=[bkeyf(n), ('ln_rstd', W)], writes=[('ln_t', W, n % 2)])
            S.op('dve', lambda e, n=n, t=t: e.tensor_tensor(out=t[:], in0=t[:], in1=L['nmr'][:], op=ALU.add),
                 reads=[('ln_t', W, n % 2), ('ln_nmr', W)], writes=[('ln_t', W, n % 2)])
            S.op('act', lambda e, n=n, t=t: e.activation(out=buf[:, n, :], in_=t[:], func=AF.Identity,
                                                         bias=lnp[:, bcol + n:bcol + n + 1], scale=lnp[:, gcol + n:gcol + n + 1]),
                 reads=[('ln_t', W, n % 2), 'lnp'], writes=[bkeyf(n)])
            if bfout is not None:
                S.op('dve', lambda e, n=n: e.tensor_copy(out=bfout[:, n, :], in_=buf[:, n, :]),
                     reads=[bkeyf(n)], writes=[bfkey(n)])
            yield

    def phase_a():
        w_in = sb("w_in_bf", [128, KT, PW], BF16)
        w_out = sb("w_out_bf", [128, KT, D], BF16)
        b_fm = sb("b_fm_sb", [128, 20])
        b_tm = sb("b_tm_sb", [128, 1032])
        lbl = sb("lbl_sb", [128, 8])
        lb = sb("lb_sb", [128, 4])
        oml = sb("oml_sb", [128, 4])
        convw = sb("convw_sb", [128, 16])
        convb = sb("convb_sb", [128, 4])
        gain = sb("gain_sb", [128, 8])
        dg = sb("dg", [128, 16, 128], BF16)
        resetm = sb("resetm", [128, 512])
        xres = sb("xres", [128, KT, BLK])
        xb = [sb("xb%d" % i, [128, KT, BLK], BF16) for i in range(2)]
        NS = 2
        qt = [sb("qt%d" % i, [128, 4, BLK], BF16) for i in range(NS)]
        kTt = [sb("kTt%d" % i, [128, 4, BLK], BF16) for i in range(NS)]
        gH = [sb("gH%d" % i, [128, 4, BLK], BF16) for i in range(1)]
        gM = [sb("gM%d" % i, [128, 4, BLK], BF16) for i in range(1)]
        qk = [sb("qk%d" % i, [128, 4, BLK], BF16) for i in range(NS)]
        dec = [sb("dec%d" % i, [128, 4, 12]) for i in range(NS)]
        vh = [sb("vh%d" % i, [128, 512], BF16) for i in range(2)]
        vm = [sb("vm%d" % i, [128, 4, 129], BF16) for i in range(2)]
        convbuf = [sb("convbuf%d" % j, [128, 3 + BLK], BF16) for j in range(4)]
        tU = sb("tU", [128, BLK]); tX = sb("tX", [128, BLK]); tL1 = sb("tL1", [128, BLK]); tL2 = sb("tL2", [128, BLK])
        tB = sb("tB", [128, BLK])
        nbm = sb("nbm", [128, 4]); dl = sb("dl", [128, 4]); kb = sb("kb", [128, 4])
        nb_fm = sb("nb_fm", [128, 20]); lomb = sb("lomb", [128, 4])
        gsb = [sb("gsb%d" % i, [128, 8]) for i in range(2)]
        gef = [sb("gef%d" % i, [128, 4]) for i in range(2)]
        gsp = [sb("gsp%d" % i, [128, 4]) for i in range(2)]
        gtm = [sb("gtm%d" % i, [128, 4]) for i in range(2)]
        wS = [sb("wS%d" % i, [128, 4]) for i in range(2)]
        rT = [sb("rT%d" % i, [128, 4]) for i in range(2)]
        rI = [sb("rI%d" % i, [128, 4]) for i in range(2)]
        egl = [sb("egl%d" % i, [128, 4]) for i in range(2)]
        Sst = [sb("Sst%d" % h, [128, 128]) for h in range(4)]
        Sbf = [sb("Sbf%d" % h, [128, 128], BF16) for h in range(4)]
        Stmp = [sb("Stmp%d" % h, [128, 128]) for h in range(4)]
        Cst = [sb("Cst%d" % h, [128, 129]) for h in range(4)]
        Cbf = [sb("Cbf%d" % h, [128, 129], BF16) for h in range(4)]
        Ctmp = [sb("Ctmp%d" % h, [128, 129]) for h in range(4)]
        kTM2 = [sb("kTM2_%d" % i, [128, 2, 128], BF16) for i in range(2)]
        AT2 = [sb("AT2_%d" % i, [128, 2, 128], BF16) for i in range(2)]
        ATr2 = [sb("ATr2_%d" % i, [128, 2, 128], BF16) for i in range(2)]
        on2 = [sb("on2_%d" % i, [128, 2, 128], BF16) for i in range(2)]
        junk = sb("junk", [128, 128])
        ssq2 = [sb("ssq2_%d" % i, [128, 2]) for i in range(2)]
        rsd2 = [sb("rsd2_%d" % i, [128, 2]) for i in range(2)]
        mt2 = [sb("mt2_%d" % i, [128, 4, 2]) for i in range(2)]
        ocat = sb("ocat", [128, 8, BLK], BF16)
        mrr = {'k': 0}

        WG = [('hf', C_HF, 512), ('mqk', C_MQ, 512), ('hv', C_HV, 512), ('mv', C_MV, 512), ('gt', C_IG, 8),
              ('hq', C_HQ, 512), ('hg', C_HG, 512), ('mo', C_MO, 512)]

        def wkey(c0):
            for nm, g0, gn in WG:
                if g0 <= c0 < g0 + gn:
                    return ('w_in', nm)
            raise KeyError(c0)
        def load_w_in(part):
            for nm, g0, gn in (WG[:2], WG[2:5], WG[5:])[part]:
                S.dma('pool', 'w_in_' + nm, lambda e, g0=g0, gn=gn: e.dma_start(out=w_in[:, :, g0:g0 + gn], in_=w_in_d[:, :, g0:g0 + gn]),
                      writes=[('w_in', nm)])
        for i, (t, d, k) in enumerate(((b_fm, b_fm_d, 'b_fm'), (b_tm, b_tm_d, 'b_tm'), (lbl, lbl_d, 'lbl'),
                                       (convw, convw_d, 'convw'), (convb, convb_d, 'convb'), (gain, gain_d, 'gain'))):
            S.dma('sp', 'cA%d' % i, lambda e, t=t, d=d: e.dma_start(out=t[:], in_=d), writes=[k])
        S.op('dve', lambda e: e.tensor_tensor(out=lb[:], in0=lbl[:, 0:4], in1=lbl[:, 4:8], op=ALU.subtract), reads=['lbl'], writes=['lb'])
        S.op('act', lambda e: e.activation(out=lb[:], in_=lb[:], func=AF.Sigmoid), reads=['lb'], writes=['lb'])
        S.op('dve', lambda e: e.tensor_scalar(out=oml[:], in0=lb[:], scalar1=-1.0, scalar2=1.0, op0=ALU.mult, op1=ALU.add),
             reads=['lb'], writes=['oml'])
        S.op('act', lambda e: e.activation(out=lomb[:], in_=oml[:], func=AF.Ln), reads=['oml'], writes=['lomb'])
        S.op('dve', lambda e: e.tensor_scalar(out=nb_fm[:], in0=b_fm[:], scalar1=-1.0, scalar2=None, op0=ALU.mult),
             reads=['b_fm'], writes=['nb_fm'])
        for j in range(4):
            for k in range(4):
                S.op('dve', lambda e, j=j, k=k: e.tensor_scalar(out=dg[:, j * 4 + k, :], in0=identf[:],
                                                                scalar1=convw[:, j * 4 + k:j * 4 + k + 1], scalar2=None, op0=ALU.mult),
                     reads=['identf', 'convw'], writes=[('dg', j)])
        S.op('pool', lambda e: e.memset(resetm[:], 1.0), writes=['resetm'])
        S.op('pool', lambda e: e.memset(resetm[:].rearrange("p (c t) -> p c t", t=128)[:, :, 0:1], 0.0),
             reads=['resetm'], writes=['resetm'])
        for g_ in range(2):
            S.op('pool', lambda e, g_=g_: e.memset(AT2[g_][:], 0.0), writes=[('AT2', g_)])
        for j in range(4):
            S.op('pool', lambda e, j=j: e.memset(convbuf[j][:, 0:3], 0.0), writes=[('cb_carry', j)])
        for h in range(4):
            S.op('pool', lambda e, h=h: e.memset(Sst[h][:], 0.0), writes=[('S', h)])
            S.op('pool', lambda e, h=h: e.memset(Cst[h][:], 0.0), writes=[('C', h)])
            S.op('pool', lambda e, h=h: e.memset(Cbf[h][:], 0.0), writes=[('Cbf', h)])
        for s in range(2):
            S.op('pool', lambda e, s=s: e.memset(vm[s][:], 1.0), writes=[('vm', s)])

        def load_w_out():
            wst = xres[:].rearrange("p (a two) b -> p a (two b)", two=2)
            for half in range(2):
                S.dma('sp', 'wo_st', lambda e, half=half: e.dma_start(out=wst, in_=w_out_d[:, half * 4:half * 4 + 4, :]),
                      writes=[('xres', n) for n in range(KT)])
                for q in range(4):
                    kt = half * 4 + q
                    S.op('dve', lambda e, q=q, kt=kt: e.tensor_scalar(out=w_out[:, kt, :], in0=wst[:, q, :],
                                                                      scalar1=gain[:, kt:kt + 1], scalar2=None, op0=ALU.mult),
                         reads=[('xres', n) for n in range(KT)] + ['gain'], writes=[('w_out', kt)])

        def load_x(src, blk, slot):
            for half in range(2):
                S.dma('pool', 'xb%d_%d' % (slot, half),
                      lambda e, half=half: e.dma_start(out=xb[slot][:, half * 4:half * 4 + 4, :],
                                                       in_=src[:, half * 4:half * 4 + 4, blk * BLK:(blk + 1) * BLK]),
                      writes=[('xb', slot, half)])

        def fm_proj(slot, j):
            ps, pk = next_pb()
            c0 = FM_COLS[j]
            for kt in range(KT):
                S.op('pe', lambda e, kt=kt: e.matmul(ps[:], lhsT=w_in[:, kt, c0:c0 + 128], rhs=xb[slot][:, kt, :],
                                                     start=(kt == 0), stop=(kt == KT - 1)),
                     reads=[wkey(c0), ('xb', slot, kt // 4)], writes=[pk], inc=(kt == KT - 1))
            return ps, pk

        def bias(j):
            return b_fm[:, j:j + 1]

        def fm_a(slot, fs, own, need_q=True):
            for h in range(4):
                ps, pk = fm_proj(slot, J_HF + h)
                S.op('act', lambda e, ps=ps, h=h: e.activation(out=tU[:], in_=ps[:], func=AF.Exp, bias=nb_fm[:, J_HF + h:J_HF + h + 1], scale=-1.0),
                     reads=[pk, 'nb_fm'], writes=['tU'])
                S.op('dve', lambda e, ps=ps, h=h: e.tensor_scalar(out=tX[:], in0=ps[:], scalar1=bias(J_HF + h), scalar2=None, op0=ALU.add),
                     reads=[pk, 'b_fm'], writes=['tX'])
                yield
                S.op('act', lambda e, h=h: e.activation(out=tL1[:], in_=tU[:], func=AF.Ln, scale=lb[:, h:h + 1], bias=1.0),
                     reads=['tU', 'lb'], writes=['tL1'])
                S.op('act', lambda e: e.activation(out=tL2[:], in_=tU[:], func=AF.Ln, bias=1.0, scale=1.0), reads=['tU'], writes=['tL2'])
                yield
                S.op('pool', lambda e: e.tensor_tensor(out=tL1[:], in0=tL1[:], in1=tL2[:], op=ALU.subtract), reads=['tL1', 'tL2'], writes=['tL1'])
                S.op('pool', lambda e: e.tensor_tensor(out=tX[:], in0=tX[:], in1=tL2[:], op=ALU.add), reads=['tX', 'tL2'], writes=['tX'])
                yield
                S.op('dve', lambda e: e.tensor_tensor_scan(out=tB[:], data0=resetm[:], data1=tL1[:], initial=0.0, op0=ALU.mult, op1=ALU.add),
                     reads=['tL1', 'resetm'], writes=['tB'])
                yield
                tBv = tB[:].rearrange("p (c t) -> p c t", t=128)
                S.op('dve', lambda e, tBv=tBv: e.tensor_scalar(out=nbm[:], in0=tBv[:, :, MID], scalar1=-1.0, scalar2=None, op0=ALU.mult),
                     reads=['tB'], writes=['nbm'])
                S.op('dve', lambda e, tBv=tBv: e.tensor_tensor(out=dl[:], in0=tBv[:, :, 127], in1=tBv[:, :, MID], op=ALU.subtract),
                     reads=['tB'], writes=['dl'])
                S.op('dve', lambda e, tBv=tBv, h=h: e.tensor_scalar(out=kb[:], in0=tBv[:, :, MID], scalar1=lomb[:, h:h + 1], scalar2=None, op0=ALU.add),
                     reads=['tB', 'lomb'], writes=['kb'])
                S.op('pool', lambda e: e.tensor_tensor(out=tX[:], in0=tX[:], in1=tB[:], op=ALU.add), reads=['tX', 'tB'], writes=['tX'])
                yield
                for c in range(4):
                    cs = slice(c * 128, (c + 1) * 128)
                    S.op('act', lambda e, c=c, cs=cs, h=h: e.activation(out=kTt[fs][:, h, cs], in_=tX[:, cs], func=AF.Exp,
                                                                         bias=kb[:, c:c + 1], scale=-1.0),
                         reads=['tX', 'kb'], writes=[('kTt', fs, h)])
                S.op('act', lambda e, h=h: e.activation(out=dec[fs][:, h, 0:4], in_=dl[:], func=AF.Exp), reads=['dl'], writes=[('dec', fs, h, 0)])
                S.op('act', lambda e, h=h, tBv=tBv: e.activation(out=dec[fs][:, h, 4:8], in_=tBv[:, :, 127], func=AF.Exp),
                     reads=['tB'], writes=[('dec', fs, h, 1)])
                S.op('act', lambda e, h=h, tBv=tBv: e.activation(out=dec[fs][:, h, 8:12], in_=tBv[:, :, MID], func=AF.Exp),
                     reads=['tB'], writes=[('dec', fs, h, 2)])
                yield
                if own:
                    ps, pk = fm_proj(slot, J_HQ + h)
                    S.op('act', lambda e, ps=ps, h=h: e.activation(out=tU[:], in_=ps[:], func=AF.Exp, bias=nb_fm[:, J_HQ + h:J_HQ + h + 1], scale=-1.0),
                         reads=[pk, 'nb_fm'], writes=['tU'])
                    S.op('dve', lambda e, ps=ps, h=h: e.tensor_scalar(out=tL1[:], in0=ps[:], scalar1=bias(J_HQ + h), scalar2=None, op0=ALU.add),
                         reads=[pk, 'b_fm'], writes=['tL1'])
                    yield
                    S.op('act', lambda e: e.activation(out=tL2[:], in_=tU[:], func=AF.Ln, bias=1.0, scale=1.0), reads=['tU'], writes=['tL2'])
                    yield
                    S.op('pool', lambda e: e.tensor_tensor(out=tL2[:], in0=tB[:], in1=tL2[:], op=ALU.subtract), reads=['tB', 'tL2'], writes=['tL2'])
                    yield
                    for c in range(4):
                        cs = slice(c * 128, (c + 1) * 128)
                        S.op('act', lambda e, c=c, cs=cs: e.activation(out=tL2[:, cs], in_=tL2[:, cs], func=AF.Exp, bias=nbm[:, c:c + 1], scale=1.0),
                             reads=['tL2', 'nbm'], writes=['tL2'])
                    yield
                    S.op('pool', lambda e, h=h: e.tensor_tensor(out=qt[fs][:, h, :], in0=tL1[:], in1=tL2[:], op=ALU.mult),
                         reads=['tL1', 'tL2'], writes=[('qt', fs, h)])
                    yield
            for j in range(4):
                if j < 2 and not need_q:
                    continue
                ps, pk = fm_proj(slot, J_QK + j)
                S.op('act', lambda e, ps=ps, j=j: e.activation(out=convbuf[j][:, 3:3 + BLK], in_=ps[:], func=AF.Identity,
                                                               bias=bias(J_QK + j), scale=1.0),
                     reads=[pk, 'b_fm'], writes=[('cb_body', j)])
                ps2, pk2 = next_pb()
                for k in range(4):
                    S.op('pe', lambda e, ps2=ps2, j=j, k=k: e.matmul(ps2[:], lhsT=dg[:, j * 4 + k, :], rhs=convbuf[j][:, k:k + BLK],
                                                                    start=(k == 0), stop=(k == 3)),
                         reads=[('dg', j), ('cb_body', j), ('cb_carry', j)], writes=[pk2], inc=(k == 3))
                S.op('act', lambda e, ps2=ps2, j=j: e.activation(out=qk[fs][:, j, :], in_=ps2[:], func=AF.Silu,
                                                                 bias=convb[:, j:j + 1], scale=1.0),
                     reads=[pk2, 'convb'], writes=[('qk', fs, j)])
                S.op('pool', lambda e, j=j: e.tensor_copy(out=convbuf[j][:, 0:3], in_=convbuf[j][:, BLK:BLK + 3]),
                     reads=[('cb_body', j)], writes=[('cb_carry', j)])
                yield

        def fm_b(slot):
            for h in range(4):
                ps, pk = fm_proj(slot, J_MO + h)
                S.op('act', lambda e, ps=ps, h=h: e.activation(out=gM[0][:, h, :], in_=ps[:], func=AF.Sigmoid, bias=bias(J_MO + h), scale=1.0),
                     reads=[pk, 'b_fm'], writes=[('gM', 0, h)])
                yield
            for h in range(4):
                ps, pk = fm_proj(slot, J_HG + h)
                S.op('act', lambda e, ps=ps, h=h: e.activation(out=gH[0][:, h, :], in_=ps[:], func=AF.Silu, bias=bias(J_HG + h), scale=1.0),
                     reads=[pk, 'b_fm'], writes=[('gH', 0, h)])
                yield

        def tm_proj(slot, c, gs):
            cs = slice(c * 128, (c + 1) * 128)
            ps, pk = next_pb()
            for kt in range(KT):
                S.op('pe', lambda e, ps=ps, kt=kt: e.matmul(ps[:], lhsT=xb[slot][:, kt, cs], rhs=w_in[:, kt, C_HV:C_HV + 512],
                                                            start=(kt == 0), stop=(kt == KT - 1)),
                     reads=[('w_in', 'hv'), ('xb', slot, kt // 4)], writes=[pk], inc=(kt == KT - 1))
            S.op('dve', lambda e, ps=ps: e.tensor_tensor(out=vh[gs][:], in0=ps[:], in1=b_tm[:, 0:512], op=ALU.add),
                 reads=[pk, 'b_tm'], writes=[('vh', gs)])
            ps, pk = next_pb()
            for kt in range(KT):
                S.op('pe', lambda e, ps=ps, kt=kt: e.matmul(ps[:], lhsT=xb[slot][:, kt, cs], rhs=w_in[:, kt, C_MV:C_MV + 512],
                                                            start=(kt == 0), stop=(kt == KT - 1)),
                     reads=[('w_in', 'mv'), ('xb', slot, kt // 4)], writes=[pk], inc=(kt == KT - 1))
            S.op('dve', lambda e, ps=ps: e.tensor_tensor(out=vm[gs][:, :, 0:128],
                                                         in0=ps[:].rearrange("p (h v) -> p h v", v=128),
                                                         in1=b_tm[:, 512:1024].rearrange("p (h v) -> p h v", v=128), op=ALU.add),
                 reads=[pk, 'b_tm'], writes=[('vm', gs)])

        def gates(slot, c, gs):
            cs = slice(c * 128, (c + 1) * 128)
            for kt in range(KT):
                S.op('pe', lambda e, kt=kt: e.matmul(PG[:, 0:8], lhsT=xb[slot][:, kt, cs], rhs=w_in[:, kt, C_IG:C_IG + 8],
                                                     start=(kt == 0), stop=(kt == KT - 1)),
                     reads=[('w_in', 'gt'), ('xb', slot, kt // 4)], writes=[PGK], inc=(kt == KT - 1))
            S.op('dve', lambda e: e.tensor_tensor(out=gsb[gs][:], in0=PG[:, 0:8], in1=b_tm[:, 1024:1032], op=ALU.add),
                 reads=[PGK, 'b_tm'], writes=[('gsb', gs)])
            S.op('act', lambda e: e.activation(out=gef[gs][:], in_=gsb[gs][:, 4:8], func=AF.Exp, scale=-1.0),
                 reads=[('gsb', gs)], writes=[('gef', gs)])
            S.op('act', lambda e: e.activation(out=gsp[gs][:], in_=gef[gs][:], func=AF.Ln, bias=1.0, scale=1.0),
                 reads=[('gef', gs)], writes=[('gsp', gs)])
            S.op('pe', lambda e: e.matmul(PG[:, 8:12], lhsT=mask01[:], rhs=gsp[gs][:], start=True, stop=True),
                 reads=['mask01', ('gsp', gs)], writes=[PGK])
            S.op('pe', lambda e: e.matmul(PG[:, 12:16], lhsT=onesF[:], rhs=gsp[gs][:], start=True, stop=True),
                 reads=['onesF', ('gsp', gs)], writes=[PGK])
            S.op('dve', lambda e: e.tensor_tensor(out=gtm[gs][:], in0=PG[:, 8:12], in1=gsb[gs][:, 0:4], op=ALU.add),
                 reads=[PGK, ('gsb', gs)], writes=[('gtm', gs)])
            S.op('act', lambda e: e.activation(out=wS[gs][:], in_=gtm[gs][:], func=AF.Exp), reads=[('gtm', gs)], writes=[('wS', gs)])
            S.op('act', lambda e: e.activation(out=rI[gs][:], in_=PG[:, 8:12], func=AF.Exp, scale=1.0, bias=math.log(8.0)),
                 reads=[PGK], writes=[('rT', gs)])
            S.op('act', lambda e: e.activation(out=egl[gs][:], in_=PG[:, 12:16], func=AF.Exp, scale=-1.0),
                 reads=[PGK], writes=[('egl', gs)])

        def grp_stages(kind, g, h0, fs, c, gs, own):
            cs = slice(c * 128, (c + 1) * 128)
            Xg, Yg, XKg, YKg = X[g], Y[g], XK[g], YK[g]
            hs = (h0, h0 + 1)
            jj = h0 // 2
            if kind == 'H':
                for k, h in enumerate(hs):
                    S.op('pe', lambda e, k=k, h=h: e.transpose(out=XT[g][:, k * 128:(k + 1) * 128], in_=kTt[fs][:, h, cs], identity=ident[:]),
                         reads=[('kTt', fs, h), 'ident'], writes=[XKg], inc=(k == 1))
                if own:
                    for k, h in enumerate(hs):
                        S.op('pe', lambda e, k=k, h=h: e.matmul(Xg[:, k * 128:(k + 1) * 128], lhsT=kTt[fs][:, h, cs], rhs=qt[fs][:, h, cs],
                                                                start=True, stop=True),
                             reads=[('kTt', fs, h), ('qt', fs, h)], writes=[XKg], inc=(k == 1))
            else:
                for k, h in enumerate(hs):
                    b0 = k * 64
                    S.op('pe', lambda e, k=k, b0=b0: e.transpose(out=XT[g][:, k * 128:k * 128 + 64], in_=qk[fs][b0:b0 + 64, 2 + jj, cs],
                                                                 identity=ident[b0:b0 + 64, b0:b0 + 64]),
                         reads=[('qk', fs, 2 + jj), 'ident'], writes=[XKg], serial=(k == 1))
                    if own:
                        S.op('pe', lambda e, k=k, b0=b0: e.matmul(Xg[:, k * 128:(k + 1) * 128], lhsT=qk[fs][b0:b0 + 64, 2 + jj, cs],
                                                                  rhs=qk[fs][b0:b0 + 64, jj, cs], start=True, stop=True),
                             reads=[('qk', fs, jj), ('qk', fs, 2 + jj)], writes=[XKg])
            yield
            if kind == 'H':
                S.op('act', lambda e: e.copy(out=kTM2[g][:].rearrange("p a b -> p (a b)"), in_=XT[g][:, 0:256]),
                     reads=[XKg], writes=[('kTM2', g)])
                if own:
                    S.op('dve', lambda e: e.copy_predicated(out=AT2[g][:].rearrange("p a b -> p (a b)"),
                                                            mask=mask2[:].rearrange("p a b -> p (a b)").bitcast(mybir.dt.int32),
                                                            data=Xg[:, 0:256]),
                         reads=[XKg, 'mask2', ('AT2', g)], writes=[('AT2', g)])
                    if c == 0:
                        for h in hs:
                            S.op('dve', lambda e, h=h: e.tensor_scalar(out=Sbf[h][:], in0=Sst[h][:], scalar1=dec[fs][:, h, 8 + c:9 + c],
                                                                       scalar2=None, op0=ALU.mult),
                                 reads=[('S', h), ('dec', fs, h, 2)], writes=[('Sbf', h)])
            else:
                for k, h in enumerate(hs):
                    S.op('act', lambda e, k=k, h=h: e.activation(out=kTM2[g][:, k, 0:64], in_=XT[g][:, k * 128:k * 128 + 64], func=AF.Identity,
                                                                 scale=wS[gs][:, h:h + 1]),
                         reads=[XKg, ('wS', gs)], writes=[('kTM2', g)])
                if own:
                    for k, h in enumerate(hs):
                        S.op('dve', lambda e, k=k, h=h: e.scalar_tensor_tensor(out=AT2[g][:, k, :], in0=Xg[:, k * 128:(k + 1) * 128],
                                                                               scalar=wS[gs][:, h:h + 1], in1=mask01[:], op0=ALU.mult, op1=ALU.mult),
                             reads=[XKg, 'mask01', ('wS', gs)], writes=[('AT2', g)])
            yield
            for k, h in enumerate(hs):
                if kind == 'H':
                    v_ap = vh[gs][:, h * 128:(h + 1) * 128]
                    S.op('pe', lambda e, k=k, v_ap=v_ap: e.matmul(Yg[:, k * 128:(k + 1) * 128], lhsT=kTM2[g][:, k, :], rhs=v_ap, start=True, stop=True),
                         reads=[('kTM2', g), ('vh', gs)], writes=[YKg])
                else:
                    b0 = k * 64
                    S.op('pe', lambda e, k=k, h=h, b0=b0: e.matmul(Yg[b0:b0 + 64, 0:129], lhsT=kTM2[g][:, k, 0:64], rhs=vm[gs][:, h, :], start=True, stop=True),
                         reads=[('kTM2', g), ('vm', gs)], writes=[YKg])
            if own:
                for k, h in enumerate(hs):
                    if kind == 'H':
                        v_ap = vh[gs][:, h * 128:(h + 1) * 128]
                        po = Yg[:, 256 + k * 128:256 + (k + 1) * 128]
                        S.op('pe', lambda e, k=k, po=po, v_ap=v_ap: e.matmul(po, lhsT=AT2[g][:, k, :], rhs=v_ap, start=True, stop=False),
                             reads=[('AT2', g), ('vh', gs)], writes=[YKg], inc=False)
                        S.op('pe', lambda e, po=po, h=h: e.matmul(po, lhsT=qt[fs][:, h, cs], rhs=Sbf[h][:], start=False, stop=True),
                             reads=[('qt', fs, h), ('Sbf', h)], writes=[YKg])
                    else:
                        b0 = k * 64
                        po = Yg[:, 129 + k * 129:129 + (k + 1) * 129]
                        S.op('pe', lambda e, k=k, h=h, po=po: e.matmul(po, lhsT=AT2[g][:, k, :], rhs=vm[gs][:, h, :], start=True, stop=False),
                             reads=[('AT2', g), ('vm', gs)], writes=[YKg], inc=False)
                        S.op('pe', lambda e, h=h, po=po, b0=b0: e.matmul(po, lhsT=qk[fs][b0:b0 + 64, jj, cs], rhs=Cbf[h][b0:b0 + 64, :], start=False, stop=True),
                             reads=[('qk', fs, jj), ('Cbf', h)], writes=[YKg])
            yield
            for k, h in enumerate(hs):
                if kind == 'H':
                    S.op('dve', lambda e, h=h: e.tensor_scalar(out=Stmp[h][:], in0=Sst[h][:], scalar1=dec[fs][:, h, 4 + c:5 + c],
                                                               scalar2=None, op0=ALU.mult),
                         reads=[('S', h), ('dec', fs, h, 1)], writes=[('Stmp', h)])
                    S.op('dve', lambda e, k=k, h=h: e.scalar_tensor_tensor(out=Sst[h][:], in0=Yg[:, k * 128:(k + 1) * 128], scalar=dec[fs][:, h, c:c + 1],
                                                                           in1=Stmp[h][:], op0=ALU.mult, op1=ALU.add),
                         reads=[YKg, ('dec', fs, h, 0), ('Stmp', h)], writes=[('S', h)])
                    if own and c < 3:
                        S.op('dve', lambda e, h=h: e.tensor_scalar(out=Sbf[h][:], in0=Sst[h][:], scalar1=dec[fs][:, h, 9 + c:10 + c],
                                                                   scalar2=None, op0=ALU.mult),
                             reads=[('S', h), ('dec', fs, h, 2)], writes=[('Sbf', h)])
                else:
                    b0 = k * 64
                    S.op('dve', lambda e, h=h, b0=b0: e.tensor_scalar(out=Ctmp[h][b0:b0 + 64, :], in0=Cst[h][b0:b0 + 64, :],
                                                                      scalar1=egl[gs][b0:b0 + 64, h:h + 1], scalar2=None, op0=ALU.mult),
                         reads=[('C', h), ('egl', gs)], writes=[('Ctmp', h)])
                    S.op('dve', lambda e, h=h, b0=b0: e.scalar_tensor_tensor(out=Cst[h][b0:b0 + 64, :], in0=Yg[b0:b0 + 64, 0:129],
                                                                             scalar=egl[gs][b0:b0 + 64, h:h + 1], in1=Ctmp[h][b0:b0 + 64, :],
                                                                             op0=ALU.mult, op1=ALU.add),
                         reads=[YKg, ('egl', gs), ('Ctmp', h)], writes=[('C', h)])
                    S.op('pool', lambda e, h=h, b0=b0: e.tensor_copy(out=Cbf[h][b0:b0 + 64, :], in_=Cst[h][b0:b0 + 64, :]),
                         reads=[('C', h)], writes=[('Cbf', h)])
            if not own:
                yield
                return
            if kind == 'H':
                pos = [Yg[:, 256 + k * 128:256 + (k + 1) * 128] for k in range(2)]
            else:
                pos = [Yg[:, 129 + k * 129:129 + k * 129 + 128] for k in range(2)]
                m = mt2[g]
                den = Yg[:, 129:387].rearrange("p (k c) -> p k c", c=129)[:, :, 128]
                S.op('dve', lambda e: e.tensor_tensor(out=m[:, 0, :], in0=den, in1=rI[gs][:, h0:h0 + 2], op=ALU.max),
                     reads=[YKg, ('rT', gs)], writes=[('mt2', g)])
                S.op('dve', lambda e: e.scalar_tensor_tensor(out=m[:, 1, :], in0=den, scalar=-1.0, in1=m[:, 0, :], op0=ALU.mult, op1=ALU.max),
                     reads=[YKg, ('mt2', g)], writes=[('mt2', g)])
                S.op('dve', lambda e: e.reciprocal(out=m[:, 3, :], in_=m[:, 1, :]), reads=[('mt2', g)], writes=[('mt2', g)])
            for k in range(2):
                if kind == 'H':
                    S.op('act', lambda e, k=k: e.activation(out=junk[:], in_=pos[k], func=AF.Square, accum_out=ssq2[g][:, k:k + 1]),
                         reads=[YKg], writes=['junk', ('ssq2', g)])
                else:
                    S.op('act', lambda e, k=k: e.activation(out=junk[:], in_=pos[k], func=AF.Square, accum_out=ssq2[g][:, k:k + 1],
                                                            scale=mt2[g][:, 3, k:k + 1]),
                         reads=[YKg, ('mt2', g)], writes=['junk', ('ssq2', g)])
            S.op('act', lambda e: e.activation(out=rsd2[g][:], in_=ssq2[g][:], func=AF.Ln, scale=1.0 / 128.0, bias=RMS_EPS),
                 reads=[('ssq2', g)], writes=[('rsd2', g)])
            S.op('act', lambda e: e.activation(out=rsd2[g][:], in_=rsd2[g][:], func=AF.Exp, scale=-0.5),
                 reads=[('rsd2', g)], writes=[('rsd2', g)])
            if kind == 'M':
                S.op('dve', lambda e: e.tensor_tensor(out=rsd2[g][:], in0=rsd2[g][:], in1=mt2[g][:, 3, :], op=ALU.mult),
                     reads=[('rsd2', g), ('mt2', g)], writes=[('rsd2', g)])
            for k in range(2):
                S.op('act', lambda e, k=k: e.activation(out=on2[g][:, k, :], in_=pos[k], func=AF.Identity, scale=rsd2[g][:, k:k + 1]),
                     reads=[YKg, ('rsd2', g)], writes=[('on2', g)])
            yield
            for k in range(2):
                S.op('pe', lambda e, k=k: e.transpose(out=XT2[g][:, k * 128:(k + 1) * 128], in_=on2[g][:, k, :], identity=ident[:]),
                     reads=[('on2', g), 'ident'], writes=[XKg], inc=(k == 1))
            yield
            j0 = h0 if kind == 'H' else 4 + h0
            gt = gH[0] if kind == 'H' else gM[0]
            gk = 'gH' if kind == 'H' else 'gM'
            S.op('dve', lambda e: e.tensor_tensor(out=ocat[:, j0:j0 + 2, cs], in0=XT2[g][:, 0:256].rearrange("p (a b) -> p a b", b=128),
                                                  in1=gt[:, h0:h0 + 2, cs], op=ALU.mult),
                 reads=[XKg, (gk, 0, h0), (gk, 0, h0 + 1)], writes=[('ocat', j0), ('ocat', j0 + 1)])
            yield

        def mixer_kind(kind, fs, c, gs, own, mid_hook=None):
            ga = grp_stages(kind, 0, 0, fs, c, gs, own)
            gb = grp_stages(kind, 1, 2, fs, c, gs, own)
            alive = [ga, gb]
            step = 0
            while alive:
                for g_ in list(alive):
                    try:
                        next(g_)
                    except StopIteration:
                        alive.remove(g_)
                step += 1
                if step == 4 and mid_hook is not None:
                    yield
                    mid_hook()
                yield

        def outproj_ln(blk, next_slot=None):
            W = BLK
            if W not in ln_tmp:
                ln_tmp[W] = dict(
                    ybf=[sb("ln_ybf%d_%d" % (W, i), [128, W], BF16) for i in range(2)],
                    ysq=[sb("ln_ysq%d_%d" % (W, i), [128, W], BF16) for i in range(2)],
                    rstd=sb("ln_rstd%d" % W, [128, W]),
                    nmr=sb("ln_nmr%d" % W, [128, W]),
                    t=[sb("ln_t%d_%d" % (W, i), [128, W]) for i in range(2)])
            L = ln_tmp[W]
            pm, pmk, pq, pqk = X[0], XK[0], Y[0], YK[0]

            def prep(n):
                S.op('dve', lambda e: e.tensor_copy(out=L['ybf'][n % 2][:], in_=xres[:, n, :]), reads=[('xres', n)], writes=[('ln_ybf', W, n % 2)])
                S.op('act', lambda e: e.activation(out=L['ysq'][n % 2][:], in_=xres[:, n, :], func=AF.Square),
                     reads=[('xres', n)], writes=[('ln_ysq', W, n % 2)])

            def stat(n):
                S.op('pe', lambda e: e.matmul(pm[:], lhsT=onesM[:], rhs=L['ybf'][n % 2][:], start=(n == 0), stop=(n == KT - 1)),
                     reads=[('ln_ybf', W, n % 2), 'onesM'], writes=[pmk])
                S.op('pe', lambda e: e.matmul(pq[:], lhsT=onesM[:], rhs=L['ysq'][n % 2][:], start=(n == 0), stop=(n == KT - 1)),
                     reads=[('ln_ysq', W, n % 2), 'onesM'], writes=[pqk])

            S.dma('sp', 'xres', lambda e: e.dma_start(out=xres[:], in_=xT[:, :, blk * BLK:(blk + 1) * BLK]),
                  writes=[('xres', n) for n in range(KT)])
            for n in range(KT + 2):
                if n < KT:
                    ps, pk = next_pb()
                    for kt in range(KT):
                        S.op('pe', lambda e, ps=ps, kt=kt, n=n: e.matmul(ps[:], lhsT=w_out[:, kt, n * 128:(n + 1) * 128], rhs=ocat[:, kt, :],
                                                                        start=(kt == 0), stop=(kt == KT - 1)),
                             reads=[('w_out', kt), ('ocat', kt)], writes=[pk], inc=(kt == KT - 1))
                    S.op('dve', lambda e, ps=ps, n=n: e.scalar_tensor_tensor(out=xres[:, n, :], in0=xres[:, n, :], scalar=ALPHA, in1=ps[:],
                                                                             op0=ALU.mult, op1=ALU.add),
                         reads=[pk, ('xres', n)], writes=[('xres', n)])
                if 0 <= n - 1 < KT:
                    prep(n - 1)
                if 0 <= n - 2 < KT:
                    stat(n - 2)
                yield
            msq = L['t'][0]
            S.op('act', lambda e: e.activation(out=msq[:], in_=pm[:], func=AF.Square), reads=[pmk], writes=[('ln_t', W, 0)])
            S.op('dve', lambda e: e.scalar_tensor_tensor(out=L['rstd'][:], in0=pq[:], scalar=LN_EPS, in1=msq[:], op0=ALU.add, op1=ALU.subtract),
                 reads=[pqk, ('ln_t', W, 0)], writes=[('ln_rstd', W)])
            S.op('act', lambda e: e.activation(out=L['rstd'][:], in_=L['rstd'][:], func=AF.Ln), reads=[('ln_rstd', W)], writes=[('ln_rstd', W)])
            S.op('act', lambda e: e.activation(out=L['rstd'][:], in_=L['rstd'][:], func=AF.Exp, scale=-0.5), reads=[('ln_rstd', W)], writes=[('ln_rstd', W)])
            S.op('dve', lambda e: e.scalar_tensor_tensor(out=L['nmr'][:], in0=pm[:], scalar=-1.0, in1=L['rstd'][:], op0=ALU.mult, op1=ALU.mult),
                 reads=[pmk, ('ln_rstd', W)], writes=[('ln_nmr', W)])
            yield
            fb = fm_b(next_slot) if next_slot is not None else iter(())
            for n in range(KT):
                try:
                    next(fb)
                except StopIteration:
                    pass
                t = L['t'][n % 2]
                S.op('dve', lambda e, n=n, t=t: e.tensor_tensor(out=t[:], in0=xres[:, n, :], in1=L['rstd'][:], op=ALU.mult),
                     reads=[('xres', n), ('ln_rstd', W)], writes=[('ln_t', W, n % 2)])
                S.op('dve', lambda e, n=n, t=t: e.tensor_tensor(out=t[:], in0=t[:], in1=L['nmr'][:], op=ALU.add),
                     reads=[('ln_t', W, n % 2), ('ln_nmr', W)], writes=[('ln_t', W, n % 2)])
                S.op('act', lambda e, n=n, t=t: e.activation(out=xres[:, n, :], in_=t[:], func=AF.Identity,
                                                             bias=lnp[:, 8 + n:9 + n], scale=lnp[:, n:n + 1]),
                     reads=[('ln_t', W, n % 2), 'lnp'], writes=[('xres', n)])
                yield
            S.dma('sp', 'x1st', lambda e: e.dma_start(out=x1s[:, :, blk * BLK:(blk + 1) * BLK], in_=xres[:]),
                  reads=[('xres', n) for n in range(KT)], writes=[('x1s', blk)])
            yield

        nblk_pre = T_PRE // BLK
        nblk_own = T_OWN // BLK
        seq = [(xTp, b, False) for b in range(nblk_pre)] + [(xT, b, True) for b in range(nblk_own)]
        gch = {'n': 0}

        def ch_all(slot, fs, own, prefetch=None):
            g0 = gch['n']
            gch['n'] += 4
            gates(slot, 0, g0 % 2)
            yield
            tm_proj(slot, 0, g0 % 2)
            yield
            for c in range(4):
                gs = (g0 + c) % 2
                if c == 3 and prefetch is not None:
                    load_x(prefetch[0], prefetch[1], slot)
                yield from mixer_kind('H', fs, c, gs, own)
                if c < 3:
                    gates(slot, c + 1, (gs + 1) % 2)
                    yield
                    if own:
                        yield from mixer_kind('M', fs, c, gs, own, mid_hook=lambda c=c, gs=gs: tm_proj(slot, c + 1, (gs + 1) % 2))
                    else:
                        yield from mixer_kind('M', fs, c, gs, own)
                        tm_proj(slot, c + 1, (gs + 1) % 2)
                        yield
                else:
                    yield from mixer_kind('M', fs, c, gs, own)

        def carry_flag():
            for j in range(4):
                S.op('dve', lambda e, j=j: e.tensor_scalar(out=convbuf[j][:, 0:3], in0=convbuf[j][:, 0:3], scalar1=flag[:, 0:1],
                                                           scalar2=None, op0=ALU.mult),
                     reads=[('cb_carry', j), 'flag'], writes=[('cb_carry', j)])

        def state_flag():
            for h in range(4):
                S.op('dve', lambda e, h=h: e.tensor_scalar(out=Sst[h][:], in0=Sst[h][:], scalar1=flag[:, 0:1], scalar2=None, op0=ALU.mult),
                     reads=[('S', h), 'flag'], writes=[('S', h)])
                S.op('dve', lambda e, h=h: e.tensor_scalar(out=Cst[h][:], in0=Cst[h][:], scalar1=flag[:, 0:1], scalar2=None, op0=ALU.mult),
                     reads=[('C', h), 'flag'], writes=[('C', h)])
                S.op('dve', lambda e, h=h: e.tensor_copy(out=Cbf[h][:], in_=Cst[h][:]), reads=[('C', h)], writes=[('Cbf', h)])

        load_x(seq[0][0], seq[0][1], 0)
        load_w_in(0)
        load_w_in(1)
        if len(seq) > 1:
            load_x(seq[1][0], seq[1][1], 1)
        if seq[0][2]:
            carry_flag()
        def needq(i):
            return seq[i][2] or (i + 1 < len(seq) and seq[i + 1][2])
        run(fm_a(0, 0, seq[0][2], needq(0)))
        load_w_in(2)
        load_w_out()
        for bi, (src, blk, own) in enumerate(seq):
            slot = bi % 2
            nxt = seq[bi + 1] if bi + 1 < len(seq) else None
            nn = seq[bi + 2] if bi + 2 < len(seq) else None
            if own and blk == 0:
                state_flag()
            main = [ch_all(slot, slot, own, prefetch=(nn[0], nn[1]) if nn is not None else None)]
            nmain = 4 * (2 + 14)
            if own:
                nxt_own = nxt is not None and nxt[2]
                if blk == 0:
                    main = [fm_b(slot)] + main
                    nmain += 8
                main = main + [outproj_ln(blk, (bi + 1) % 2 if nxt_own else None)]
                nmain += 21
            if nxt is not None:
                if nxt[2] and nxt[1] == 0:
                    carry_flag()
                merge(chain(*main), nmain, fm_a((bi + 1) % 2, (bi + 1) % 2, nxt[2], needq(bi + 1)), 48 if nxt[2] else 28,
                      frac=1.0)
            else:
                run(chain(*main))
            if own and blk == 0:
                dump('ocat', lambda: ocat[:], [128, 8, BLK], [('ocat', j) for j in range(8)])

    def phase_b():
        W = BLKB
        wg = sb("wg_bf", [128, KT, DFF], BF16)
        wu = sb("wu_bf", [128, KT, DFF], BF16)
        wd = sb("wd_bf", [128, FT, D], BF16)
        pwp = sb("pwp_bf", [128, 2, D], BF16)
        pwg = sb("pwg_bf", [128, KT, D], BF16)
        pbg = sb("pbg_sb", [128, 8])
        xr = [sb("xr%d" % i, [128, KT, W]) for i in range(2)]
        xbf = [sb("xbf%d" % i, [128, KT, W], BF16) for i in range(3)]
        pb = [sb("pb%d" % i, [128, 2, W], BF16) for i in range(2)]
        _hT = sb("hT", [128, FT, W], BF16)
        hT = [_hT, _hT]
        sg = [sb("sg%d" % i, [128, W]) for i in range(2)]
        sgp = [sb("sgp%d" % i, [128, W]) for i in range(2)]
        def load_weights():
            FG = 4
            for f0 in [0, 1, 2] + list(range(4, FT, FG)):
                f1 = min(FT, f0 + (1 if f0 < 2 else (2 if f0 == 2 else FG)))
                S.dma('pool', 'wg%d' % f0, lambda e, f0=f0, f1=f1: e.dma_start(out=wg[:, :, f0 * 128:f1 * 128], in_=wg_d[:, :, f0 * 128:f1 * 128]),
                      writes=[('wg', f) for f in range(f0, f1)])
                S.dma('pool', 'wu%d' % f0, lambda e, f0=f0, f1=f1: e.dma_start(out=wu[:, :, f0 * 128:f1 * 128], in_=wu_d[:, :, f0 * 128:f1 * 128]),
                      writes=[('wu', f) for f in range(f0, f1)])
            for f0 in range(0, FT, FG):
                f1 = min(FT, f0 + FG)
                S.dma('pool', 'wd%d' % f0, lambda e, f0=f0, f1=f1: e.dma_start(out=wd[:, f0:f1, :], in_=wd_d[:, f0:f1, :]),
                      writes=[('wd', f) for f in range(f0, f1)])
            for kt in range(KT):
                S.dma('pool', 'pwg%d' % kt, lambda e, kt=kt: e.dma_start(out=pwg[:, kt, :], in_=pwg_d[:, kt, :]), writes=[('pwg', kt)])
            S.dma('pool', 'pwp', lambda e: e.dma_start(out=pwp[:], in_=pwp_d), writes=['pwp'])
        S.dma('sp', 'pbg', lambda e: e.dma_start(out=pbg[:], in_=pbg_d), writes=['pbg'])

        nb = T_OWN // W

        def load_r(b):
            s = b % 2
            ts = slice(b * W, (b + 1) * W)
            S.dma('sp', 'xr%d' % s, lambda e: e.dma_start(out=xr[s][:], in_=x1s[:, :, ts]),
                  reads=[('x1s', (b * W) // BLK)], writes=[('xr', s, n) for n in range(KT)])
            S.dma('pool', 'pb%d' % s, lambda e: e.dma_start(out=pb[s][:], in_=pT[:, :, ts]), writes=[('pb', s)])

        def load_bf(b):
            s3 = b % 3
            ts = slice(b * W, (b + 1) * W)
            S.dma('pool', 'xbf%d' % s3, lambda e: e.dma_start(out=xbf[s3][:], in_=x1s[:, :, ts]),
                  reads=[('x1s', (b * W) // BLK)], writes=[('xbf', s3, n) for n in range(KT)])

        def gateup(b):
            s = b % 2
            s3 = b % 3
            for f in range(FT):
                pg_, pgk = next_pb()
                for kt in range(KT):
                    S.op('pe', lambda e, pg_=pg_, kt=kt, f=f: e.matmul(pg_[:, 0:W], lhsT=wg[:, kt, f * 128:(f + 1) * 128], rhs=xbf[s3][:, kt, :],
                                                                      start=(kt == 0), stop=(kt == KT - 1)),
                         reads=[('wg', f), ('xbf', s3, kt)], writes=[pgk], inc=(kt == KT - 1))
                pu_, puk = next_pb()
                for kt in range(KT):
                    S.op('pe', lambda e, pu_=pu_, kt=kt, f=f: e.matmul(pu_[:, 0:W], lhsT=wu[:, kt, f * 128:(f + 1) * 128], rhs=xbf[s3][:, kt, :],
                                                                      start=(kt == 0), stop=(kt == KT - 1)),
                         reads=[('wu', f), ('xbf', s3, kt)], writes=[puk], inc=(kt == KT - 1))
                S.op('act', lambda e, pg_=pg_, f=f: e.activation(out=sg[f % 2][:], in_=pg_[:, 0:W], func=AF.Silu),
                     reads=[pgk], writes=[('sg', f % 2)])
                S.op('dve', lambda e, pu_=pu_, f=f: e.tensor_tensor(out=hT[s][:, f, :], in0=pu_[:, 0:W], in1=sg[f % 2][:], op=ALU.mult),
                     reads=[puk, ('sg', f % 2)], writes=[('hT', f)])
                yield

        def down(b):
            s = b % 2
            for n in range(KT):
                pd_, pdk = next_pb()
                for f in range(FT):
                    S.op('pe', lambda e, pd_=pd_, f=f, n=n: e.matmul(pd_[:, 0:W], lhsT=wd[:, f, n * 128:(n + 1) * 128], rhs=hT[s][:, f, :],
                                                                    start=(f == 0), stop=(f == FT - 1)),
                         reads=[('wd', f), ('hT', f)], writes=[pdk], inc=(f == FT - 1))
                S.op('dve', lambda e, pd_=pd_, n=n: e.scalar_tensor_tensor(out=xr[s][:, n, :], in0=xr[s][:, n, :], scalar=ALPHA, in1=pd_[:, 0:W],
                                                                           op0=ALU.mult, op1=ALU.add),
                     reads=[pdk, ('xr', s, n)], writes=[('xr', s, n)])
                yield

        BK_A, BK_S, BK_O = XK[0], XK[1], YK[0]
        PMA, PMS, PMO = X[0], X[1], Y[0]

        L2 = dict(ybf=[sb("l2_ybf%d" % i, [128, W], BF16) for i in range(2)],
                  ysq=[sb("l2_ysq%d" % i, [128, W], BF16) for i in range(2)],
                  rstd=sb("l2_rstd", [128, W]), nmr=sb("l2_nmr", [128, W]),
                  t=[sb("l2_t%d" % i, [128, W]) for i in range(2)])

        def ln_prep(s, n):
            S.op('dve', lambda e: e.tensor_copy(out=L2['ybf'][n % 2][:], in_=xr[s][:, n, :]), reads=[('xr', s, n)], writes=[('l2ybf', n % 2)])
            S.op('act', lambda e: e.activation(out=L2['ysq'][n % 2][:], in_=xr[s][:, n, :], func=AF.Square),
                 reads=[('xr', s, n)], writes=[('l2ysq', n % 2)])

        def ln_stat(s, n):
            S.op('pe', lambda e: e.matmul(PMA[:, 0:W], lhsT=onesM[:], rhs=L2['ybf'][n % 2][:], start=(n == 0), stop=(n == KT - 1)),
                 reads=[('l2ybf', n % 2), 'onesM'], writes=[BK_A])
            S.op('pe', lambda e: e.matmul(PMS[:, 0:W], lhsT=onesM[:], rhs=L2['ysq'][n % 2][:], start=(n == 0), stop=(n == KT - 1)),
                 reads=[('l2ysq', n % 2), 'onesM'], writes=[BK_S])

        def ln_chain(s):
            msq = L2['t'][0]
            S.op('act', lambda e: e.activation(out=msq[:], in_=PMA[:, 0:W], func=AF.Square), reads=[BK_A], writes=[('l2t', 0)])
            S.op('dve', lambda e: e.scalar_tensor_tensor(out=L2['rstd'][:], in0=PMS[:, 0:W], scalar=LN_EPS, in1=msq[:],
                                                         op0=ALU.add, op1=ALU.subtract),
                 reads=[BK_S, ('l2t', 0)], writes=['l2rstd'])
            S.op('act', lambda e: e.activation(out=L2['rstd'][:], in_=L2['rstd'][:], func=AF.Ln), reads=['l2rstd'], writes=['l2rstd'])
            S.op('act', lambda e: e.activation(out=L2['rstd'][:], in_=L2['rstd'][:], func=AF.Exp, scale=-0.5), reads=['l2rstd'], writes=['l2rstd'])
            S.op('dve', lambda e: e.scalar_tensor_tensor(out=L2['nmr'][:], in0=PMA[:, 0:W], scalar=-1.0, in1=L2['rstd'][:],
                                                         op0=ALU.mult, op1=ALU.mult),
                 reads=[BK_A, 'l2rstd'], writes=['l2nmr'])

        def ln_norm(s, n, s3):
            t = L2['t'][n % 2]
            S.op('dve', lambda e: e.tensor_tensor(out=t[:], in0=xr[s][:, n, :], in1=L2['rstd'][:], op=ALU.mult),
                 reads=[('xr', s, n), 'l2rstd'], writes=[('l2t', n % 2)])
            S.op('dve', lambda e: e.tensor_tensor(out=t[:], in0=t[:], in1=L2['nmr'][:], op=ALU.add),
                 reads=[('l2t', n % 2), 'l2nmr'], writes=[('l2t', n % 2)])
            S.op('act', lambda e: e.activation(out=xr[s][:, n, :], in_=t[:], func=AF.Identity,
                                               bias=lnp[:, 24 + n:25 + n], scale=lnp[:, 16 + n:17 + n]),
                 reads=[('l2t', n % 2), 'lnp'], writes=[('xr', s, n)])
            S.op('dve', lambda e: e.tensor_copy(out=xbf[s3][:, n, :], in_=xr[s][:, n, :]), reads=[('xr', s, n)], writes=[('xbf', s3, n)])

        def ple(b):
            s = b % 2
            s3 = b % 3
            ts = slice(b * W, (b + 1) * W)
            for n in range(KT):
                pgo, BK_O = (Y[0], YK[0]) if n % 2 == 0 else (Y[1], YK[1])
                for kt in range(KT):
                    S.op('pe', lambda e, kt=kt, n=n, pgo=pgo: e.matmul(pgo[:, 0:W], lhsT=pwg[:, kt, n * 128:(n + 1) * 128], rhs=xbf[s3][:, kt, :],
                                                                      start=(kt == 0), stop=(kt == KT - 1)),
                         reads=[('pwg', kt), ('xbf', s3, kt)], writes=[BK_O], inc=(kt == KT - 1))
                pp_, ppk = (PMA, BK_A) if n % 2 == 0 else (PMS, BK_S)
                for k2 in range(2):
                    S.op('pe', lambda e, pp_=pp_, k2=k2, n=n: e.matmul(pp_[:, 0:W], lhsT=pwp[:, k2, n * 128:(n + 1) * 128], rhs=pb[s][:, k2, :],
                                                                      start=(k2 == 0), stop=(k2 == 1)),
                         reads=['pwp', ('pb', s)], writes=[ppk], inc=(k2 == 1))
                S.op('act', lambda e, n=n, pgo=pgo: e.activation(out=sgp[n % 2][:], in_=pgo[:, 0:W], func=AF.Sigmoid,
                                                                 bias=pbg[:, n:n + 1], scale=1.0),
                     reads=[BK_O, 'pbg'], writes=[('sgp', n % 2)])
                S.op('dve', lambda e, pp_=pp_, n=n: e.tensor_tensor(out=sgp[n % 2][:], in0=pp_[:, 0:W], in1=sgp[n % 2][:], op=ALU.mult),
                     reads=[ppk, ('sgp', n % 2)], writes=[('sgp', n % 2)])
                S.op('pool', lambda e, n=n: e.tensor_tensor(out=xr[s][:, n, :], in0=xr[s][:, n, :], in1=sgp[n % 2][:], op=ALU.add),
                     reads=[('xr', s, n), ('sgp', n % 2)], writes=[('xr', s, n)])
            S.dma('sp', 'out%d' % s, lambda e: e.dma_start(out=outT[:, :, ts], in_=xr[s][:]),
                  reads=[('xr', s, n) for n in range(KT)], writes=[('out', s)])

        load_bf(0)
        load_r(0)
        if nb > 1:
            load_bf(1)
        load_weights()
        run(gateup(0))
        run(down(0))
        for b in range(nb):
            s = b % 2
            if b + 2 < nb:
                load_bf(b + 2)
            if b + 1 < nb:
                load_r(b + 1)
                g = gateup(b + 1)
            else:
                g = iter(())
            f = 0
            alive = True
            while alive or f <= 2 * (KT - 1) + 2:
                if f % 2 == 0 and f // 2 < KT:
                    ln_prep(s, f // 2)
                try:
                    next(g)
                except StopIteration:
                    alive = False
                if f % 2 == 0 and 0 <= f // 2 - 1 < KT:
                    ln_stat(s, f // 2 - 1)
                f += 1
            ln_chain(s)
            d = down(b + 1) if b + 1 < nb else iter(())
            for n in range(KT):
                try:
                    next(d)
                except StopIteration:
                    pass
                ln_norm(s, n, b % 3)
            run(d)
            ple(b)
        S.final_wait('sp', [('out', 0), ('out', 1)])

    base = len(ctxs)
    phase_a()
    S.final_wait('sp', [('x1s', b) for b in range(T_OWN // BLK)] + [('dbg', n) for n in dbg_out])
    S.emit()
    for cm in reversed(ctxs[base:]):
        cm.__exit__(None, None, None)
    del ctxs[base:]
    phase_b()
    S.emit()
    for cm in reversed(ctxs):
        cm.__exit__(None, None, None)
    S.close()
    return nc, S, dbg_out


def _tile_rows(w):
    K, N = w.shape
    return np.ascontiguousarray(w.reshape(K // 128, 128, N).transpose(1, 0, 2))


def _cols(v):
    return np.ascontiguousarray(v.reshape(-1, 128).T)


def _tokT(a):
    T, F = a.shape
    return np.ascontiguousarray(a.T.reshape(F // 128, 128, T).transpose(1, 0, 2))


def make_weights(inp):
    f = np.float32
    w_in = np.asarray(inp["w_in"], f)[0]
    b_in = np.asarray(inp["b_in"], f)[0]
    lg = np.asarray(inp["hg_lb_logits"], f)
    cw = np.asarray(inp["ml_conv_w"], f)[0]
    cb = np.asarray(inp["ml_conv_b"], f)[0]
    W = {}
    W["w_in"] = _tile_rows(w_in)
    W["b_fm"] = np.ascontiguousarray(np.stack([b_in[c:c + 128] for c in FM_COLS], axis=1))
    btm = np.concatenate([b_in[C_HV:C_HV + 512], b_in[C_MV:C_MV + 512], b_in[C_IG:C_IG + 8]])
    W["b_tm"] = np.ascontiguousarray(np.broadcast_to(btm[None, :], (128, 1032)))
    W["lbl"] = np.ascontiguousarray(np.concatenate([_cols(lg[0]), _cols(lg[1])], axis=1))
    W["convw"] = np.ascontiguousarray(cw.reshape(4, 4, 128).transpose(2, 1, 0).reshape(128, 16))
    W["convb"] = _cols(cb)
    W["gain"] = _cols(np.concatenate([np.asarray(inp["hg_norm_g"], f)[0], np.asarray(inp["ml_norm_g"], f)[0]]))
    W["w_out"] = _tile_rows(np.asarray(inp["w_out"], f)[0])
    W["lnp"] = np.ascontiguousarray(np.concatenate([_cols(np.asarray(inp[k], f)[0]) for k in ("ln1_g", "ln1_b", "ln2_g", "ln2_b")], axis=1))
    W["wg"] = _tile_rows(np.asarray(inp["w_ffn_gate"], f)[0])
    W["wu"] = _tile_rows(np.asarray(inp["w_ffn_up"], f)[0])
    W["wd"] = _tile_rows(np.asarray(inp["w_ffn_down"], f)[0])
    W["pwp"] = _tile_rows(np.asarray(inp["ple_w_proj"], f)[0])
    W["pwg"] = _tile_rows(np.asarray(inp["ple_w_gate"], f)[0])
    W["pbg"] = _cols(np.asarray(inp["ple_b_gate"], f)[0])
    return W


def make_core(W, x_own, x_pre, p_own, flagv):
    m = dict(W)
    m["xT"] = _tokT(x_own)
    m["xTp"] = _tokT(x_pre)
    m["pT"] = _tokT(p_own)
    m["flag"] = np.full((128, 1), flagv, np.float32)
    return m


def untile_out(oT):
    return np.ascontiguousarray(oT.transpose(1, 0, 2).reshape(D, -1).T)


_NC_CACHE = {}


def kernel(**inputs):
    x = np.asarray(inputs["x"], np.float32)
    p = np.asarray(inputs["p"], np.float32)[0]
    B, SEQ, _ = x.shape
    HALF = SEQ // 2
    key = (HALF,)
    if key not in _NC_CACHE:
        _NC_CACHE[key] = build(HALF, HALF)[0]
    nc = _NC_CACHE[key]
    W = make_weights(inputs)
    in_maps = []
    for b in range(B):
        for h in range(2):
            t0 = h * HALF
            in_maps.append(make_core(W, x[b, t0:t0 + HALF], x[b, 0:HALF], p[b, t0:t0 + HALF], float(h)))
    res = run_bass_kernel_spmd(nc, in_maps, core_ids=list(range(2 * B)))
    out = np.empty((B, SEQ, D), np.float32)
    for b in range(B):
        for h in range(2):
            out[b, h * HALF:(h + 1) * HALF] = untile_out(np.asarray(res.results[2 * b + h]["outT"]))
    return out
```
